# Optimizing a Trainium2 kernel written in Bass

```python
import jax, jax.numpy as jnp
from jax import lax
import numpy as np

D_MODEL = 1024
BATCH = 8
SEQ = 2048
DEPTH = 2

GM_GROUPS = 4
GM_DIM = 64
GM_WIDTH = GM_GROUPS * GM_DIM
GM_CHUNK = 128
ML_HEADS = 4
ML_QK = 64
ML_V = 128
ML_QK_W = ML_HEADS * ML_QK
ML_V_W = ML_HEADS * ML_V
ML_CHUNK = 64
GD_HEADS = 4
GD_K = 64
GD_V = 64
GD_K_W = GD_HEADS * GD_K
GD_V_W = GD_HEADS * GD_V
GD_CHUNK = 64
GD_CONV = 5
D_FF = 2816
FFN_CONV = 3
EPS = 1e-6

D_MIX = GM_WIDTH + ML_V_W + GD_V_W
SIZES = [GM_WIDTH, GM_WIDTH,
         ML_QK_W, ML_QK_W, ML_V_W, ML_V_W,
         4 * ML_HEADS,
         2 * GD_K_W + GD_V_W,
         GD_V_W,
         2 * GD_HEADS, 2 * GD_HEADS]
N_IN = sum(SIZES)
SPLIT_IDX = [sum(SIZES[:i + 1]) for i in range(len(SIZES) - 1)]

kernel_name = "hybrid_gmlp_mlstm_gdn_encoder"


def rmsnorm(x, g):
    xf = x.astype(jnp.float32)
    y = xf * lax.rsqrt(jnp.mean(xf * xf, axis=-1, keepdims=True) + EPS)
    return (y * g.astype(jnp.float32)).astype(x.dtype)


def l2norm(x):
    return x * lax.rsqrt(jnp.sum(x * x, axis=-1, keepdims=True) + EPS)


def dwconv_centred(x, w):
    width = w.shape[0]
    pad = (width - 1) // 2
    return lax.conv_general_dilated(
        x, w.astype(x.dtype)[:, None, :], window_strides=(1,),
        padding=[(pad, width - 1 - pad)],
        dimension_numbers=("NWC", "WIO", "NWC"),
        feature_group_count=x.shape[-1])


def to_heads(t, n_heads):
    b, s, _ = t.shape
    return t.reshape(b, s, n_heads, -1).transpose(0, 2, 1, 3).astype(jnp.float32)


def flip_t(a):
    return jnp.flip(a, axis=2)


def gmlp_group(u, v, gm_norm, w_s, b_s):
    b, s, _ = u.shape
    nc = s // GM_CHUNK
    vn = rmsnorm(v.reshape(b, s, GM_GROUPS, GM_DIM), gm_norm)
    vn = vn.reshape(b, nc, GM_CHUNK, GM_GROUPS, GM_DIM)
    sg = jnp.einsum("gpq,bcqge->bcpge", w_s.astype(vn.dtype), vn) + b_s.T.astype(vn.dtype)[:, :, None]
    return u * sg.reshape(b, s, GM_WIDTH)


def mlstm_dir(q, k, v, ig, lf):
    b, h, s, dk = q.shape
    dv = v.shape[-1]
    L = ML_CHUNK
    nc = s // L
    q = q.reshape(b, h, nc, L, dk) * (dk ** -0.5)
    k = k.reshape(b, h, nc, L, dk)
    v = v.reshape(b, h, nc, L, dv)
    ig = ig.reshape(b, h, nc, L)
    bcum = jnp.cumsum(lf.reshape(b, h, nc, L), axis=-1)
    b_last = bcum[..., -1]
    a = b_last[..., None] - bcum + ig
    m_loc = jnp.max(a, axis=-1)
    wa = jnp.exp(a - m_loc[..., None])
    C_loc = jnp.einsum("bhcsk,bhcsv->bhckv", k * wa[..., None], v)
    n_loc = jnp.einsum("bhcs,bhcsk->bhck", wa, k)

    def step(carry, xs):
        C, n, m = carry
        bl, ml, Cl, nl = xs
        m_new = jnp.maximum(bl + m, ml)
        f_old = jnp.exp(bl + m - m_new)
        f_loc = jnp.exp(ml - m_new)
        C_new = f_old[..., None, None] * C + f_loc[..., None, None] * Cl
        n_new = f_old[..., None] * n + f_loc[..., None] * nl
        return (C_new, n_new, m_new), (C, n, m)

    init = (jnp.zeros((b, h, dk, dv), jnp.float32), jnp.zeros((b, h, dk), jnp.float32),
            jnp.zeros((b, h), jnp.float32))
    xs = (jnp.moveaxis(b_last, 2, 0), jnp.moveaxis(m_loc, 2, 0),
          jnp.moveaxis(C_loc, 2, 0), jnp.moveaxis(n_loc, 2, 0))
    _, (C_in, n_in, m_in) = lax.scan(step, init, xs)
    C_in = jnp.moveaxis(C_in, 0, 2)
    n_in = jnp.moveaxis(n_in, 0, 2)
    m_in = jnp.moveaxis(m_in, 0, 2)

    lower = jnp.tril(jnp.ones((L, L), bool))
    D = jnp.where(lower, bcum[..., :, None] - bcum[..., None, :] + ig[..., None, :], -jnp.inf)
    m_inter = bcum + m_in[..., None]
    m_t = jnp.maximum(m_inter, jnp.max(D, axis=-1))
    S = jnp.einsum("bhctk,bhcsk->bhcts", q, k) * jnp.exp(D - m_t[..., None])
    w_inter = jnp.exp(m_inter - m_t)
    num = (w_inter[..., None] * jnp.einsum("bhctk,bhckv->bhctv", q, C_in)
           + jnp.einsum("bhcts,bhcsv->bhctv", S, v))
    den = w_inter * jnp.einsum("bhctk,bhck->bhct", q, n_in) + jnp.sum(S, axis=-1)
    out = num / jnp.maximum(jnp.abs(den), jnp.exp(-m_t))[..., None]
    return out.reshape(b, h, s, dv)


def gdn_dir(q, k, v, g, beta):
    b, h, s, dk = q.shape
    dv = v.shape[-1]
    L = GD_CHUNK
    nc = s // L
    q = q.reshape(b, h, nc, L, dk)
    k = k.reshape(b, h, nc, L, dk)
    v = v.reshape(b, h, nc, L, dv)
    beta = beta.reshape(b, h, nc, L)
    gc = jnp.cumsum(g.reshape(b, h, nc, L), axis=-1)
    k_beta = k * beta[..., None]
    v_beta = v * beta[..., None]
    incl = jnp.tril(jnp.ones((L, L), bool))
    strict = jnp.tril(jnp.ones((L, L), bool), k=-1)
    decay = jnp.exp(jnp.where(incl, gc[..., :, None] - gc[..., None, :], -jnp.inf))
    Lmat = jnp.where(strict, jnp.einsum("bhctk,bhcsk->bhcts", k_beta, k) * decay, 0.0)
    eye = jnp.eye(L, dtype=jnp.float32)
    rhs = jnp.concatenate([v_beta, k_beta * jnp.exp(gc)[..., None]], axis=-1)
    sol = lax.linalg.triangular_solve(Lmat + eye, rhs, left_side=True, lower=True,
                                      unit_diagonal=True)
    u_w = sol[..., :dv]
    w_w = sol[..., dv:]
    attn = jnp.where(incl, jnp.einsum("bhctk,bhcsk->bhcts", q, k) * decay, 0.0)

    def step(S, xs):
        q_c, k_c, u_c, w_c, gc_c, attn_c = xs
        v_new = u_c - jnp.einsum("bhtk,bhkv->bhtv", w_c, S)
        o = (jnp.einsum("bhtk,bhkv->bhtv", q_c * jnp.exp(gc_c)[..., None], S)
             + jnp.einsum("bhts,bhsv->bhtv", attn_c, v_new))
        g_last = gc_c[..., -1]
        S_new = (S * jnp.exp(g_last)[..., None, None]
                 + jnp.einsum("bhsk,bhsv->bhkv", k_c * jnp.exp(g_last[..., None] - gc_c)[..., None], v_new))
        return S_new, o

    xs = tuple(jnp.moveaxis(t, 2, 0) for t in (q, k, u_w, w_w, gc, attn))
    _, o = lax.scan(step, jnp.zeros((b, h, dk, dv), jnp.float32), xs)
    return jnp.moveaxis(o, 0, 2).reshape(b, h, s, dv)


def hybrid_layer(x, norm_mix, w_in, gm_norm, gm_ws, gm_bs, ml_gate_bias, ml_head_norm,
                 gd_conv, gd_A_log, gd_dt_bias, gd_head_norm, w_out,
                 norm_ffn, w_up, ffn_conv, ffn_conv_b, w_down):
    b, s, _ = x.shape
    h = rmsnorm(x, norm_mix)
    proj = h @ w_in
    (a_u, a_v, m_q, m_k, m_v, m_o, m_g, c_qkv, c_z, c_a, c_b) = jnp.split(proj, SPLIT_IDX, axis=-1)

    y_a = gmlp_group(jax.nn.gelu(a_u), jax.nn.gelu(a_v), gm_norm, gm_ws, gm_bs)

    q = to_heads(m_q, ML_HEADS)
    k = to_heads(m_k, ML_HEADS)
    v = to_heads(m_v, ML_HEADS)
    gates = (m_g.astype(jnp.float32) + ml_gate_bias.astype(jnp.float32))
    gates = gates.reshape(b, s, 4, ML_HEADS).transpose(2, 0, 3, 1)
    i_fw, i_bw, f_fw, f_bw = gates[0], gates[1], gates[2], gates[3]
    h_fw = mlstm_dir(q, k, v, i_fw, jax.nn.log_sigmoid(f_fw))
    h_bw = flip_t(mlstm_dir(flip_t(q), flip_t(k), flip_t(v), flip_t(i_bw),
                            flip_t(jax.nn.log_sigmoid(f_bw))))
    hB = (h_fw + h_bw).transpose(0, 2, 1, 3)
    hB = rmsnorm(hB, ml_head_norm) * jax.nn.sigmoid(m_o.astype(jnp.float32)).reshape(b, s, ML_HEADS, ML_V)
    y_b = hB.reshape(b, s, ML_V_W)

    qkv = jax.nn.silu(dwconv_centred(c_qkv, gd_conv))
    cq, ck, cv = jnp.split(qkv, [GD_K_W, 2 * GD_K_W], axis=-1)
    gq = l2norm(to_heads(cq, GD_HEADS)) * (GD_K ** -0.5)
    gk = l2norm(to_heads(ck, GD_HEADS))
    gv = to_heads(cv, GD_HEADS)
    a_in = c_a.astype(jnp.float32).reshape(b, s, 2, GD_HEADS).transpose(2, 0, 3, 1)
    beta = jax.nn.sigmoid(c_b.astype(jnp.float32).reshape(b, s, 2, GD_HEADS).transpose(2, 0, 3, 1))
    A = jnp.exp(gd_A_log.astype(jnp.float32))[:, None, :, None]
    g = -A * jax.nn.softplus(a_in + gd_dt_bias.astype(jnp.float32)[:, None, :, None])
    o_fw = gdn_dir(gq, gk, gv, g[0], beta[0])
    o_bw = flip_t(gdn_dir(flip_t(gq), flip_t(gk), flip_t(gv), flip_t(g[1]), flip_t(beta[1])))
    oC = (o_fw + o_bw).transpose(0, 2, 1, 3)
    oC = rmsnorm(oC, gd_head_norm) * jax.nn.silu(c_z.astype(jnp.float32)).reshape(b, s, GD_HEADS, GD_V)
    y_c = oC.reshape(b, s, GD_V_W)

    y = jnp.concatenate([y_a.astype(x.dtype), y_b.astype(x.dtype), y_c.astype(x.dtype)], axis=-1)
    x = x + y @ w_out

    hf = rmsnorm(x, norm_ffn)
    up = dwconv_centred(hf @ w_up, ffn_conv) + ffn_conv_b
    gt, val = jnp.split(up, 2, axis=-1)
    return x + (jax.nn.silu(gt) * val) @ w_down


def setup_inputs(seed: int = 0) -> dict:
    key = jax.random.key(seed)
    ks = jax.random.split(key, 24)

    def nrm(k, shape, scale):
        return jax.random.normal(k, shape, jnp.float32) * scale

    def gain(k, shape):
        return 1.0 + 0.02 * jax.random.normal(k, shape, jnp.float32)

    x = jax.random.normal(ks[0], (BATCH, SEQ, D_MODEL), jnp.float32)
    norm_mix = gain(ks[1], (DEPTH, D_MODEL))
    w_in = nrm(ks[2], (DEPTH, D_MODEL, N_IN), D_MODEL ** -0.5)
    gm_norm = gain(ks[3], (DEPTH, GM_GROUPS, GM_DIM))
    gm_ws = nrm(ks[4], (DEPTH, GM_GROUPS, GM_CHUNK, GM_CHUNK), GM_CHUNK ** -0.5)
    gm_bs = 1.0 + 0.1 * jax.random.normal(ks[5], (DEPTH, GM_GROUPS, GM_CHUNK), jnp.float32)
    i_bias = 0.1 * jax.random.normal(ks[6], (DEPTH, 2 * ML_HEADS), jnp.float32)
    f_bias = 3.0 + 0.5 * jax.random.normal(ks[7], (DEPTH, 2 * ML_HEADS), jnp.float32)
    ml_gate_bias = jnp.concatenate([i_bias, f_bias], axis=-1)
    ml_head_norm = gain(ks[8], (DEPTH, ML_HEADS, ML_V))
    gd_conv = nrm(ks[9], (DEPTH, GD_CONV, 2 * GD_K_W + GD_V_W), GD_CONV ** -0.5)
    gd_A_log = jnp.log(jax.random.uniform(ks[10], (DEPTH, 2, GD_HEADS), jnp.float32, 1.0, 16.0))
    dt = jnp.exp(jax.random.uniform(ks[11], (DEPTH, 2, GD_HEADS), jnp.float32,
                                    float(np.log(1e-3)), float(np.log(1e-1))))
    gd_dt_bias = dt + jnp.log(-jnp.expm1(-dt))
    gd_head_norm = gain(ks[12], (DEPTH, GD_HEADS, GD_V))
    w_out = nrm(ks[13], (DEPTH, D_MIX, D_MODEL), D_MIX ** -0.5)
    norm_ffn = gain(ks[14], (DEPTH, D_MODEL))
    w_up = nrm(ks[15], (DEPTH, D_MODEL, 2 * D_FF), D_MODEL ** -0.5)
    ffn_conv = nrm(ks[16], (DEPTH, FFN_CONV, 2 * D_FF), FFN_CONV ** -0.5)
    ffn_conv_b = nrm(ks[17], (DEPTH, 2 * D_FF), 0.02)
    w_down = nrm(ks[18], (DEPTH, D_FF, D_MODEL), D_FF ** -0.5)
    norm_final = gain(ks[19], (D_MODEL,))
    return {"x": x, "norm_mix": norm_mix, "w_in": w_in, "gm_norm": gm_norm, "gm_ws": gm_ws,
            "gm_bs": gm_bs, "ml_gate_bias": ml_gate_bias, "ml_head_norm": ml_head_norm,
            "gd_conv": gd_conv, "gd_A_log": gd_A_log, "gd_dt_bias": gd_dt_bias,
            "gd_head_norm": gd_head_norm, "w_out": w_out, "norm_ffn": norm_ffn, "w_up": w_up,
            "ffn_conv": ffn_conv, "ffn_conv_b": ffn_conv_b, "w_down": w_down,
            "norm_final": norm_final}


def reference(x, norm_mix, w_in, gm_norm, gm_ws, gm_bs, ml_gate_bias, ml_head_norm,
              gd_conv, gd_A_log, gd_dt_bias, gd_head_norm, w_out, norm_ffn, w_up,
              ffn_conv, ffn_conv_b, w_down, norm_final):
    for i in range(DEPTH):
        x = hybrid_layer(x, norm_mix[i], w_in[i], gm_norm[i], gm_ws[i], gm_bs[i],
                         ml_gate_bias[i], ml_head_norm[i], gd_conv[i], gd_A_log[i],
                         gd_dt_bias[i], gd_head_norm[i], w_out[i], norm_ffn[i], w_up[i],
                         ffn_conv[i], ffn_conv_b[i], w_down[i])
    return rmsnorm(x, norm_final)
```

```python
import numpy as np
from contextlib import ExitStack
import concourse.bass as bass
import concourse.mybir as mybir
from concourse.bass_utils import run_bass_kernel_spmd

F32 = mybir.dt.float32
BF16 = mybir.dt.bfloat16
AF = mybir.ActivationFunctionType
ALU = mybir.AluOpType
AX = mybir.AxisListType

D = 1024
DFF = 2816
NIN = 3104
EPS = 1e-6
NFC = DFF // 128
SEQ = 2048
DEPTH = 2


class Sched:
    ENG = ('pe', 'act', 'dve', 'pool', 'sp')

    def __init__(self, nc, es):
        self.nc = nc
        self.es = es
        self.sem = {e: es.enter_context(nc.semaphore('s_' + e)) for e in ('pe', 'act', 'dve', 'pool')}
        self.cnt = dict.fromkeys(('pe', 'act', 'dve', 'pool'), 0)
        self.q = {e: [] for e in self.ENG}
        self.lastw = {}
        self.readers = {}
        self.know = {}
        self._tm = dict(eng={}, w={}, r={})
        self.tokclock = {}
        self.tokseq = {}
        self.seq = 0
        self.chan = {}

    def _deps(self, reads, writes):
        toks = []
        for r in reads:
            t = self.lastw.get(r)
            if t:
                toks.append(t)
        for w in writes:
            t = self.lastw.get(w)
            if t:
                toks.append(t)
            toks.extend(self.readers.get(w, ()))
        return toks

    def _waits(self, eng, toks):
        K = self.know.setdefault(eng, {})
        need = {}
        for (k, v) in sorted(set(toks), key=lambda t: -self.tokseq.get(t, 0)):
            if eng == 'pe' and k == 'pe':
                continue
            if K.get(k, 0) >= v:
                continue
            if need.get(k, 0) < v:
                need[k] = v
            K[k] = v
            for kk, vv in self.tokclock.get((k, v), {}).items():
                if K.get(kk, 0) < vv:
                    K[kk] = vv
        return list(need.items())

    def _commit(self, tok, eng, reads, writes):
        self.seq += 1
        self.tokseq[tok] = self.seq
        clk = dict(self.know.get(eng, {}))
        self.tokclock[tok] = clk
        for r in reads:
            if r not in writes:
                self.readers.setdefault(r, []).append(tok)
        for w in writes:
            self.lastw[w] = tok
            self.readers[w] = []

    def op(self, eng, fn, reads=(), writes=()):
        psr = tuple(r for r in reads if isinstance(r, tuple) and r[0] == 'ps')
        reads = tuple(r for r in reads if not (isinstance(r, tuple) and r[0] == 'ps'))
        writes = tuple(writes) + tuple(r for r in psr if r not in writes)
        waits = self._waits(eng, self._deps(reads, writes))
        self.cnt[eng] += 1
        tok = (eng, self.cnt[eng])
        self.q[eng].append((waits, fn, ('inc', eng)))
        self._commit(tok, eng, reads, writes)

    def dma(self, queue, fn, reads, writes, chan):
        reads = tuple(reads)
        writes = tuple(writes)
        if chan not in self.chan:
            self.chan[chan] = [self.es.enter_context(self.nc.semaphore('c_' + chan)), 0]
        waits = self._waits(queue, self._deps(reads, writes))
        c = self.chan[chan]
        c[1] += 16
        tok = (('dma', chan), c[1])
        self.q[queue].append((waits, fn, ('dma', chan)))
        self._commit(tok, queue, reads, writes)

    def record(self, fn):
        lst = []
        orig = self.op
        self._cur = lst
        self.op = lambda *a, **kw: self._cur.append((a, kw))
        try:
            fn()
        finally:
            self.op = orig
            self._cur = None
        return lst

    def atomic(self):
        sched = self

        class _A:
            def __enter__(self_):
                self_.parent = getattr(sched, '_cur', None)
                if self_.parent is not None:
                    sched._cur = []
                return self_

            def __exit__(self_, *exc):
                if self_.parent is not None:
                    sub = sched._cur
                    sched._cur = self_.parent
                    sched._cur.append(('atomic', sub))
                return False
        return _A()

    COST = dict(pe=0.14, act=0.38, dve=0.42, pool=0.6, sp=2.0)

    def play(self, *lists):
        flat = []
        for l in lists:
            out = []

            def walk(items):
                for it in items:
                    if it[0] == 'atomic':
                        out.append(('grp', [x for x in self._flatten(it[1])]))
                    else:
                        out.append(('grp', [it]))
            walk(l)
            if out:
                flat.append(out)
        if not flat:
            return
        tm = self._tm
        idx = [0] * len(flat)
        remaining = sum(len(l) for l in flat)
        while remaining:
            best, best_t = None, None
            for li, l in enumerate(flat):
                if idx[li] >= len(l):
                    continue
                a, kw = l[idx[li]][1][0]
                t = self._est_start(a[0], kw.get('reads', ()), kw.get('writes', ()))
                key = (t, -(len(l) - idx[li]))
                if best is None or key < best_t:
                    best, best_t = li, key
            for a, kw in flat[best][idx[best]][1]:
                self._est_commit(a[0], kw.get('reads', ()), kw.get('writes', ()))
                self.op(*a, **kw)
            idx[best] += 1
            remaining -= 1

    def _flatten(self, items):
        for it in items:
            if it[0] == 'atomic':
                for x in self._flatten(it[1]):
                    yield x
            else:
                yield it

    def _est_start(self, eng, reads, writes):
        tm = self._tm
        t = tm['eng'].get(eng, 0.0)
        for r in reads:
            t = max(t, tm['w'].get(r, 0.0))
        for w in writes:
            t = max(t, tm['w'].get(w, 0.0), tm['r'].get(w, 0.0))
        return t

    def _est_commit(self, eng, reads, writes, cost=None):
        tm = self._tm
        if cost is None:
            cost = self.COST.get(eng, 0.4)
        t0 = self._est_start(eng, reads, writes)
        t1 = t0 + cost
        tm['eng'][eng] = t1
        lat = t1 + 0.15
        for r in reads:
            tm['r'][r] = max(tm['r'].get(r, 0.0), lat)
        for w in writes:
            tm['w'][w] = lat
            tm['r'][w] = 0.0

    def wait_all(self, eng):
        toks = [(e, self.cnt[e]) for e in self.cnt if self.cnt[e] > 0]
        toks += [(('dma', ch), c[1]) for ch, c in self.chan.items() if c[1] > 0]
        waits = self._waits(eng, toks)
        self.q[eng].append((waits, None, None))

    def barrier(self):
        for e in self.ENG:
            self.wait_all(e)

    def _semof(self, k):
        if isinstance(k, tuple):
            return self.chan[k[1]][0]
        return self.sem[k]

    def emit(self, block):
        handles = dict(pe=block.tensor, act=block.scalar, dve=block.vector, pool=block.gpsimd, sp=block.sync)

        def mk(eng):
            def body(e):
                for waits, fn, inc in self.q[eng]:
                    for k, v in waits:
                        e.wait_ge(self._semof(k), v)
                    if fn is None:
                        continue
                    ins = fn(e)
                    if inc[0] == 'inc':
                        ins.then_inc(self.sem[inc[1]], 1)
                    else:
                        ins.then_inc(self.chan[inc[1]][0], 16)
            return body
        for eng in self.ENG:
            handles[eng](mk(eng))


PP_FIELDS = [
    ('cw_ffn', 3 * 44), ('cb_ffn', 44),
    ('gmn', 256), ('bsf', 256), ('mlb', 16), ('mln', 512),
    ('alog', 8), ('dtb', 8), ('gdn', 256), ('cw_gd', 5 * 6),
]
PP_OFF = {}
_o = 0
for _n, _s in PP_FIELDS:
    PP_OFF[_n] = (_o, _s)
    _o += _s
NPP = _o


def pack_params(inp, l):
    pp = np.zeros((128, NPP), np.float32)

    def put(name, arr):
        o, s = PP_OFF[name]
        pp[:, o:o + s] = np.asarray(arr, np.float32).reshape(128, s)
    fc = np.asarray(inp['ffn_conv'][l])
    put('cw_ffn', fc.reshape(3, 44, 128).transpose(2, 0, 1))
    put('cb_ffn', np.asarray(inp['ffn_conv_b'][l]).reshape(44, 128).T)
    put('gmn', np.broadcast_to(np.asarray(inp['gm_norm'][l]).reshape(1, 256), (128, 256)))
    bs = np.asarray(inp['gm_bs'][l])
    put('bsf', np.broadcast_to(bs.T[:, :, None], (128, 4, 64)))
    put('mlb', np.broadcast_to(np.asarray(inp['ml_gate_bias'][l]).reshape(1, 16), (128, 16)))
    put('mln', np.broadcast_to(np.asarray(inp['ml_head_norm'][l]).reshape(1, 512), (128, 512)))
    put('alog', np.broadcast_to(np.asarray(inp['gd_A_log'][l]).reshape(1, 8), (128, 8)))
    put('dtb', np.broadcast_to(np.asarray(inp['gd_dt_bias'][l]).reshape(1, 8), (128, 8)))
    put('gdn', np.broadcast_to(np.asarray(inp['gd_head_norm'][l]).reshape(1, 256), (128, 256)))
    gc = np.asarray(inp['gd_conv'][l])
    put('cw_gd', gc.reshape(5, 6, 128).transpose(2, 0, 1))
    return pp


CM_FIELDS = [('ident', 128), ('eps', 1), ('one', 1), ('ones', 64), ('BD', 128),
             ('A20', 128), ('A21', 128), ('U2n0', 128), ('U2n1', 128),
             ('NEGF0', 128), ('NEGF1', 128), ('NEGSF0', 128), ('NEGSF1', 128)]
CM_OFF = {}
_o = 0
for _n, _s in CM_FIELDS:
    CM_OFF[_n] = (_o, _s)
    _o += _s
NCM = _o


def const_masks():
    cm = np.zeros((128, NCM), np.float32)

    def put(name, arr):
        o, n = CM_OFF[name]
        cm[:, o:o + n] = arr
    put('ident', np.eye(128, dtype=np.float32))
    put('eps', EPS)
    put('one', 1.0)
    put('ones', 1.0)
    r = np.arange(128)
    rl = r % 64
    same = (r[:, None] // 64) == (r[None, :] // 64)
    put('BD', same.astype(np.float32))
    for d in range(2):
        def peq(a, b):
            return (a <= b) if d == 0 else (a >= b)

        def prec(a, b):
            return (a < b) if d == 0 else (a > b)
        put('A2%d' % d, (same & prec(rl[None, :], rl[:, None])).astype(np.float32))
        put('U2n%d' % d, -(same & peq(rl[:, None], rl[None, :])).astype(np.float32))
        put('NEGF%d' % d, np.where(same & peq(rl[:, None], rl[None, :]), 0.0, -30000.0))
        put('NEGSF%d' % d, np.where(same & prec(rl[:, None], rl[None, :]), 0.0, -30000.0))
    return cm


def _nfree(ap):
    n = 1
    for d_ in ap.shape[1:]:
        n *= int(d_)
    return n


def _c(f, cost):
    f.cost = cost
    return f


def MM(out, lhsT, rhs, start=True, stop=True):
    n = _nfree(out)
    mult = 4.0 if lhsT.dtype == F32 else 1.0
    return _c(lambda e: e.matmul(out, lhsT=lhsT, rhs=rhs, start=start, stop=stop), 0.10 + mult * max(n, 64) / 1900.0)


def TR(out, in_, ident):
    return _c(lambda e: e.transpose(out=out, in_=in_, identity=ident), 0.16)


def ACT(out, in_, func, **kw):
    return _c(lambda e: e.activation(out=out, in_=in_, func=func, **kw), 0.22 + _nfree(out) / 1300.0)


def TT(out, in0, in1, op):
    return _c(lambda e: e.tensor_tensor(out=out, in0=in0, in1=in1, op=op), 0.16 + _nfree(out) / 960.0)


def STT(out, in0, scalar, in1, op0, op1):
    return _c(lambda e: e.scalar_tensor_tensor(out=out, in0=in0, scalar=scalar, in1=in1, op0=op0, op1=op1), 0.16 + _nfree(out) / 960.0)


def TS(out, in0, s1, s2=None, op0=ALU.mult, op1=None):
    c_ = 0.14 + _nfree(out) / 960.0
    if op1 is None:
        return _c(lambda e: e.tensor_scalar(out=out, in0=in0, scalar1=s1, scalar2=None, op0=op0), c_)
    return _c(lambda e: e.tensor_scalar(out=out, in0=in0, scalar1=s1, scalar2=s2, op0=op0, op1=op1), c_)


def RED(out, in_, op=ALU.add, axis=AX.X):
    return _c(lambda e: e.tensor_reduce(out=out, in_=in_, axis=axis, op=op), 0.16 + _nfree(in_) / 960.0)


def CP(out, in_):
    return _c(lambda e: e.tensor_copy(out=out, in_=in_), 0.16 + _nfree(out) / 960.0)


def MSET(ap, v):
    return _c(lambda e: e.memset(ap, v), 0.2 + _nfree(ap) / 960.0)


def DMA(out, in_):
    return _c(lambda e: e.dma_start(out=out, in_=in_), 2.0)


class K:
    pass


def build(T=SEQ, depth=DEPTH, do_mixer=True, do_ffn=True):
    NT = T // 128
    nc = bass.Bass("TRN2", target_bir_lowering=False)
    k = K()
    k.nc, k.T, k.NT, k.depth = nc, T, NT, depth
    dr = {}
    dr['x'] = nc.dram_tensor("x", [T, D], F32, kind="ExternalInput").ap()
    dr['out'] = nc.dram_tensor("out", [T, D], F32, kind="ExternalOutput").ap()
    dr['norms'] = nc.dram_tensor("norms", [2 * depth + 1, D], F32, kind="ExternalInput").ap()
    dr['w_in'] = nc.dram_tensor("w_in", [depth, D, NIN], F32, kind="ExternalInput").ap()
    dr['w_out'] = nc.dram_tensor("w_out", [depth, D, D], F32, kind="ExternalInput").ap()
    dr['w_up'] = nc.dram_tensor("w_up", [depth, D, 2 * DFF], F32, kind="ExternalInput").ap()
    dr['w_down'] = nc.dram_tensor("w_down", [depth, DFF, D], F32, kind="ExternalInput").ap()
    dr['pp'] = nc.dram_tensor("pp", [depth, 128, NPP], F32, kind="ExternalInput").ap()
    dr['wsT'] = nc.dram_tensor("wsT", [depth, 128, 512], F32, kind="ExternalInput").ap()
    dr['cm'] = nc.dram_tensor("cm", [128, NCM], F32, kind="ExternalInput").ap()
    k.dr = dr

    with ExitStack() as es:
        S = Sched(nc, es)
        k.S = S
        _LAST['S'] = S
        k.es = es

        def sb(name, shape, dt):
            return es.enter_context(nc.sbuf_tensor("sb_" + name, shape, dt))
        k.sb = sb
        k.X = sb("X", [128, NT, D], F32)
        k.hT = sb("hT", [128, 8, T + 2], BF16)
        k.gb = sb("gb", [128, D], F32)
        k.pp = sb("pp", [128, NPP], F32)
        k.cm = sb("cm", [128, NCM], F32)
        k.ident = sb("ident", [128, 128], BF16)
        k.ss = sb("ss", [128, NT], F32)
        k.rstd = sb("rstd", [128, NT], F32)
        k.xn = [sb("xn%d" % i, [128, D], BF16) for i in range(2)]
        k.ps = [es.enter_context(nc.psum_tensor("ps%d" % i, [128, 512], F32)) for i in range(8)]

        S.dma('sp', DMA(k.cm[:], dr['cm']), reads=[], writes=['cm'], chan='cm')
        S.op('dve', CP(k.ident[:], k.cm[:, CM_OFF['ident'][0]:CM_OFF['ident'][0] + 128]), reads=['cm'], writes=['ident'])
        k.negfb = [sb("negfb%d" % d_, [128, 256], BF16) for d_ in range(2)]
        for d_ in range(2):
            for h_ in range(2):
                S.op('dve', CP(k.negfb[d_][:, h_ * 128:(h_ + 1) * 128], cmc(k, 'NEGF%d' % d_)), reads=['cm'], writes=['negfb'])
        S.op('pool', MSET(k.hT[:, :, 0:1], 0.0), writes=['hTpad'])
        S.op('pool', MSET(k.hT[:, :, T + 1:T + 2], 0.0), writes=['hTpad'])

        xv = dr['x'].rearrange("(n p) d -> n p d", p=128)
        for i in range(NT):
            S.dma('sp', DMA(k.X[:, i, :], xv[i]), reads=[], writes=[('X', i, 0), ('X', i, 1)], chan='x%d' % i)

        for l in range(depth):
            S.dma('sp', DMA(k.pp[:], dr['pp'][l]), reads=[], writes=['pp'], chan='pp')
            if do_mixer:
                phase_mixer(k, l, do_mixer if isinstance(do_mixer, str) else 'ABC')
            if do_ffn:
                phase_ffn(k, l)
        final_norm(k)
        S.wait_all('sp')
        with nc.Block() as block:
            S.emit(block)
    return nc


def rms_stats(k, norm_idx):
    S, NT = k.S, k.NT
    S.dma('sp', DMA(k.gb[:], k.dr['norms'][norm_idx:norm_idx + 1, :].partition_broadcast(128)),
          reads=[], writes=['gb'], chan='gb')
    for i in range(NT):
        S.op('act', ACT(k.xn[0][:], k.X[:, i, :], AF.Square, accum_out=k.ss[:, i:i + 1]),
             reads=[('X', i, 0), ('X', i, 1)], writes=[('xn', 0), ('ss', i)])
    allss = [('ss', i) for i in range(NT)]
    S.op('act', ACT(k.rstd[:], k.ss[:], AF.Ln, scale=1.0 / D, bias=k.cm[:, CM_OFF['eps'][0]:CM_OFF['eps'][0] + 1]),
         reads=allss + ['cm'], writes=['rstd'])
    S.op('act', ACT(k.rstd[:], k.rstd[:], AF.Exp, scale=-0.5), reads=['rstd'], writes=['rstd'])


def norm_to_T(k, norm_idx, banks=(6, 7)):
    S, NT = k.S, k.NT
    rms_stats(k, norm_idx)
    for i in range(NT):
        xn = k.xn[i % 2]
        bank = banks[i % 2]
        pst = k.ps[bank][:].bitcast(BF16).rearrange("p (a b) -> p a b", a=8)
        S.op('dve', STT(xn[:], k.X[:, i, :], k.rstd[:, i:i + 1], k.gb[:], ALU.mult, ALU.mult),
             reads=[('X', i, 0), ('X', i, 1), 'rstd', 'gb'], writes=[('xn', i % 2)])
        for kc in range(8):
            S.op('pe', TR(pst[:, kc, :], xn[:, kc * 128:(kc + 1) * 128], k.ident[:]),
                 reads=[('xn', i % 2), 'ident'], writes=[('ps', bank)])
        S.op('act', ACT(k.hT[:, :, 1 + i * 128:1 + (i + 1) * 128], pst, AF.Copy),
             reads=[('ps', bank)], writes=[('hT', i)])


def final_norm(k):
    S, NT = k.S, k.NT
    rms_stats(k, 2 * k.depth)
    ov = k.dr['out'].rearrange("(n p) d -> n p d", p=128)
    for i in range(NT):
        S.op('dve', STT(k.X[:, i, :], k.X[:, i, :], k.rstd[:, i:i + 1], k.gb[:], ALU.mult, ALU.mult),
             reads=['rstd', 'gb'], writes=[('X', i, 0), ('X', i, 1)])
        S.dma('sp', DMA(ov[i], k.X[:, i, :]), reads=[('X', i, 0), ('X', i, 1)], writes=[('out', i)], chan='o%d' % i)


def phase_ffn(k, l):
    S, T, NT, nc = k.S, k.T, k.NT, k.nc
    dr = k.dr
    HALF = min(1024, T)
    NBLK = T // HALF
    NB = max(1, HALF // 512)
    BW = min(512, HALF)
    TPB = HALF // 128
    GRP = 4
    o_cw = PP_OFF['cw_ffn'][0]
    o_cb = PP_OFF['cb_ffn'][0]
    wupv = dr['w_up'][l].rearrange("(kc p) n -> p kc n", p=128)
    wdnv = dr['w_down'][l].rearrange("(fc p) n -> p fc n", p=128)

    with ExitStack() as es:
        def sb(name, shape, dt):
            return es.enter_context(nc.sbuf_tensor("sb_" + name + "_f%d" % l, shape, dt))
        upbuf = [[sb("upbuf%d%d" % (a, b), [128, HALF + 2], F32) for b in range(2)] for a in range(2)]
        acc = [[sb("acc%d%d" % (a, b), [128, HALF], F32) for b in range(2)] for a in range(2)]
        gT = [sb("gT%d" % a, [128, GRP, HALF], BF16) for a in range(2)]
        wup = [[sb("wup%d_%d" % (p_, a), [128, 8, 256], BF16) for a in range(2)] for p_ in range(2)]
        wdn = [sb("wdn%d" % a, [128, GRP, D], BF16) for a in range(2)]

        norm_to_T(k, 2 * l + 1)
        wctr = gctr = uctr = dctr = pctr = 0
        for blk in range(NBLK):
            t0 = blk * HALF
            groups = [list(range(g, min(g + GRP, NFC))) for g in range(0, NFC, GRP)]
            for grp in groups:
                gi = gctr % 2
                gctr += 1
                f0 = grp[0]
                ng = len(grp)
                S.dma('pool', DMA(wdn[gi][:, 0:ng, :], wdnv[:, f0:f0 + ng, :]), reads=[], writes=[('wdn', gi)], chan='wdn%d' % gi)
                for jj, j in enumerate(grp):
                    pb = pctr % 2
                    pctr += 1
                    for part in range(2):
                        fc = j + part * NFC
                        wr = (wctr // 2) % 2
                        wsub = jj % 2
                        if wsub == 0:
                            npair = min(2, ng - jj)
                            S.dma('pool', DMA(wup[part][wr][:, :, 0:npair * 128], wupv[:, :, fc * 128:(fc + npair) * 128]),
                                  reads=[], writes=[('wup', part, wr)], chan='wup%d_%d' % (part, wr))
                        wt = wup[part][wr][:, :, wsub * 128:(wsub + 1) * 128]
                        wkey = ('wup', part, wr)
                        ui = uctr % 2
                        uctr += 1
                        for nb in range(NB):
                            bank = ui * 2 + nb
                            hreads = [('hT', (t0 + nb * BW) // 128 + q) for q in range(BW // 128)]
                            for kc in range(8):
                                S.op('pe', MM(k.ps[bank][:, 0:BW], wt[:, kc, :],
                                              k.hT[:, kc, 1 + t0 + nb * BW:1 + t0 + (nb + 1) * BW], kc == 0, kc == 7),
                                     reads=[wkey] + hreads, writes=[('ps', bank)])
                        hb = 4 + ui
                        for kc in range(8):
                            S.op('pe', MM(k.ps[hb][:, 0:2], wt[:, kc, :], k.hT[:, kc, t0:t0 + HALF + 2:HALF + 1], kc == 0, kc == 7),
                                 reads=[wkey, ('hT', max(0, t0 // 128 - 1)), ('hT', min(NT - 1, (t0 + HALF) // 128)), 'hTpad'],
                                 writes=[('ps', hb)])
                        ubuf = upbuf[part][pb]
                        ac = acc[part][pb]
                        for nb in range(NB):
                            bank = ui * 2 + nb
                            S.op('act', ACT(ubuf[:, 1 + nb * BW:1 + (nb + 1) * BW], k.ps[bank][:, 0:BW], AF.Copy),
                                 reads=[('ps', bank)], writes=[('upbuf', part, pb, nb)])
                            S.op('act', ACT(ac[:, nb * BW:(nb + 1) * BW], k.ps[bank][:, 0:BW], AF.Identity,
                                            scale=k.pp[:, o_cw + 44 + fc:o_cw + 44 + fc + 1],
                                            bias=k.pp[:, o_cb + fc:o_cb + fc + 1]),
                                 reads=[('ps', bank), 'pp'], writes=[('acc', part, pb, nb)])
                        S.op('act', ACT(ubuf[:, 0:HALF + 2:HALF + 1], k.ps[hb][:, 0:2], AF.Copy),
                             reads=[('ps', hb)], writes=[('upbuf', part, pb, 'h')])
                        allub = [('upbuf', part, pb, nb) for nb in range(NB)] + [('upbuf', part, pb, 'h')]
                        allac = [('acc', part, pb, nb) for nb in range(NB)]
                        for tap in (0, 2):
                            S.op('dve', STT(ac[:], ubuf[:, tap:tap + HALF],
                                            k.pp[:, o_cw + tap * 44 + fc:o_cw + tap * 44 + fc + 1], ac[:], ALU.mult, ALU.add),
                                 reads=allub + ['pp'], writes=allac)
                    wctr += 1
                    ag = acc[0][pb]
                    av = acc[1][pb]
                    S.op('act', ACT(ag[:], ag[:], AF.Silu), reads=[], writes=[('acc', 0, pb, nb) for nb in range(NB)])
                    S.op('dve', TT(gT[gi][:, jj, :], ag[:], av[:], ALU.mult),
                         reads=[('acc', 0, pb, nb) for nb in range(NB)] + [('acc', 1, pb, nb) for nb in range(NB)],
                         writes=[('gT', gi, jj)])
                for tl in range(TPB):
                    ti = t0 // 128 + tl
                    for half in range(2):
                        bank = 6 + dctr % 2
                        dctr += 1
                        for jj in range(ng):
                            S.op('pe', MM(k.ps[bank][:, :], gT[gi][:, jj, tl * 128:(tl + 1) * 128],
                                          wdn[gi][:, jj, half * 512:(half + 1) * 512], jj == 0, jj == ng - 1),
                                 reads=[('gT', gi, jj), ('wdn', gi)], writes=[('ps', bank)])
                        xs = k.X[:, ti, half * 512:(half + 1) * 512]
                        S.op('dve', TT(xs, xs, k.ps[bank][:, :], ALU.add), reads=[('ps', bank)], writes=[('X', ti, half)])
        S.barrier()


def cmc(k, name, a=0, b=None):
    o, n = CM_OFF[name]
    return k.cm[:, o + a:o + (n if b is None else b)]


def ppc(k, name, a=0, b=None):
    o, n = PP_OFF[name]
    return k.pp[:, o + a:o + (n if b is None else b)]


def rsqrt_small(k, out, in_, n, scale, tmp, reads, writes, tmpkey):
    S = k.S
    S.op('act', ACT(tmp, in_, AF.Ln, scale=scale, bias=cmc(k, 'eps')), reads=list(reads) + ['cm'], writes=[tmpkey])
    S.op('act', ACT(out, tmp, AF.Exp, scale=-0.5), reads=[tmpkey], writes=writes)


def phase_mixer(k, l, groups):
    S, T, NT, nc = k.S, k.T, k.NT, k.nc
    norm_to_T(k, 2 * l)
    if 'A' in groups:
        group_A(k, l)
    if 'B' in groups:
        group_B(k, l)
    if 'C' in groups:
        group_C(k, l)


def load_w(k, dst, src, key, chan, queue='pool'):
    k.S.dma(queue, DMA(dst, src), reads=[], writes=[key], chan=chan)


def xT_and_wout(k, y_bf, nkc, wo, wokey, ti, ykey, yT, yTkey, tbank, obanks):
    S = k.S
    pst = k.ps[tbank][:].bitcast(BF16).rearrange("p (a b) -> p a b", a=8)
    for kc in range(nkc):
        S.op('pe', TR(pst[:, kc, :], y_bf[:, kc * 128:(kc + 1) * 128], k.ident[:]), reads=[ykey, 'ident'], writes=[('ps', tbank)])
    S.op('act', ACT(yT[:, 0:nkc, :], pst[:, 0:nkc, :], AF.Copy), reads=[('ps', tbank)], writes=[yTkey])
    for half in range(2):
        bank = obanks[half]
        with S.atomic():
            for kc in range(nkc):
                S.op('pe', MM(k.ps[bank][:, :], yT[:, kc, :], wo[:, kc, half * 512:(half + 1) * 512], kc == 0, kc == nkc - 1),
                     reads=[yTkey, wokey], writes=[('ps', bank)])
        xs = k.X[:, ti, half * 512:(half + 1) * 512]
        S.op('dve', TT(xs, xs, k.ps[bank][:, :], ALU.add), reads=[('ps', bank)], writes=[('X', ti, half)])


def group_A(k, l):
    S, T, NT, nc = k.S, k.T, k.NT, k.nc
    dr = k.dr
    winv = dr['w_in'][l].rearrange("(kc p) n -> p kc n", p=128)
    woutv = dr['w_out'][l].rearrange("(kc p) n -> p kc n", p=128)
    with ExitStack() as es:
        def sb(name, shape, dt):
            return es.enter_context(nc.sbuf_tensor("sb_" + name + "_a%d" % l, shape, dt))
        wA = sb("wA", [128, 8, 512], BF16)
        woA = sb("woA", [128, 2, D], BF16)
        wsT = sb("wsT", [128, 512], BF16)
        B = []
        for b in range(2):
            B.append(dict(uv=sb("uv%d" % b, [128, 512], F32), vt=sb("vt%d" % b, [128, 256], F32), ssq=sb("ssq%d" % b, [128, 4], F32),
                          rs4=sb("rs4%d" % b, [128, 4], F32), tm4=sb("tm4%d" % b, [128, 4], F32), vnb=sb("vnb%d" % b, [128, 256], BF16),
                          ya=sb("ya%d" % b, [128, 256], BF16), yaT=sb("yaT%d" % b, [128, 2, 128], BF16)))
        load_w(k, wA[:], winv[:, :, 0:512], 'wA', 'wA')
        load_w(k, woA[:], woutv[:, 0:2, :], 'woA', 'woA')
        load_w(k, wsT[:], dr['wsT'][l], 'wsT', 'wsT')

        def tile_ops(i):
            b = i % 2
            W = B[b]
            u, vt, ssq, rs4, tm4, vnb, ya, yaT = W['uv'], W['vt'], W['ssq'], W['rs4'], W['tm4'], W['vnb'], W['ya'], W['yaT']
            kb = 'A%d' % b
            pa, pg, ptr, po = b, 2 + b, 4 + b, 6 + b
            for kc in range(8):
                S.op('pe', MM(k.ps[pa][:, :], k.hT[:, kc, 1 + i * 128:1 + (i + 1) * 128], wA[:, kc, :], kc == 0, kc == 7),
                     reads=[('hT', i), 'wA'], writes=[('ps', pa)])
            S.op('act', ACT(u[:], k.ps[pa][:, :], AF.Gelu_apprx_tanh), reads=[('ps', pa)], writes=[kb + 'uv'])
            S.op('dve', TT(vt[:], u[:, 256:512], u[:, 256:512], ALU.mult), reads=[kb + 'uv'], writes=[kb + 'vt'])
            S.op('dve', RED(ssq[:], vt[:].rearrange("p (g e) -> p g e", g=4)), reads=[kb + 'vt'], writes=[kb + 'ssq'])
            rsqrt_small(k, rs4[:], ssq[:], 4, 1.0 / 64, tm4[:], [kb + 'ssq'], [kb + 'rs4'], kb + 'tm4')
            S.op('dve', TT(vt[:].rearrange("p (g e) -> p g e", g=4), u[:, 256:512].rearrange("p (g e) -> p g e", g=4),
                           rs4[:].unsqueeze(2).broadcast_to([128, 4, 64]), ALU.mult), reads=[kb + 'uv', kb + 'rs4'], writes=[kb + 'vt'])
            S.op('dve', TT(vnb[:], vt[:], ppc(k, 'gmn'), ALU.mult), reads=[kb + 'vt', 'pp'], writes=[kb + 'vnb'])
            for g in range(4):
                S.op('pe', MM(k.ps[pg][:, g * 64:(g + 1) * 64], wsT[:, g * 128:(g + 1) * 128], vnb[:, g * 64:(g + 1) * 64], True, True),
                     reads=['wsT', kb + 'vnb'], writes=[('ps', pg)])
            S.op('dve', TT(vt[:], k.ps[pg][:, 0:256], ppc(k, 'bsf'), ALU.add), reads=[('ps', pg), 'pp'], writes=[kb + 'vt'])
            S.op('dve', TT(ya[:], vt[:], u[:, 0:256], ALU.mult), reads=[kb + 'vt', kb + 'uv'], writes=[kb + 'ya'])
            xT_and_wout(k, ya, 2, woA, 'woA', i, kb + 'ya', yaT, kb + 'yaT', ptr, (po, po))
        for i in range(0, NT, 2):
            S.play(S.record(lambda: tile_ops(i)), S.record(lambda: tile_ops(i + 1)) if i + 1 < NT else [])
        S.barrier()


def gate_tables(k, l, sb):
    S, NT, nc = k.S, k.NT, k.nc
    winv = k.dr['w_in'][l].rearrange("(kc p) n -> p kc n", p=128)
    wg = sb("wg", [128, 8, 16], BF16)
    G_ig = sb("G_ig", [128, NT, 8], F32)
    G_lfn = sb("G_lfn", [128, NT, 8], F32)
    load_w(k, wg[:], winv[:, :, 2048:2064], 'wg', 'wg')
    for i in range(NT):
        for kc in range(8):
            S.op('pe', MM(k.ps[0][:, i * 16:(i + 1) * 16], k.hT[:, kc, 1 + i * 128:1 + (i + 1) * 128], wg[:, kc, :], kc == 0, kc == 7),
                 reads=[('hT', i), 'wg'], writes=[('ps', 0)])
    pv = k.ps[0][:, 0:NT * 16].rearrange("p (n c) -> p n c", c=16)
    mlb = ppc(k, 'mlb')
    S.op('dve', TT(G_ig[:], pv[:, :, 0:8], mlb[:, 0:8].unsqueeze(1).broadcast_to([128, NT, 8]), ALU.add),
         reads=[('ps', 0), 'pp'], writes=['G_ig'])
    S.op('dve', TT(G_lfn[:], pv[:, :, 8:16], mlb[:, 8:16].unsqueeze(1).broadcast_to([128, NT, 8]), ALU.add),
         reads=[('ps', 0), 'pp'], writes=['G_lfn'])
    S.op('act', ACT(G_lfn[:], G_lfn[:], AF.Exp, scale=-1.0), reads=[], writes=['G_lfn'])
    S.op('act', ACT(G_lfn[:], G_lfn[:], AF.Ln, bias=cmc(k, 'one')), reads=['cm'], writes=['G_lfn'])
    return G_ig, G_lfn


def scan_prep_common(k, d, lf, gatekey, HH, W, pfx, strict=False, pfxd=None):
    S = k.S
    N = HH * 128
    H2 = HH // 2
    pfxd = pfxd or pfx
    S.op('pool', TT(W['GxE'][:].rearrange("p (h t) -> p h t", h=HH),
                   cmc(k, 'U2n%d' % d).unsqueeze(1).broadcast_to([128, HH, 128]),
                   lf.unsqueeze(2).broadcast_to([128, HH, 128]), ALU.mult),
         reads=['cm', gatekey], writes=[pfx + 'GxE'])
    S.op('pe', MM(k.ps[0][:, 0:N], cmc(k, 'A2%d' % d), W['GxE'][:], True, False), reads=['cm', pfx + 'GxE'], writes=[('ps', 0)])
    S.op('pe', MM(k.ps[0][:, 0:N], k.ident[:], k.negfb[d][:, 0:N], False, True), reads=['ident', 'negfb'], writes=[('ps', 0)])
    S.op('act', ACT(W['decT'][:], k.ps[0][:, 0:N], AF.Exp), reads=[('ps', 0)], writes=[pfx + 'decT'])
    for h in range(HH):
        hp, h2 = h % 2, h // 2
        S.op('pe', MM(k.ps[1][64 * hp:64 * hp + 64, h2 * 128:(h2 + 1) * 128], cmc(k, 'ones'),
                      W['GxE'][:, h * 128:(h + 1) * 128], True, True),
             reads=['cm', pfx + 'GxE'], writes=[('ps', 1)])
    S.op('pe', MM(k.ps[1][:, 256:256 + HH], cmc(k, 'A2%d' % d), lf, True, True), reads=['cm', gatekey], writes=[('ps', 1)])
    S.op('act', ACT(W['Eexp'][:], k.ps[1][:, 0:H2 * 128], AF.Exp), reads=[('ps', 1)], writes=[pfxd + 'Eexp'])
    S.op('act', ACT(W['dend'][:], k.ps[1][:, 256:256 + HH], AF.Exp, scale=-1.0), reads=[('ps', 1)], writes=[pfxd + 'dend'])


def group_B(k, l):
    S, T, NT, nc = k.S, k.T, k.NT, k.nc
    dr = k.dr
    winv = dr['w_in'][l].rearrange("(kc p) n -> p kc n", p=128)
    woutv = dr['w_out'][l].rearrange("(kc p) n -> p kc n", p=128)
    TB = min(512, T)
    with ExitStack() as es0:
        def sb0(name, shape, dt):
            return es0.enter_context(nc.sbuf_tensor("sb_" + name + "_b%d" % l, shape, dt))
        G_ig, G_lfn = gate_tables(k, l, sb0)
        alloc = {}
        if DBG.get('stop') == 1:
            S.barrier()
            return
        for hpass in range(2):
            hb0 = 2 * hpass
            with ExitStack() as es:
                def sb(name, shape, dt):
                    if name not in alloc:
                        alloc[name] = es0.enter_context(nc.sbuf_tensor("sb_" + name + "_b%d" % l, shape, dt))
                    return alloc[name]
                wqk = sb("wqk", [128, 8, 256], BF16)
                wvo = sb("wvo", [128, 8, 512], BF16)
                woB = sb("woB", [128, 2, D], BF16)
                qT = sb("qT", [128, T], BF16)
                qTm = sb("qTm", [128, 2, T], BF16)
                kT = sb("kT", [128, T], BF16)
                ktok = sb("ktok", [128, NT, 128], BF16)
                vp = sb("vp", [128, NT, 2, 130], BF16)
                og = sb("og", [128, NT, 256], BF16)
                HB = sb("HB", [128, NT, 256], F32)
                Cst = sb("Cst", [128, 2, 130], F32)
                Cbf = sb("Cbf", [128, 2, 130], BF16)
                Wd = []
                for d in range(2):
                    W = dict(
                        GxE=sb("GxE%d" % d, [128, 256], F32),
                        decT=sb("decT%d" % d, [128, 256], F32), dend=sb("dend%d" % d, [128, 2], F32),
                        Eexp=sb("Eexp%d" % d, [128, 128], F32), eig=sb("eig%d" % d, [128, 2], F32),
                        qtm=sb("qtm%d" % d, [128, 2, 128], BF16), STm=sb("STm%d" % d, [128, 256], BF16),
                        vw=sb("vw%d" % d, [128, 2, 130], BF16), kdm=sb("kdm%d" % d, [128, 2, 128], BF16),
                        den=sb("den%d" % d, [128, 2], F32), rden=sb("rden%d" % d, [128, 2], F32),
                        tmp=sb("tmpo%d" % d, [128, 256], F32))
                    Wd.append(W)
                    S.op('pool', MSET(W['qtm'][:], 0.0), writes=['B%dqtm' % d])
                    S.op('pool', MSET(W['kdm'][:], 0.0), writes=['B%dkdm' % d])
                vt = sb("fvt", [128, 256], F32)
                ssq = sb("fssq", [128, 2], F32)
                rs2 = sb("frs2", [128, 2], F32)
                tm2 = sb("ftm2", [128, 2], F32)
                yb = sb("yb", [128, 256], BF16)
                ybT = sb("ybT", [128, 2, 128], BF16)

                load_w(k, wqk[:, :, 0:128], winv[:, :, 512 + hb0 * 64:512 + hb0 * 64 + 128], 'wqk', 'wqk')
                load_w(k, wqk[:, :, 128:256], winv[:, :, 768 + hb0 * 64:768 + hb0 * 64 + 128], 'wqk2', 'wqk2')
                load_w(k, wvo[:, :, 0:256], winv[:, :, 1024 + hb0 * 128:1024 + hb0 * 128 + 256], 'wvo', 'wvo')
                load_w(k, wvo[:, :, 256:512], winv[:, :, 1536 + hb0 * 128:1536 + hb0 * 128 + 256], 'wvo2', 'wvo2')
                load_w(k, woB[:], woutv[:, 2 + hb0:2 + hb0 + 2, :], 'woB', 'woB')
                S.op('pool', MSET(vp[:], 1.0), writes=['vp1'] + [('vp', i) for i in range(NT)])
                S.op('pool', MSET(Cst[:], 0.0), writes=['Cst0', 'Cst1'])
                S.op('pool', MSET(Cbf[:], 0.0), writes=['Cbf0', 'Cbf1'])
                S.op('pool', MSET(qTm[:], 0.0), writes=[('qTm', tb) for tb in range(T // TB)])
                if DBG.get('stop') == 21:
                    S.barrier()
                    return
                for tb in range(T // TB):
                    hreads = [('hT', tb * (TB // 128) + q) for q in range(TB // 128)]
                    sl = slice(tb * TB, (tb + 1) * TB)
                    for c in range(2):
                        bank = 2 + c
                        for kc in range(8):
                            S.op('pe', MM(k.ps[bank][:, 0:TB], wqk[:, kc, c * 128:(c + 1) * 128],
                                          k.hT[:, kc, 1 + tb * TB:1 + (tb + 1) * TB], kc == 0, kc == 7),
                                 reads=['wqk', 'wqk2'] + hreads, writes=[('ps', bank)])
                    S.op('act', ACT(qT[:, sl], k.ps[2][:, 0:TB], AF.Copy, scale=0.125), reads=[('ps', 2)], writes=[('qT', tb)])
                    S.op('act', ACT(qTm[0:64, 0, sl], k.ps[2][0:64, 0:TB], AF.Copy, scale=0.125), reads=[('ps', 2)], writes=[('qTm', tb)])
                    S.op('act', ACT(qTm[64:128, 1, sl], k.ps[2][64:128, 0:TB], AF.Copy, scale=0.125), reads=[('ps', 2)], writes=[('qTm', tb)])
                    S.op('act', ACT(kT[:, sl], k.ps[3][:, 0:TB], AF.Copy), reads=[('ps', 3)], writes=[('kT', tb)])
                if DBG.get('stop') == 22:
                    S.barrier()
                    return
                for i in range(NT):
                    b0, b1 = 4 + (i % 2) * 2, 5 + (i % 2) * 2
                    for kc in range(8):
                        S.op('pe', MM(k.ps[b0][:, 0:128], k.hT[:, kc, 1 + i * 128:1 + (i + 1) * 128], wqk[:, kc, 128:256], kc == 0, kc == 7),
                             reads=[('hT', i), 'wqk2'], writes=[('ps', b0)])
                    for kc in range(8):
                        S.op('pe', MM(k.ps[b1][:, :], k.hT[:, kc, 1 + i * 128:1 + (i + 1) * 128], wvo[:, kc, :], kc == 0, kc == 7),
                             reads=[('hT', i), 'wvo', 'wvo2'], writes=[('ps', b1)])
                    S.op('act', ACT(ktok[:, i, :], k.ps[b0][:, 0:128], AF.Copy), reads=[('ps', b0)], writes=[('ktok', i)])
                    S.op('act', ACT(vp[:, i, :, 0:128], k.ps[b1][:, 0:256].rearrange("p (h v) -> p h v", h=2), AF.Copy),
                         reads=[('ps', b1)], writes=[('vp', i)])
                    S.op('act', ACT(og[:, i, :], k.ps[b1][:, 256:512], AF.Sigmoid), reads=[('ps', b1)], writes=[('og', i)])
                    S.op('pool', TT(og[:, i, :], og[:, i, :], ppc(k, 'mln', hb0 * 128, hb0 * 128 + 256), ALU.mult), reads=['pp'], writes=[('og', i)])

                if DBG.get('stop') == 2:
                    S.barrier()
                    return

                def prep(d, ti):
                    W = Wd[d]
                    pfx = 'B%d' % d
                    tb = ti * 128 // TB
                    tsl = slice(ti * 128, (ti + 1) * 128)
                    lf2 = G_lfn[:, ti, d * 4 + hb0:d * 4 + hb0 + 2]
                    scan_prep_common(k, d, lf2, 'G_lfn', 2, W, pfx)
                    S.op('act', ACT(W['eig'][:], G_ig[:, ti, d * 4 + hb0:d * 4 + hb0 + 2], AF.Exp), reads=['G_ig'], writes=[pfx + 'eig'])
                    for hp in range(2):
                        HP = slice(64 * hp, 64 * hp + 64)
                        S.op('pool', TT(W['qtm'][HP, hp, :], qT[HP, tsl], W['Eexp'][HP, :], ALU.mult),
                             reads=[('qT', tb), pfx + 'Eexp'], writes=[pfx + 'qtm'])
                    for hp in range(2):
                        S.op('pe', MM(k.ps[2][:, hp * 128:(hp + 1) * 128], kT[:, tsl], qTm[:, hp, tsl], True, True),
                             reads=[('kT', tb), ('qTm', tb)], writes=[('ps', 2)])
                    S.op('dve', TT(W['STm'][:], k.ps[2][:, 0:256], W['decT'][:], ALU.mult), reads=[('ps', 2), pfx + 'decT'], writes=[pfx + 'STm'])
                    S.op('pool', TT(W['vw'][:], vp[:, ti, :, :], W['eig'][:].unsqueeze(2).broadcast_to([128, 2, 130]), ALU.mult),
                         reads=[('vp', ti), 'vp1', pfx + 'eig'], writes=[pfx + 'vw'])
                    for p in range(2):
                        P = slice(64 * p, 64 * p + 64)
                        S.op('pool', TT(W['kdm'][P, p, :].rearrange("p (h c) -> p h c", h=2), ktok[P, ti, :].rearrange("p (h c) -> p h c", h=2),
                                       W['dend'][P, :].unsqueeze(2).broadcast_to([64, 2, 64]), ALU.mult),
                             reads=[('ktok', ti), pfx + 'dend'], writes=[pfx + 'kdm'])

                def chain(d, ti, p, second):
                    W = Wd[d]
                    pfx = 'B%d' % d
                    P = slice(64 * p, 64 * p + 64)
                    for hp in range(2):
                        o_ = k.ps[3][P, hp * 130:(hp + 1) * 130]
                        S.op('pe', MM(o_, W['STm'][:, hp * 128 + 64 * p:hp * 128 + 64 * p + 64], W['vw'][:, hp, :], True, False),
                             reads=[pfx + 'STm', pfx + 'vw'], writes=[('ps', 3)])
                        S.op('pe', MM(o_, W['qtm'][:, hp, 64 * p:64 * p + 64], Cbf[:, d, :], False, True),
                             reads=[pfx + 'qtm', 'Cbf%d' % d], writes=[('ps', 3)])
                    for hp in range(2):
                        S.op('pe', MM(k.ps[4][64 * hp:64 * hp + 64, 0:130], W['kdm'][:, p, hp * 64:(hp + 1) * 64], W['vw'][:, hp, :], True, True),
                             reads=[pfx + 'kdm', pfx + 'vw'], writes=[('ps', 4)])
                    tlast = 64 * p + 63 if d == 0 else 64 * p
                    S.op('dve', STT(Cst[:, d, :], Cst[:, d, :], W['Eexp'][:, tlast:tlast + 1], k.ps[4][:, 0:130], ALU.mult, ALU.add),
                         reads=[pfx + 'Eexp', ('ps', 4)], writes=['Cst%d' % d])
                    S.op('act', ACT(Cbf[:, d, :], Cst[:, d, :], AF.Copy), reads=['Cst%d' % d], writes=['Cbf%d' % d])

                def norm_out(d, ti, second):
                    W = Wd[d]
                    pfx = 'B%d' % d
                    dcol = k.ps[3][:, 128:260:130]
                    S.op('dve', TS(W['den'][:], dcol, 1.0, None, ALU.max), reads=[('ps', 3)], writes=[pfx + 'den'])
                    S.op('dve', STT(W['den'][:], dcol, -1.0, W['den'][:], ALU.mult, ALU.max), reads=[('ps', 3)], writes=[pfx + 'den'])
                    den_ap, rden_ap = W['den'][:], W['rden'][:]
                    S.op('dve', (lambda a, b_: lambda e: e.reciprocal(out=a, in_=b_))(rden_ap, den_ap), reads=[pfx + 'den'], writes=[pfx + 'rden'])
                    src = k.ps[3][:, 0:260].rearrange("p (h c) -> p h c", h=2)[:, :, 0:128]
                    rb = W['rden'][:].unsqueeze(2).broadcast_to([128, 2, 128])
                    hbv = HB[:, ti, :].rearrange("p (h c) -> p h c", h=2)
                    hk = [('HB', ti, 0), ('HB', ti, 1)]
                    if not second:
                        S.op('dve', TT(hbv, src, rb, ALU.mult), reads=[('ps', 3), pfx + 'rden'], writes=hk)
                    else:
                        tv = W['tmp'][:].rearrange("p (h c) -> p h c", h=2)
                        S.op('dve', TT(tv, src, rb, ALU.mult), reads=[('ps', 3), pfx + 'rden'], writes=[pfx + 'tmp'])
                        S.op('dve', TT(HB[:, ti, :], HB[:, ti, :], W['tmp'][:], ALU.add), reads=[pfx + 'tmp'], writes=hk)

                def finalize(ti):
                    hb = HB[:, ti, :]
                    hbk = [('HB', ti, 0), ('HB', ti, 1)]
                    for h in range(2):
                        hs = slice(h * 128, (h + 1) * 128)
                        S.op('act', ACT(vt[:, hs], hb[:, hs], AF.Square, accum_out=ssq[:, h:h + 1]), reads=hbk, writes=['fvt', ('fssq', h)])
                    rsqrt_small(k, rs2[:], ssq[:], 2, 1.0 / 128, tm2[:], [('fssq', 0), ('fssq', 1)], ['frs2'], 'ftm2')
                    for h in range(2):
                        hs = slice(h * 128, (h + 1) * 128)
                        S.op('dve', STT(yb[:, hs], hb[:, hs], rs2[:, h:h + 1], og[:, ti, hs], ALU.mult, ALU.mult),
                             reads=['frs2', ('og', ti)] + hbk, writes=['yb'])
                    xT_and_wout(k, yb, 2, woB, 'woB', ti, 'yb', ybT, 'ybT', 5, (6, 7))

                touched = [False] * NT
                stages = []
                for step in range(NT):
                    for d in range(2):
                        ti = step if d == 0 else NT - 1 - step
                        second = touched[ti]

                        def body(d=d, ti=ti, second=second):
                            for p in ((0, 1) if d == 0 else (1, 0)):
                                chain(d, ti, p, second)
                            norm_out(d, ti, second)
                            if second:
                                finalize(ti)
                        stages.append(((lambda d=d, ti=ti: prep(d, ti)), body))
                        touched[ti] = True
                S.play(S.record(stages[0][0]))
                for i in range(len(stages)):
                    nxt = S.record(stages[i + 1][0]) if i + 1 < len(stages) else []
                    S.play(S.record(stages[i][1]), nxt)
        S.barrier()


def gdn_gate_tables(k, l, sb):
    S, NT, nc = k.S, k.NT, k.nc
    winv = k.dr['w_in'][l].rearrange("(kc p) n -> p kc n", p=128)
    wab = sb("wab", [128, 8, 16], BF16)
    gn = sb("gn", [128, NT, 8], F32)
    beta = sb("beta", [128, NT, 8], F32)
    nbeta = sb("nbeta", [128, NT, 8], F32)
    eA = sb("eA", [128, 8], F32)
    load_w(k, wab[:], winv[:, :, 3088:3104], 'wab', 'wab')
    for i in range(NT):
        for kc in range(8):
            S.op('pe', MM(k.ps[0][:, i * 16:(i + 1) * 16], k.hT[:, kc, 1 + i * 128:1 + (i + 1) * 128], wab[:, kc, :], kc == 0, kc == 7),
                 reads=[('hT', i), 'wab'], writes=[('ps', 0)])
    pv = k.ps[0][:, 0:NT * 16].rearrange("p (n c) -> p n c", c=16)
    S.op('dve', TT(gn[:], pv[:, :, 0:8], ppc(k, 'dtb').unsqueeze(1).broadcast_to([128, NT, 8]), ALU.add),
         reads=[('ps', 0), 'pp'], writes=['gn'])
    S.op('dve', CP(beta[:], pv[:, :, 8:16]), reads=[('ps', 0)], writes=['beta'])
    S.op('act', ACT(gn[:], gn[:], AF.Exp), reads=[], writes=['gn'])
    S.op('act', ACT(gn[:], gn[:], AF.Ln, bias=cmc(k, 'one')), reads=['cm'], writes=['gn'])
    S.op('act', ACT(eA[:], ppc(k, 'alog'), AF.Exp), reads=['pp'], writes=['eA'])
    S.op('dve', TT(gn[:], gn[:], eA[:].unsqueeze(1).broadcast_to([128, NT, 8]), ALU.mult), reads=['eA'], writes=['gn'])
    S.op('act', ACT(beta[:], beta[:], AF.Sigmoid), reads=[], writes=['beta'])
    S.op('dve', TS(nbeta[:], beta[:], -1.0, None, ALU.mult), reads=['beta'], writes=['nbeta'])
    return gn, beta, nbeta


def group_C(k, l):
    S, T, NT, nc = k.S, k.T, k.NT, k.nc
    dr = k.dr
    winv = dr['w_in'][l].rearrange("(kc p) n -> p kc n", p=128)
    woutv = dr['w_out'][l].rearrange("(kc p) n -> p kc n", p=128)
    TB = min(512, T)
    NTB = T // TB
    o_cg = PP_OFF['cw_gd'][0]
    identf = cmc(k, 'ident')
    with ExitStack() as es0:
        def sb0(name, shape, dt):
            return es0.enter_context(nc.sbuf_tensor("sb_" + name + "_c%d" % l, shape, dt))
        gn, beta, nbeta = gdn_gate_tables(k, l, sb0)
        for hpass in range(2):
            hb0 = 2 * hpass
            with ExitStack() as esp:
                def sbp(name, shape, dt):
                    return esp.enter_context(nc.sbuf_tensor("sb_" + name + "_c%d_%d" % (l, hpass), shape, dt))
                qT = sbp("qT", [128, T], BF16)
                kT = sbp("kT", [128, T], BF16)
                ktok = sbp("ktok", [128, NT, 128], BF16)
                vtok = sbp("vtok", [128, NT, 128], BF16)
                zg = sbp("zg", [128, NT, 128], BF16)
                with ExitStack() as es:
                    def sb(name, shape, dt):
                        return es.enter_context(nc.sbuf_tensor("sb_" + name + "_c1%d_%d" % (l, hpass), shape, dt))
                    convbufs = [sb("convbuf%d" % i, [128, T + 4], F32) for i in range(2)]
                    accs = [sb("cacc%d" % i, [128, T], F32) for i in range(2)]
                    xs = sb("cxs", [128, T], BF16)
                    sqf = [sb("sqf%d" % i, [128, TB], F32) for i in range(2)]
                    lnb = [sb("lnb%d" % i, [128, TB], F32) for i in range(2)]
                    wc = [sb("wc%d" % i, [128, 8, 128], BF16) for i in range(2)]
                    wz = sb("wz", [128, 8, 128], BF16)
                    for cb_ in convbufs:
                        S.op('pool', MSET(cb_[:, 0:2], 0.0), writes=['cbpad'])
                        S.op('pool', MSET(cb_[:, T + 2:T + 4], 0.0), writes=['cbpad'])
                    for ci, cc in enumerate((hpass, 2 + hpass, 4 + hpass)):
                        w_ = wc[ci % 2]
                        convbuf = convbufs[ci % 2]
                        acc = accs[ci % 2]
                        ck = 'cacc%d' % (ci % 2)
                        load_w(k, w_[:], winv[:, :, 2064 + cc * 128:2064 + (cc + 1) * 128], ('wc', ci % 2), 'wc%d' % (ci % 2))
                        for tb in range(NTB):
                            bank = tb % 2
                            hreads = [('hT', tb * (TB // 128) + q) for q in range(TB // 128)]
                            for kc in range(8):
                                S.op('pe', MM(k.ps[bank][:, 0:TB], w_[:, kc, :], k.hT[:, kc, 1 + tb * TB:1 + (tb + 1) * TB], kc == 0, kc == 7),
                                     reads=[('wc', ci % 2)] + hreads, writes=[('ps', bank)])
                            S.op('act', ACT(convbuf[:, 2 + tb * TB:2 + (tb + 1) * TB], k.ps[bank][:, 0:TB], AF.Copy),
                                 reads=[('ps', bank)], writes=[('cb', ci % 2, tb)])
                        cbk = [('cb', ci % 2, tb) for tb in range(NTB)] + ['cbpad']
                        for j in range(5):
                            wj = k.pp[:, o_cg + j * 6 + cc:o_cg + j * 6 + cc + 1]
                            if j == 0:
                                S.op('dve', TS(acc[:], convbuf[:, 0:T], wj, None, ALU.mult), reads=cbk + ['pp'], writes=[ck])
                            else:
                                S.op('dve', STT(acc[:], convbuf[:, j:j + T], wj, acc[:], ALU.mult, ALU.add), reads=cbk + ['pp'], writes=[ck])
                        if ci == 2:
                            S.op('act', ACT(xs[:], acc[:], AF.Silu), reads=[ck], writes=['cxs'])
                            for i in range(NT):
                                bank = 2 + i % 2
                                pst = k.ps[bank][:].bitcast(BF16)
                                S.op('pe', TR(pst[:, 0:128], xs[:, i * 128:(i + 1) * 128], k.ident[:]), reads=['cxs', 'ident'], writes=[('ps', bank)])
                                S.op('act', ACT(vtok[:, i, :], pst[:, 0:128], AF.Copy), reads=[('ps', bank)], writes=[('vtok', i)])
                        else:
                            S.op('act', ACT(acc[:], acc[:], AF.Silu), reads=[], writes=[ck])
                            dst = qT if ci == 0 else kT
                            dkey = 'cqT' if ci == 0 else 'ckT'
                            for tb in range(NTB):
                                b = tb % 2
                                bank = 2 + b
                                sl = slice(tb * TB, (tb + 1) * TB)
                                S.op('dve', TT(sqf[b][:], acc[:, sl], acc[:, sl], ALU.mult), reads=[ck], writes=[('sqf', b)])
                                S.op('pe', MM(k.ps[bank][:, 0:TB], cmc(k, 'BD'), sqf[b][:], True, True), reads=['cm', ('sqf', b)], writes=[('ps', bank)])
                                S.op('act', ACT(lnb[b][:], k.ps[bank][:, 0:TB], AF.Ln, bias=cmc(k, 'eps')), reads=[('ps', bank), 'cm'], writes=[('lnb', b)])
                                S.op('act', ACT(lnb[b][:], lnb[b][:], AF.Exp, scale=-0.5), reads=[], writes=[('lnb', b)])
                                if ci == 0:
                                    S.op('dve', STT(dst[:, sl], acc[:, sl], 0.125, lnb[b][:], ALU.mult, ALU.mult), reads=[ck, ('lnb', b)], writes=[(dkey, tb)])
                                else:
                                    S.op('dve', TT(dst[:, sl], acc[:, sl], lnb[b][:], ALU.mult), reads=[ck, ('lnb', b)], writes=[(dkey, tb)])
                            if ci == 1:
                                for i in range(NT):
                                    bank = 4 + i % 2
                                    pst = k.ps[bank][:].bitcast(BF16)
                                    S.op('pe', TR(pst[:, 0:128], kT[:, i * 128:(i + 1) * 128], k.ident[:]), reads=[('ckT', i * 128 // TB), 'ident'], writes=[('ps', bank)])
                                    S.op('act', ACT(ktok[:, i, :], pst[:, 0:128], AF.Copy), reads=[('ps', bank)], writes=[('cktok', i)])
                    load_w(k, wz[:], winv[:, :, 2832 + hpass * 128:2832 + (hpass + 1) * 128], 'wz', 'wz')
                    for i in range(NT):
                        bank = 6 + i % 2
                        for kc in range(8):
                            S.op('pe', MM(k.ps[bank][:, 0:128], k.hT[:, kc, 1 + i * 128:1 + (i + 1) * 128], wz[:, kc, :], kc == 0, kc == 7),
                                 reads=[('hT', i), 'wz'], writes=[('ps', bank)])
                        S.op('act', ACT(zg[:, i, :], k.ps[bank][:, 0:128], AF.Silu), reads=[('ps', bank)], writes=[('zg', i)])
                        S.op('pool', TT(zg[:, i, :], zg[:, i, :], ppc(k, 'gdn', hb0 * 64, hb0 * 64 + 128), ALU.mult), reads=['pp'], writes=[('zg', i)])
                    S.barrier()
                if DBG.get('stop') == 31:
                    S.barrier()
                    return
                with ExitStack() as es:
                    def sb(name, shape, dt):
                        return es.enter_context(nc.sbuf_tensor("sb_" + name + "_c2%d_%d" % (l, hpass), shape, dt))
                    OC = sb("OC", [128, NT, 128], F32)
                    woC = sb("woC", [128, 1, D], BF16)
                    Sst = sb("Sst", [128, 2, 64], F32)
                    Sbf = sb("Sbf", [128, 2, 64], BF16)
                    load_w(k, woC[:], woutv[:, 6 + hpass:7 + hpass, :], 'woC', 'woC')
                    S.op('pool', MSET(Sst[:], 0.0), writes=['Sst0', 'Sst1'])
                    S.op('pool', MSET(Sbf[:], 0.0), writes=['Sbf0', 'Sbf1'])
                    GxE = sb("GxE", [128, 256], F32)
                    decT = sb("decT", [128, 256], F32)
                    nbod = sb("nbod", [128, 256], F32)
                    offd = sb("offd", [128, 256], F32)
                    tmpf = sb("tmpf", [128, 256], F32)
                    qm = sb("qm", [128, 2, 128], BF16)
                    identf2 = sb("identf2", [128, 256], F32)
                    S.op('pool', MSET(qm[:], 0.0), writes=['Cqm'])
                    for hp in range(2):
                        S.op('dve', CP(identf2[:, hp * 128:(hp + 1) * 128], identf), reads=['cm'], writes=['identf2'])
                        S.op('dve', TS(offd[:, hp * 128:(hp + 1) * 128], identf, -1.0, 1.0, ALU.mult, ALU.add), reads=['cm'], writes=['offd'])
                    Dd = []
                    Wd = []
                    for d in range(2):
                        Dd.append(dict(
                            Qb=[sb("Qb%d_%d" % (d, i), [128, 256], BF16) for i in range(2)],
                            Pb=[sb("Pb%d_%d" % (d, i), [128, 256], BF16) for i in range(2)],
                            Wb=[sb("Wb%d_%d" % (d, i), [128, 256], BF16) for i in range(2)],
                            Wf=sb("Wf%d" % d, [128, 256], F32),
                            Wfin=sb("Wfin%d" % d, [128, 256], BF16), R0=sb("R0%d" % d, [128, 128], BF16),
                            vnew=sb("vnew%d" % d, [128, 128], BF16)))
                        for nm in ('R0', 'vnew'):
                            S.op('pool', MSET(Dd[d][nm][:], 0.0), writes=['C%d%s' % (d, nm)])
                        row = []
                        for par in range(2):
                            W = dict(GxE=GxE, decT=decT,
                                     Eexp=sb("Eexp%d%d" % (d, par), [128, 128], F32), dend=sb("dend%d%d" % (d, par), [128, 2], F32),
                                     qtm=sb("qtm%d%d" % (d, par), [128, 2, 128], BF16), ktm=sb("ktm%d%d" % (d, par), [128, 2, 128], BF16),
                                     kdm=sb("kdm%d%d" % (d, par), [128, 2, 128], BF16), attnT=sb("attnT%d%d" % (d, par), [128, 256], BF16))
                            for nm in ('qtm', 'ktm', 'kdm'):
                                S.op('pool', MSET(W[nm][:], 0.0), writes=['C%d%d%s' % (d, par, nm)])
                            row.append(W)
                        Wd.append(row)
                    vt = sb("fvt", [128, 128], F32)
                    ssq = sb("fssq", [128, 2], F32)
                    rs2 = sb("frs2", [128, 2], F32)
                    tm2 = sb("ftm2", [128, 2], F32)
                    yc = sb("yc", [128, 128], BF16)
                    ycT = sb("ycT", [128, 1, 128], BF16)

                    def prepA(d, ti, par):
                        W = Wd[d][par]
                        Dx = Dd[d]
                        Qb, Pb, Wb, Wf = Dx['Qb'], Dx['Pb'], Dx['Wb'], Dx['Wf']
                        pfx = 'C%d' % d
                        pf2 = 'C%d%d' % (d, par)
                        tb = ti * 128 // TB
                        tsl = slice(ti * 128, (ti + 1) * 128)
                        lf2 = gn[:, ti, d * 4 + hb0:d * 4 + hb0 + 2]
                        scan_prep_common(k, d, lf2, 'gn', 2, W, 'CS', pfxd=pf2)
                        for hp in range(2):
                            HP = slice(64 * hp, 64 * hp + 64)
                            S.op('pool', TT(W['qtm'][HP, hp, :], qT[HP, tsl], W['Eexp'][HP, :], ALU.mult),
                                 reads=[('cqT', tb), pf2 + 'Eexp'], writes=[pf2 + 'qtm'])
                            S.op('pool', TT(W['ktm'][HP, hp, :], kT[HP, tsl], W['Eexp'][HP, :], ALU.mult),
                                 reads=[('ckT', tb), pf2 + 'Eexp'], writes=[pf2 + 'ktm'])
                            S.op('act', ACT(qm[HP, hp, :], qT[HP, tsl], AF.Copy), reads=[('cqT', tb)], writes=['Cqm'])
                        for p in range(2):
                            P = slice(64 * p, 64 * p + 64)
                            S.op('pool', TT(W['kdm'][P, p, :].rearrange("p (h c) -> p h c", h=2), ktok[P, ti, :].rearrange("p (h c) -> p h c", h=2),
                                            W['dend'][P, :].unsqueeze(2).broadcast_to([64, 2, 64]), ALU.mult),
                                 reads=[('cktok', ti), pf2 + 'dend'], writes=[pf2 + 'kdm'])
                        for hp in range(2):
                            S.op('pe', MM(k.ps[3][:, hp * 128:(hp + 1) * 128], kT[:, tsl], qm[:, hp, :], True, True),
                                 reads=[('ckT', tb), 'Cqm'], writes=[('ps', 3)])
                        S.op('dve', TT(W['attnT'][:], k.ps[3][:, 0:256], decT[:], ALU.mult), reads=[('ps', 3), 'CSdecT'], writes=[pf2 + 'attnT'])
                        for hp in range(2):
                            HP = slice(64 * hp, 64 * hp + 64)
                            S.op('act', ACT(qm[HP, hp, :], kT[HP, tsl], AF.Copy), reads=[('ckT', tb)], writes=['Cqm'])
                        for hp in range(2):
                            S.op('pe', MM(k.ps[3][:, 256 + hp * 128:256 + (hp + 1) * 128], kT[:, tsl], qm[:, hp, :], True, True),
                                 reads=[('ckT', tb), 'Cqm'], writes=[('ps', 3)])
                        nb = nbeta[:, ti, d * 4 + hb0:d * 4 + hb0 + 2]
                        S.op('pool', TT(nbod[:].rearrange("p (h t) -> p h t", h=2), offd[:].rearrange("p (h t) -> p h t", h=2),
                                        nb.unsqueeze(2).broadcast_to([128, 2, 128]), ALU.mult), reads=['offd', 'nbeta'], writes=['Cnbod'])
                        S.op('dve', TT(tmpf[:], k.ps[3][:, 256:512], decT[:], ALU.mult), reads=[('ps', 3), 'CSdecT'], writes=['Ctmpf'])
                        S.op('dve', TT(Qb[0][:], tmpf[:], nbod[:], ALU.mult), reads=['Ctmpf', 'Cnbod'], writes=[(pfx + 'Q', 0)])
                        for hp in range(2):
                            S.op('pe', MM(k.ps[2][:, hp * 128:(hp + 1) * 128], Qb[0][:, hp * 128:(hp + 1) * 128], k.ident[:], True, True),
                                 reads=[(pfx + 'Q', 0), 'ident'], writes=[('ps', 2)])
                        S.op('dve', CP(Pb[0][:], k.ps[2][:, 0:256]), reads=[('ps', 2)], writes=[(pfx + 'P', 0)])
                        S.op('dve', TT(Wf[:], Qb[0][:], identf2[:], ALU.add), reads=[(pfx + 'Q', 0), 'identf2'], writes=[pfx + 'Wf'])
                        S.op('act', ACT(Wb[0][:], Wf[:], AF.Copy), reads=[pfx + 'Wf'], writes=[(pfx + 'Wb', 0)])

                    def prepB(d, ti, par):
                        Dx = Dd[d]
                        Qb, Pb, Wb, Wf = Dx['Qb'], Dx['Pb'], Dx['Wb'], Dx['Wf']
                        pfx = 'C%d' % d
                        NL = 5
                        for j in range(NL):
                            a, b_ = j % 2, (j + 1) % 2
                            last = (j == NL - 1)
                            if not last:
                                for hp in range(2):
                                    c_ = slice(hp * 128, (hp + 1) * 128)
                                    S.op('pe', MM(k.ps[4][:, c_], Pb[a][:, c_], Qb[a][:, c_], True, True),
                                         reads=[(pfx + 'P', a), (pfx + 'Q', a)], writes=[('ps', 4)])
                                S.op('act', ACT(Qb[b_][:], k.ps[4][:, 0:256], AF.Copy), reads=[('ps', 4)], writes=[(pfx + 'Q', b_)])
                            for hp in range(2):
                                c_ = slice(hp * 128, (hp + 1) * 128)
                                S.op('pe', MM(k.ps[5][:, c_], Qb[a][:, c_], Pb[a][:, c_], True, True),
                                     reads=[(pfx + 'P', a), (pfx + 'Q', a)], writes=[('ps', 5)])
                            S.op('dve', CP(Pb[b_][:], k.ps[5][:, 0:256]), reads=[('ps', 5)], writes=[(pfx + 'P', b_)])
                            for hp in range(2):
                                c_ = slice(hp * 128, (hp + 1) * 128)
                                S.op('pe', MM(k.ps[6][:, c_], Pb[b_][:, c_], Wb[a][:, c_], True, True),
                                     reads=[(pfx + 'P', b_), (pfx + 'Wb', a)], writes=[('ps', 6)])
                            if not last:
                                S.op('dve', TT(Wf[:], Wf[:], k.ps[6][:, 0:256], ALU.add), reads=[('ps', 6)], writes=[pfx + 'Wf'])
                                S.op('act', ACT(Wb[b_][:], Wf[:], AF.Copy), reads=[pfx + 'Wf'], writes=[(pfx + 'Wb', b_)])
                            else:
                                S.op('dve', TT(Dx['Wfin'][:], Wf[:], k.ps[6][:, 0:256], ALU.add), reads=[('ps', 6), pfx + 'Wf'], writes=[pfx + 'Wfin'])

                    def chain(d, ti, p, second, par):
                        W = dict(Wd[d][par])
                        W.update(Wfin=Dd[d]['Wfin'], R0=Dd[d]['R0'], vnew=Dd[d]['vnew'])
                        pfx = 'C%d' % d
                        pf2 = 'C%d%d' % (d, par)
                        P = slice(64 * p, 64 * p + 64)
                        cs = slice(64 * p, 64 * p + 64)
                        for hp in range(2):
                            S.op('pe', MM(k.ps[7][P, hp * 64:(hp + 1) * 64], W['ktm'][:, hp, cs], Sbf[:, d, :], True, True),
                                 reads=[pf2 + 'ktm', 'Sbf%d' % d], writes=[('ps', 7)])
                        S.op('dve', TT(W['R0'][P, :], vtok[P, ti, :], k.ps[7][P, 0:128], ALU.subtract), reads=[('vtok', ti), ('ps', 7)], writes=[pfx + 'R0'])
                        for hp in range(2):
                            S.op('pe', MM(k.ps[7][P, 128 + hp * 64:128 + (hp + 1) * 64], W['Wfin'][:, hp * 128 + 64 * p:hp * 128 + 64 * p + 64],
                                          W['R0'][:, hp * 64:(hp + 1) * 64], True, True),
                                 reads=[pfx + 'Wfin', pfx + 'R0'], writes=[('ps', 7)])
                        bt = beta[P, ti, d * 4 + hb0:d * 4 + hb0 + 2]
                        S.op('dve', TT(W['vnew'][P, :].rearrange("p (h c) -> p h c", h=2), k.ps[7][P, 128:256].rearrange("p (h c) -> p h c", h=2),
                                       bt.unsqueeze(2).broadcast_to([64, 2, 64]), ALU.mult), reads=[('ps', 7), 'beta'], writes=[pfx + 'vnew'])
                        for hp in range(2):
                            o_ = k.ps[7][P, 256 + hp * 64:256 + (hp + 1) * 64]
                            S.op('pe', MM(o_, W['qtm'][:, hp, cs], Sbf[:, d, :], True, False), reads=[pf2 + 'qtm', 'Sbf%d' % d], writes=[('ps', 7)])
                            S.op('pe', MM(o_, W['attnT'][:, hp * 128 + 64 * p:hp * 128 + 64 * p + 64], W['vnew'][:, hp * 64:(hp + 1) * 64], False, True),
                                 reads=[pf2 + 'attnT', pfx + 'vnew'], writes=[('ps', 7)])
                        if not second:
                            S.op('dve', CP(OC[P, ti, :], k.ps[7][P, 256:384]), reads=[('ps', 7)], writes=[('OC', ti, p)])
                        else:
                            S.op('dve', TT(OC[P, ti, :], OC[P, ti, :], k.ps[7][P, 256:384], ALU.add), reads=[('ps', 7)], writes=[('OC', ti, p)])
                        for hp in range(2):
                            S.op('pe', MM(k.ps[7][64 * hp:64 * hp + 64, 384:448], W['kdm'][:, p, hp * 64:(hp + 1) * 64], W['vnew'][:, hp * 64:(hp + 1) * 64], True, True),
                                 reads=[pf2 + 'kdm', pfx + 'vnew'], writes=[('ps', 7)])
                        tlast = 64 * p + 63 if d == 0 else 64 * p
                        S.op('dve', STT(Sst[:, d, :], Sst[:, d, :], W['Eexp'][:, tlast:tlast + 1], k.ps[7][:, 384:448], ALU.mult, ALU.add),
                             reads=[pf2 + 'Eexp', ('ps', 7)], writes=['Sst%d' % d])
                        S.op('act', ACT(Sbf[:, d, :], Sst[:, d, :], AF.Copy), reads=['Sst%d' % d], writes=['Sbf%d' % d])

                    def finalize(ti):
                        oc = OC[:, ti, :]
                        ock = [('OC', ti, 0), ('OC', ti, 1)]
                        for h in range(2):
                            hs = slice(h * 64, (h + 1) * 64)
                            S.op('act', ACT(vt[:, hs], oc[:, hs], AF.Square, accum_out=ssq[:, h:h + 1]), reads=ock, writes=['fvt', ('fssq', h)])
                        rsqrt_small(k, rs2[:], ssq[:], 2, 1.0 / 64, tm2[:], [('fssq', 0), ('fssq', 1)], ['frs2'], 'ftm2')
                        for h in range(2):
                            hs = slice(h * 64, (h + 1) * 64)
                            S.op('dve', STT(yc[:, hs], oc[:, hs], rs2[:, h:h + 1], zg[:, ti, hs], ALU.mult, ALU.mult),
                                 reads=['frs2', ('zg', ti)] + ock, writes=['yc'])
                        xT_and_wout(k, yc, 1, woC, 'woC', ti, 'yc', ycT, 'ycT', 7, (7, 7))

                    touched = [False] * NT
                    stg = []
                    for step in range(NT):
                        for d in range(2):
                            ti = step if d == 0 else NT - 1 - step
                            stg.append((d, ti, step % 2, touched[ti]))
                            touched[ti] = True

                    def recA(i):
                        d, ti, par, sec = stg[i]
                        return S.record(lambda: prepA(d, ti, par))

                    def recB(i):
                        d, ti, par, sec = stg[i]
                        return S.record(lambda: prepB(d, ti, par))

                    def recC(i):
                        d, ti, par, sec = stg[i]

                        def body():
                            for p in ((0, 1) if d == 0 else (1, 0)):
                                chain(d, ti, p, sec, par)
                            if sec:
                                finalize(ti)
                        return S.record(body)
                    n = len(stg)
                    S.play(recA(0))
                    S.play(recB(0), recA(1) if n > 1 else [])
                    for i in range(n):
                        S.play(recC(i), recB(i + 1) if i + 1 < n else [], recA(i + 2) if i + 2 < n else [])
                    S.barrier()
        S.barrier()


def make_in_maps(inputs, T=SEQ, depth=DEPTH, ncores=8):
    x = np.asarray(inputs['x'], np.float32)
    norms = np.zeros((2 * depth + 1, D), np.float32)
    for l in range(depth):
        norms[2 * l] = np.asarray(inputs['norm_mix'][l])
        norms[2 * l + 1] = np.asarray(inputs['norm_ffn'][l])
    norms[2 * depth] = np.asarray(inputs['norm_final'])
    pp = np.stack([pack_params(inputs, l) for l in range(depth)])
    ws = np.asarray(inputs['gm_ws'], np.float32)[:depth]
    wsT = np.ascontiguousarray(ws.transpose(0, 3, 1, 2)).reshape(depth, 128, 512)
    shared = dict(
        norms=norms, pp=pp, wsT=wsT, cm=const_masks(),
        w_in=np.ascontiguousarray(np.asarray(inputs['w_in'], np.float32)[:depth]),
        w_out=np.ascontiguousarray(np.asarray(inputs['w_out'], np.float32)[:depth]),
        w_up=np.ascontiguousarray(np.asarray(inputs['w_up'], np.float32)[:depth]),
        w_down=np.ascontiguousarray(np.asarray(inputs['w_down'], np.float32)[:depth]),
    )
    maps = []
    for c in range(ncores):
        m = dict(shared)
        m['x'] = np.ascontiguousarray(x[c, :T])
        maps.append(m)
    return maps


_NC_CACHE = {}
DBG = {}
_LAST = {}


def kernel(**inputs):
    if 'nc' not in _NC_CACHE:
        _NC_CACHE['nc'] = build()
    nc = _NC_CACHE['nc']
    maps = make_in_maps(inputs)
    res = run_bass_kernel_spmd(nc, maps, core_ids=list(range(8)))
    return np.stack([np.asarray(r['out'], np.float32) for r in res.results], axis=0)
```

```python
import numpy as np
from contextlib import ExitStack
import concourse.bass as bass
import concourse.mybir as mybir
from concourse.bass_utils import run_bass_kernel_spmd

F32 = mybir.dt.float32
BF16 = mybir.dt.bfloat16
AF = mybir.ActivationFunctionType
ALU = mybir.AluOpType
AX = mybir.AxisListType

D = 1024
DFF = 2816
NIN = 3104
EPS = 1e-6
NFC = DFF // 128
SEQ = 2048
DEPTH = 2


class Sched:
    ENG = ('pe', 'act', 'dve', 'pool', 'sp')

    def __init__(self, nc, es):
        self.nc = nc
        self.es = es
        self.sem = {e: es.enter_context(nc.semaphore('s_' + e)) for e in ('pe', 'act', 'dve', 'pool')}
        self.cnt = dict.fromkeys(('pe', 'act', 'dve', 'pool'), 0)
        self.q = {e: [] for e in self.ENG}
        self.lastw = {}
        self.readers = {}
        self.know = {}
        self._tm = dict(eng={}, w={}, r={})
        self.tokclock = {}
        self.tokseq = {}
        self.seq = 0
        self.chan = {}

    def _deps(self, reads, writes):
        toks = []
        for r in reads:
            t = self.lastw.get(r)
            if t:
                toks.append(t)
        for w in writes:
            t = self.lastw.get(w)
            if t:
                toks.append(t)
            toks.extend(self.readers.get(w, ()))
        return toks

    def _waits(self, eng, toks):
        K = self.know.setdefault(eng, {})
        need = {}
        for (k, v) in sorted(set(toks), key=lambda t: -self.tokseq.get(t, 0)):
            if eng == 'pe' and k == 'pe':
                continue
            if K.get(k, 0) >= v:
                continue
            if need.get(k, 0) < v:
                need[k] = v
            K[k] = v
            for kk, vv in self.tokclock.get((k, v), {}).items():
                if K.get(kk, 0) < vv:
                    K[kk] = vv
        return list(need.items())

    def _commit(self, tok, eng, reads, writes):
        self.seq += 1
        self.tokseq[tok] = self.seq
        clk = dict(self.know.get(eng, {}))
        self.tokclock[tok] = clk
        for r in reads:
            if r not in writes:
                self.readers.setdefault(r, []).append(tok)
        for w in writes:
            self.lastw[w] = tok
            self.readers[w] = []

    def op(self, eng, fn, reads=(), writes=()):
        psr = tuple(r for r in reads if isinstance(r, tuple) and r[0] == 'ps')
        reads = tuple(r for r in reads if not (isinstance(r, tuple) and r[0] == 'ps'))
        writes = tuple(writes) + tuple(r for r in psr if r not in writes)
        waits = self._waits(eng, self._deps(reads, writes))
        self.cnt[eng] += 1
        tok = (eng, self.cnt[eng])
        self.q[eng].append((waits, fn, ('inc', eng)))
        self._commit(tok, eng, reads, writes)

    def dma(self, queue, fn, reads, writes, chan):
        reads = tuple(reads)
        writes = tuple(writes)
        if chan not in self.chan:
            self.chan[chan] = [self.es.enter_context(self.nc.semaphore('c_' + chan)), 0]
        waits = self._waits(queue, self._deps(reads, writes))
        c = self.chan[chan]
        c[1] += 16
        tok = (('dma', chan), c[1])
        self.q[queue].append((waits, fn, ('dma', chan)))
        self._commit(tok, queue, reads, writes)

    def record(self, fn):
        lst = []
        orig = self.op
        self._cur = lst
        self.op = lambda *a, **kw: self._cur.append((a, kw))
        try:
            fn()
        finally:
            self.op = orig
            self._cur = None
        return lst

    def atomic(self):
        sched = self

        class _A:
            def __enter__(self_):
                self_.parent = getattr(sched, '_cur', None)
                if self_.parent is not None:
                    sched._cur = []
                return self_

            def __exit__(self_, *exc):
                if self_.parent is not None:
                    sub = sched._cur
                    sched._cur = self_.parent
                    sched._cur.append(('atomic', sub))
                return False
        return _A()

    COST = dict(pe=0.14, act=0.38, dve=0.42, pool=0.6, sp=2.0)

    def play(self, *lists):
        flat = []
        for l in lists:
            out = []

            def walk(items):
                for it in items:
                    if it[0] == 'atomic':
                        out.append(('grp', [x for x in self._flatten(it[1])]))
                    else:
                        out.append(('grp', [it]))
            walk(l)
            if out:
                flat.append(out)
        if not flat:
            return
        tm = self._tm
        idx = [0] * len(flat)
        remaining = sum(len(l) for l in flat)
        while remaining:
            best, best_t = None, None
            for li, l in enumerate(flat):
                if idx[li] >= len(l):
                    continue
                a, kw = l[idx[li]][1][0]
                t = self._est_start(a[0], kw.get('reads', ()), kw.get('writes', ()))
                key = (t, -(len(l) - idx[li]))
                if best is None or key < best_t:
                    best, best_t = li, key
            for a, kw in flat[best][idx[best]][1]:
                self._est_commit(a[0], kw.get('reads', ()), kw.get('writes', ()))
                self.op(*a, **kw)
            idx[best] += 1
            remaining -= 1

    def _flatten(self, items):
        for it in items:
            if it[0] == 'atomic':
                for x in self._flatten(it[1]):
                    yield x
            else:
                yield it

    def _est_start(self, eng, reads, writes):
        tm = self._tm
        t = tm['eng'].get(eng, 0.0)
        for r in reads:
            t = max(t, tm['w'].get(r, 0.0))
        for w in writes:
            t = max(t, tm['w'].get(w, 0.0), tm['r'].get(w, 0.0))
        return t

    def _est_commit(self, eng, reads, writes, cost=None):
        tm = self._tm
        if cost is None:
            cost = self.COST.get(eng, 0.4)
        t0 = self._est_start(eng, reads, writes)
        t1 = t0 + cost
        tm['eng'][eng] = t1
        lat = t1 + 0.15
        for r in reads:
            tm['r'][r] = max(tm['r'].get(r, 0.0), lat)
        for w in writes:
            tm['w'][w] = lat
            tm['r'][w] = 0.0

    def wait_all(self, eng):
        toks = [(e, self.cnt[e]) for e in self.cnt if self.cnt[e] > 0]
        toks += [(('dma', ch), c[1]) for ch, c in self.chan.items() if c[1] > 0]
        waits = self._waits(eng, toks)
        self.q[eng].append((waits, None, None))

    def barrier(self):
        for e in self.ENG:
            self.wait_all(e)

    def _semof(self, k):
        if isinstance(k, tuple):
            return self.chan[k[1]][0]
        return self.sem[k]

    def emit(self, block):
        handles = dict(pe=block.tensor, act=block.scalar, dve=block.vector, pool=block.gpsimd, sp=block.sync)

        def mk(eng):
            def body(e):
                for waits, fn, inc in self.q[eng]:
                    for k, v in waits:
                        e.wait_ge(self._semof(k), v)
                    if fn is None:
                        continue
                    ins = fn(e)
                    if inc[0] == 'inc':
                        ins.then_inc(self.sem[inc[1]], 1)
                    else:
                        ins.then_inc(self.chan[inc[1]][0], 16)
            return body
        for eng in self.ENG:
            handles[eng](mk(eng))


PP_FIELDS = [
    ('cw_ffn', 3 * 44), ('cb_ffn', 44),
    ('gmn', 256), ('bsf', 256), ('mlb', 16), ('mln', 512),
    ('alog', 8), ('dtb', 8), ('gdn', 256), ('cw_gd', 5 * 6),
]
PP_OFF = {}
_o = 0
for _n, _s in PP_FIELDS:
    PP_OFF[_n] = (_o, _s)
    _o += _s
NPP = _o


def pack_params(inp, l):
    pp = np.zeros((128, NPP), np.float32)

    def put(name, arr):
        o, s = PP_OFF[name]
        pp[:, o:o + s] = np.asarray(arr, np.float32).reshape(128, s)
    fc = np.asarray(inp['ffn_conv'][l])
    put('cw_ffn', fc.reshape(3, 44, 128).transpose(2, 0, 1))
    put('cb_ffn', np.asarray(inp['ffn_conv_b'][l]).reshape(44, 128).T)
    put('gmn', np.broadcast_to(np.asarray(inp['gm_norm'][l]).reshape(1, 256), (128, 256)))
    bs = np.asarray(inp['gm_bs'][l])
    put('bsf', np.broadcast_to(bs.T[:, :, None], (128, 4, 64)))
    put('mlb', np.broadcast_to(np.asarray(inp['ml_gate_bias'][l]).reshape(1, 16), (128, 16)))
    put('mln', np.broadcast_to(np.asarray(inp['ml_head_norm'][l]).reshape(1, 512), (128, 512)))
    put('alog', np.broadcast_to(np.asarray(inp['gd_A_log'][l]).reshape(1, 8), (128, 8)))
    put('dtb', np.broadcast_to(np.asarray(inp['gd_dt_bias'][l]).reshape(1, 8), (128, 8)))
    put('gdn', np.broadcast_to(np.asarray(inp['gd_head_norm'][l]).reshape(1, 256), (128, 256)))
    gc = np.asarray(inp['gd_conv'][l])
    put('cw_gd', gc.reshape(5, 6, 128).transpose(2, 0, 1))
    return pp


CM_FIELDS = [('ident', 128), ('eps', 1), ('one', 1), ('ones', 64), ('BD', 128),
             ('A20', 128), ('A21', 128), ('U2n0', 128), ('U2n1', 128),
             ('NEGF0', 128), ('NEGF1', 128), ('NEGSF0', 128), ('NEGSF1', 128)]
CM_OFF = {}
_o = 0
for _n, _s in CM_FIELDS:
    CM_OFF[_n] = (_o, _s)
    _o += _s
NCM = _o


def const_masks():
    cm = np.zeros((128, NCM), np.float32)

    def put(name, arr):
        o, n = CM_OFF[name]
        cm[:, o:o + n] = arr
    put('ident', np.eye(128, dtype=np.float32))
    put('eps', EPS)
    put('one', 1.0)
    put('ones', 1.0)
    r = np.arange(128)
    rl = r % 64
    same = (r[:, None] // 64) == (r[None, :] // 64)
    put('BD', same.astype(np.float32))
    for d in range(2):
        def peq(a, b):
            return (a <= b) if d == 0 else (a >= b)

        def prec(a, b):
            return (a < b) if d == 0 else (a > b)
        put('A2%d' % d, (same & prec(rl[None, :], rl[:, None])).astype(np.float32))
        put('U2n%d' % d, -(same & peq(rl[:, None], rl[None, :])).astype(np.float32))
        put('NEGF%d' % d, np.where(same & peq(rl[:, None], rl[None, :]), 0.0, -30000.0))
        put('NEGSF%d' % d, np.where(same & prec(rl[:, None], rl[None, :]), 0.0, -30000.0))
    return cm


def _nfree(ap):
    n = 1
    for d_ in ap.shape[1:]:
        n *= int(d_)
    return n


def _c(f, cost):
    f.cost = cost
    return f


def MM(out, lhsT, rhs, start=True, stop=True):
    n = _nfree(out)
    mult = 4.0 if lhsT.dtype == F32 else 1.0
    return _c(lambda e: e.matmul(out, lhsT=lhsT, rhs=rhs, start=start, stop=stop), 0.10 + mult * max(n, 64) / 1900.0)


def TR(out, in_, ident):
    return _c(lambda e: e.transpose(out=out, in_=in_, identity=ident), 0.16)


def ACT(out, in_, func, **kw):
    return _c(lambda e: e.activation(out=out, in_=in_, func=func, **kw), 0.22 + _nfree(out) / 1300.0)


def TT(out, in0, in1, op):
    return _c(lambda e: e.tensor_tensor(out=out, in0=in0, in1=in1, op=op), 0.16 + _nfree(out) / 960.0)


def STT(out, in0, scalar, in1, op0, op1):
    return _c(lambda e: e.scalar_tensor_tensor(out=out, in0=in0, scalar=scalar, in1=in1, op0=op0, op1=op1), 0.16 + _nfree(out) / 960.0)


def TS(out, in0, s1, s2=None, op0=ALU.mult, op1=None):
    c_ = 0.14 + _nfree(out) / 960.0
    if op1 is None:
        return _c(lambda e: e.tensor_scalar(out=out, in0=in0, scalar1=s1, scalar2=None, op0=op0), c_)
    return _c(lambda e: e.tensor_scalar(out=out, in0=in0, scalar1=s1, scalar2=s2, op0=op0, op1=op1), c_)


def RED(out, in_, op=ALU.add, axis=AX.X):
    return _c(lambda e: e.tensor_reduce(out=out, in_=in_, axis=axis, op=op), 0.16 + _nfree(in_) / 960.0)


def CP(out, in_):
    return _c(lambda e: e.tensor_copy(out=out, in_=in_), 0.16 + _nfree(out) / 960.0)


def MSET(ap, v):
    return _c(lambda e: e.memset(ap, v), 0.2 + _nfree(ap) / 960.0)


def DMA(out, in_):
    return _c(lambda e: e.dma_start(out=out, in_=in_), 2.0)


class K:
    pass


def build(T=SEQ, depth=DEPTH, do_mixer=True, do_ffn=True):
    NT = T // 128
    nc = bass.Bass("TRN2", target_bir_lowering=False)
    k = K()
    k.nc, k.T, k.NT, k.depth = nc, T, NT, depth
    dr = {}
    dr['x'] = nc.dram_tensor("x", [T, D], F32, kind="ExternalInput").ap()
    dr['out'] = nc.dram_tensor("out", [T, D], F32, kind="ExternalOutput").ap()
    dr['norms'] = nc.dram_tensor("norms", [2 * depth + 1, D], F32, kind="ExternalInput").ap()
    dr['w_in'] = nc.dram_tensor("w_in", [depth, D, NIN], F32, kind="ExternalInput").ap()
    dr['w_out'] = nc.dram_tensor("w_out", [depth, D, D], F32, kind="ExternalInput").ap()
    dr['w_up'] = nc.dram_tensor("w_up", [depth, D, 2 * DFF], F32, kind="ExternalInput").ap()
    dr['w_down'] = nc.dram_tensor("w_down", [depth, DFF, D], F32, kind="ExternalInput").ap()
    dr['pp'] = nc.dram_tensor("pp", [depth, 128, NPP], F32, kind="ExternalInput").ap()
    dr['wsT'] = nc.dram_tensor("wsT", [depth, 128, 512], F32, kind="ExternalInput").ap()
    dr['cm'] = nc.dram_tensor("cm", [128, NCM], F32, kind="ExternalInput").ap()
    k.dr = dr

    with ExitStack() as es:
        S = Sched(nc, es)
        k.S = S
        _LAST['S'] = S
        k.es = es

        def sb(name, shape, dt):
            return es.enter_context(nc.sbuf_tensor("sb_" + name, shape, dt))
        k.sb = sb
        k.X = sb("X", [128, NT, D], F32)
        k.hT = sb("hT", [128, 8, T + 2], BF16)
        k.gb = sb("gb", [128, D], F32)
        k.pp = sb("pp", [128, NPP], F32)
        k.cm = sb("cm", [128, NCM], F32)
        k.ident = sb("ident", [128, 128], BF16)
        k.ss = sb("ss", [128, NT], F32)
        k.rstd = sb("rstd", [128, NT], F32)
        k.xn = [sb("xn%d" % i, [128, D], BF16) for i in range(2)]
        k.ps = [es.enter_context(nc.psum_tensor("ps%d" % i, [128, 512], F32)) for i in range(8)]

        S.dma('sp', DMA(k.cm[:], dr['cm']), reads=[], writes=['cm'], chan='cm')
        S.op('dve', CP(k.ident[:], k.cm[:, CM_OFF['ident'][0]:CM_OFF['ident'][0] + 128]), reads=['cm'], writes=['ident'])
        k.negfb = [sb("negfb%d" % d_, [128, 256], BF16) for d_ in range(2)]
        for d_ in range(2):
            for h_ in range(2):
                S.op('dve', CP(k.negfb[d_][:, h_ * 128:(h_ + 1) * 128], cmc(k, 'NEGF%d' % d_)), reads=['cm'], writes=['negfb'])
        S.op('pool', MSET(k.hT[:, :, 0:1], 0.0), writes=['hTpad'])
        S.op('pool', MSET(k.hT[:, :, T + 1:T + 2], 0.0), writes=['hTpad'])

        xv = dr['x'].rearrange("(n p) d -> n p d", p=128)
        for i in range(NT):
            S.dma('sp' if i % 2 == 0 else 'act', DMA(k.X[:, i, :], xv[i]), reads=[], writes=[('X', i, 0), ('X', i, 1)], chan='x%d' % i)

        for l in range(depth):
            S.dma('sp', DMA(k.pp[:], dr['pp'][l]), reads=[], writes=['pp'], chan='pp')
            if do_mixer:
                phase_mixer(k, l, do_mixer if isinstance(do_mixer, str) else 'ABC')
            if do_ffn:
                phase_ffn(k, l)
        final_norm(k)
        S.wait_all('sp')
        with nc.Block() as block:
            S.emit(block)
    return nc


def rms_stats(k, norm_idx):
    S, NT = k.S, k.NT
    S.dma('sp', DMA(k.gb[:], k.dr['norms'][norm_idx:norm_idx + 1, :].partition_broadcast(128)),
          reads=[], writes=['gb'], chan='gb')
    for i in range(NT):
        if i % 2 == 0:
            S.op('act', ACT(k.xn[0][:], k.X[:, i, :], AF.Square, accum_out=k.ss[:, i:i + 1]),
                 reads=[('X', i, 0), ('X', i, 1)], writes=[('xn', 0), ('ss', i)])
        else:
            S.op('dve', (lambda o, a, acc: lambda e: e.scalar_tensor_tensor(out=o, in0=a, scalar=1.0, in1=a, op0=ALU.mult, op1=ALU.mult,
                                                                            accum_out=acc))(k.xn[1][:], k.X[:, i, :], k.ss[:, i:i + 1]),
                 reads=[('X', i, 0), ('X', i, 1)], writes=[('xn', 1), ('ss', i)])
    allss = [('ss', i) for i in range(NT)]
    S.op('act', ACT(k.rstd[:], k.ss[:], AF.Ln, scale=1.0 / D, bias=k.cm[:, CM_OFF['eps'][0]:CM_OFF['eps'][0] + 1]),
         reads=allss + ['cm'], writes=['rstd'])
    S.op('act', ACT(k.rstd[:], k.rstd[:], AF.Exp, scale=-0.5), reads=['rstd'], writes=['rstd'])


def norm_to_T(k, norm_idx, banks=(6, 7)):
    S, NT = k.S, k.NT
    rms_stats(k, norm_idx)
    for i in range(NT):
        xn = k.xn[i % 2]
        bank = banks[i % 2]
        pst = k.ps[bank][:].bitcast(BF16).rearrange("p (a b) -> p a b", a=8)
        S.op('dve', STT(xn[:], k.X[:, i, :], k.rstd[:, i:i + 1], k.gb[:], ALU.mult, ALU.mult),
             reads=[('X', i, 0), ('X', i, 1), 'rstd', 'gb'], writes=[('xn', i % 2)])
        for kc in range(8):
            S.op('pe', TR(pst[:, kc, :], xn[:, kc * 128:(kc + 1) * 128], k.ident[:]),
                 reads=[('xn', i % 2), 'ident'], writes=[('ps', bank)])
        S.op('act', ACT(k.hT[:, :, 1 + i * 128:1 + (i + 1) * 128], pst, AF.Copy),
             reads=[('ps', bank)], writes=[('hT', i)])


def final_norm(k):
    S, NT = k.S, k.NT
    rms_stats(k, 2 * k.depth)
    ov = k.dr['out'].rearrange("(n p) d -> n p d", p=128)
    for i in range(NT):
        S.op('dve', STT(k.X[:, i, :], k.X[:, i, :], k.rstd[:, i:i + 1], k.gb[:], ALU.mult, ALU.mult),
             reads=['rstd', 'gb'], writes=[('X', i, 0), ('X', i, 1)])
        S.dma('sp' if i % 2 == 0 else 'act', DMA(ov[i], k.X[:, i, :]), reads=[('X', i, 0), ('X', i, 1)], writes=[('out', i)], chan='o%d' % i)


def phase_ffn(k, l):
    S, T, NT, nc = k.S, k.T, k.NT, k.nc
    dr = k.dr
    HALF = min(1024, T)
    NBLK = T // HALF
    NB = max(1, HALF // 512)
    BW = min(512, HALF)
    TPB = HALF // 128
    GRP = 4
    o_cw = PP_OFF['cw_ffn'][0]
    o_cb = PP_OFF['cb_ffn'][0]
    wupv = dr['w_up'][l].rearrange("(kc p) n -> p kc n", p=128)
    wdnv = dr['w_down'][l].rearrange("(fc p) n -> p fc n", p=128)

    with ExitStack() as es:
        def sb(name, shape, dt):
            return es.enter_context(nc.sbuf_tensor("sb_" + name + "_f%d" % l, shape, dt))
        upbuf = [[sb("upbuf%d%d" % (a, b), [128, HALF + 2], F32) for b in range(2)] for a in range(2)]
        acc = [[sb("acc%d%d" % (a, b), [128, HALF], F32) for b in range(2)] for a in range(2)]
        gT = [sb("gT%d" % a, [128, GRP, HALF], BF16) for a in range(2)]
        wup = [[sb("wup%d_%d" % (p_, a), [128, 8, 256], BF16) for a in range(2)] for p_ in range(2)]
        wdn = [sb("wdn%d" % a, [128, GRP, D], BF16) for a in range(2)]

        norm_to_T(k, 2 * l + 1)
        wctr = gctr = uctr = dctr = pctr = 0
        for blk in range(NBLK):
            t0 = blk * HALF
            groups = [list(range(g, min(g + GRP, NFC))) for g in range(0, NFC, GRP)]
            for grp in groups:
                gi = gctr % 2
                gctr += 1
                f0 = grp[0]
                ng = len(grp)
                S.dma('pool', DMA(wdn[gi][:, 0:ng, :], wdnv[:, f0:f0 + ng, :]), reads=[], writes=[('wdn', gi)], chan='wdn%d' % gi)
                for jj, j in enumerate(grp):
                    pb = pctr % 2
                    pctr += 1
                    for part in range(2):
                        fc = j + part * NFC
                        wr = (wctr // 2) % 2
                        wsub = jj % 2
                        if wsub == 0:
                            npair = min(2, ng - jj)
                            S.dma('pool', DMA(wup[part][wr][:, :, 0:npair * 128], wupv[:, :, fc * 128:(fc + npair) * 128]),
                                  reads=[], writes=[('wup', part, wr)], chan='wup%d_%d' % (part, wr))
                        wt = wup[part][wr][:, :, wsub * 128:(wsub + 1) * 128]
                        wkey = ('wup', part, wr)
                        ui = uctr % 2
                        uctr += 1
                        for nb in range(NB):
                            bank = ui * 2 + nb
                            hreads = [('hT', (t0 + nb * BW) // 128 + q) for q in range(BW // 128)]
                            for kc in range(8):
                                S.op('pe', MM(k.ps[bank][:, 0:BW], wt[:, kc, :],
                                              k.hT[:, kc, 1 + t0 + nb * BW:1 + t0 + (nb + 1) * BW], kc == 0, kc == 7),
                                     reads=[wkey] + hreads, writes=[('ps', bank)])
                        hb = 4 + ui
                        for kc in range(8):
                            S.op('pe', MM(k.ps[hb][:, 0:2], wt[:, kc, :], k.hT[:, kc, t0:t0 + HALF + 2:HALF + 1], kc == 0, kc == 7),
                                 reads=[wkey, ('hT', max(0, t0 // 128 - 1)), ('hT', min(NT - 1, (t0 + HALF) // 128)), 'hTpad'],
                                 writes=[('ps', hb)])
                        ubuf = upbuf[part][pb]
                        ac = acc[part][pb]
                        for nb in range(NB):
                            bank = ui * 2 + nb
                            S.op('act', ACT(ubuf[:, 1 + nb * BW:1 + (nb + 1) * BW], k.ps[bank][:, 0:BW], AF.Copy),
                                 reads=[('ps', bank)], writes=[('upbuf', part, pb, nb)])
                            S.op('act', ACT(ac[:, nb * BW:(nb + 1) * BW], k.ps[bank][:, 0:BW], AF.Identity,
                                            scale=k.pp[:, o_cw + 44 + fc:o_cw + 44 + fc + 1],
                                            bias=k.pp[:, o_cb + fc:o_cb + fc + 1]),
                                 reads=[('ps', bank), 'pp'], writes=[('acc', part, pb, nb)])
                        S.op('act', ACT(ubuf[:, 0:HALF + 2:HALF + 1], k.ps[hb][:, 0:2], AF.Copy),
                             reads=[('ps', hb)], writes=[('upbuf', part, pb, 'h')])
                        allub = [('upbuf', part, pb, nb) for nb in range(NB)] + [('upbuf', part, pb, 'h')]
                        allac = [('acc', part, pb, nb) for nb in range(NB)]
                        for tap in (0, 2):
                            S.op('dve', STT(ac[:], ubuf[:, tap:tap + HALF],
                                            k.pp[:, o_cw + tap * 44 + fc:o_cw + tap * 44 + fc + 1], ac[:], ALU.mult, ALU.add),
                                 reads=allub + ['pp'], writes=allac)
                    wctr += 1
                    ag = acc[0][pb]
                    av = acc[1][pb]
                    S.op('act', ACT(ag[:], ag[:], AF.Silu), reads=[], writes=[('acc', 0, pb, nb) for nb in range(NB)])
                    S.op('dve', TT(gT[gi][:, jj, :], ag[:], av[:], ALU.mult),
                         reads=[('acc', 0, pb, nb) for nb in range(NB)] + [('acc', 1, pb, nb) for nb in range(NB)],
                         writes=[('gT', gi, jj)])
                for tl in range(TPB):
                    ti = t0 // 128 + tl
                    for half in range(2):
                        bank = 6 + dctr % 2
                        dctr += 1
                        for jj in range(ng):
                            S.op('pe', MM(k.ps[bank][:, :], gT[gi][:, jj, tl * 128:(tl + 1) * 128],
                                          wdn[gi][:, jj, half * 512:(half + 1) * 512], jj == 0, jj == ng - 1),
                                 reads=[('gT', gi, jj), ('wdn', gi)], writes=[('ps', bank)])
                        xs = k.X[:, ti, half * 512:(half + 1) * 512]
                        S.op('dve', TT(xs, xs, k.ps[bank][:, :], ALU.add), reads=[('ps', bank)], writes=[('X', ti, half)])
        S.barrier()


def cmc(k, name, a=0, b=None):
    o, n = CM_OFF[name]
    return k.cm[:, o + a:o + (n if b is None else b)]


def ppc(k, name, a=0, b=None):
    o, n = PP_OFF[name]
    return k.pp[:, o + a:o + (n if b is None else b)]


def rsqrt_small(k, out, in_, n, scale, tmp, reads, writes, tmpkey):
    S = k.S
    S.op('act', ACT(tmp, in_, AF.Ln, scale=scale, bias=cmc(k, 'eps')), reads=list(reads) + ['cm'], writes=[tmpkey])
    S.op('act', ACT(out, tmp, AF.Exp, scale=-0.5), reads=[tmpkey], writes=writes)


def phase_mixer(k, l, groups):
    S, T, NT, nc = k.S, k.T, k.NT, k.nc
    norm_to_T(k, 2 * l)
    if 'A' in groups:
        group_A(k, l)
    if 'B' in groups:
        group_B(k, l)
    if 'C' in groups:
        group_C(k, l)


def load_w(k, dst, src, key, chan, queue='pool'):
    k.S.dma(queue, DMA(dst, src), reads=[], writes=[key], chan=chan)


def xT_and_wout(k, y_bf, nkc, wo, wokey, ti, ykey, yT, yTkey, tbank, obanks):
    S = k.S
    pst = k.ps[tbank][:].bitcast(BF16).rearrange("p (a b) -> p a b", a=8)
    for kc in range(nkc):
        S.op('pe', TR(pst[:, kc, :], y_bf[:, kc * 128:(kc + 1) * 128], k.ident[:]), reads=[ykey, 'ident'], writes=[('ps', tbank)])
    S.op('act', ACT(yT[:, 0:nkc, :], pst[:, 0:nkc, :], AF.Copy), reads=[('ps', tbank)], writes=[yTkey])
    for half in range(2):
        bank = obanks[half]
        with S.atomic():
            for kc in range(nkc):
                S.op('pe', MM(k.ps[bank][:, :], yT[:, kc, :], wo[:, kc, half * 512:(half + 1) * 512], kc == 0, kc == nkc - 1),
                     reads=[yTkey, wokey], writes=[('ps', bank)])
        xs = k.X[:, ti, half * 512:(half + 1) * 512]
        S.op('dve', TT(xs, xs, k.ps[bank][:, :], ALU.add), reads=[('ps', bank)], writes=[('X', ti, half)])


def group_A(k, l):
    S, T, NT, nc = k.S, k.T, k.NT, k.nc
    dr = k.dr
    winv = dr['w_in'][l].rearrange("(kc p) n -> p kc n", p=128)
    woutv = dr['w_out'][l].rearrange("(kc p) n -> p kc n", p=128)
    with ExitStack() as es:
        def sb(name, shape, dt):
            return es.enter_context(nc.sbuf_tensor("sb_" + name + "_a%d" % l, shape, dt))
        wA = sb("wA", [128, 8, 512], BF16)
        woA = sb("woA", [128, 2, D], BF16)
        wsT = sb("wsT", [128, 512], BF16)
        B = []
        for b in range(2):
            B.append(dict(uv=sb("uv%d" % b, [128, 512], F32), vt=sb("vt%d" % b, [128, 256], F32), ssq=sb("ssq%d" % b, [128, 4], F32),
                          rs4=sb("rs4%d" % b, [128, 4], F32), tm4=sb("tm4%d" % b, [128, 4], F32), vnb=sb("vnb%d" % b, [128, 256], BF16),
                          ya=sb("ya%d" % b, [128, 256], BF16), yaT=sb("yaT%d" % b, [128, 2, 128], BF16)))
        load_w(k, wA[:], winv[:, :, 0:512], 'wA', 'wA')
        load_w(k, woA[:], woutv[:, 0:2, :], 'woA', 'woA')
        load_w(k, wsT[:], dr['wsT'][l], 'wsT', 'wsT')

        def tile_ops(i):
            b = i % 2
            W = B[b]
            u, vt, ssq, rs4, tm4, vnb, ya, yaT = W['uv'], W['vt'], W['ssq'], W['rs4'], W['tm4'], W['vnb'], W['ya'], W['yaT']
            kb = 'A%d' % b
            pa, pg, ptr, po = b, 2 + b, 4 + b, 6 + b
            for kc in range(8):
                S.op('pe', MM(k.ps[pa][:, :], k.hT[:, kc, 1 + i * 128:1 + (i + 1) * 128], wA[:, kc, :], kc == 0, kc == 7),
                     reads=[('hT', i), 'wA'], writes=[('ps', pa)])
            S.op('act', ACT(u[:], k.ps[pa][:, :], AF.Gelu_apprx_tanh), reads=[('ps', pa)], writes=[kb + 'uv'])
            S.op('dve', TT(vt[:], u[:, 256:512], u[:, 256:512], ALU.mult), reads=[kb + 'uv'], writes=[kb + 'vt'])
            S.op('dve', RED(ssq[:], vt[:].rearrange("p (g e) -> p g e", g=4)), reads=[kb + 'vt'], writes=[kb + 'ssq'])
            rsqrt_small(k, rs4[:], ssq[:], 4, 1.0 / 64, tm4[:], [kb + 'ssq'], [kb + 'rs4'], kb + 'tm4')
            S.op('dve', TT(vt[:].rearrange("p (g e) -> p g e", g=4), u[:, 256:512].rearrange("p (g e) -> p g e", g=4),
                           rs4[:].unsqueeze(2).broadcast_to([128, 4, 64]), ALU.mult), reads=[kb + 'uv', kb + 'rs4'], writes=[kb + 'vt'])
            S.op('dve', TT(vnb[:], vt[:], ppc(k, 'gmn'), ALU.mult), reads=[kb + 'vt', 'pp'], writes=[kb + 'vnb'])
            for g in range(4):
                S.op('pe', MM(k.ps[pg][:, g * 64:(g + 1) * 64], wsT[:, g * 128:(g + 1) * 128], vnb[:, g * 64:(g + 1) * 64], True, True),
                     reads=['wsT', kb + 'vnb'], writes=[('ps', pg)])
            S.op('dve', TT(vt[:], k.ps[pg][:, 0:256], ppc(k, 'bsf'), ALU.add), reads=[('ps', pg), 'pp'], writes=[kb + 'vt'])
            S.op('dve', TT(ya[:], vt[:], u[:, 0:256], ALU.mult), reads=[kb + 'vt', kb + 'uv'], writes=[kb + 'ya'])
            xT_and_wout(k, ya, 2, woA, 'woA', i, kb + 'ya', yaT, kb + 'yaT', ptr, (po, po))
        for i in range(0, NT, 2):
            S.play(S.record(lambda: tile_ops(i)), S.record(lambda: tile_ops(i + 1)) if i + 1 < NT else [])
        S.barrier()


def gate_tables(k, l, sb):
    S, NT, nc = k.S, k.NT, k.nc
    winv = k.dr['w_in'][l].rearrange("(kc p) n -> p kc n", p=128)
    wg = sb("wg", [128, 8, 16], BF16)
    G_ig = sb("G_ig", [128, NT, 8], F32)
    G_lfn = sb("G_lfn", [128, NT, 8], F32)
    load_w(k, wg[:], winv[:, :, 2048:2064], 'wg', 'wg')
    for i in range(NT):
        for kc in range(8):
            S.op('pe', MM(k.ps[0][:, i * 16:(i + 1) * 16], k.hT[:, kc, 1 + i * 128:1 + (i + 1) * 128], wg[:, kc, :], kc == 0, kc == 7),
                 reads=[('hT', i), 'wg'], writes=[('ps', 0)])
    pv = k.ps[0][:, 0:NT * 16].rearrange("p (n c) -> p n c", c=16)
    mlb = ppc(k, 'mlb')
    S.op('dve', TT(G_ig[:], pv[:, :, 0:8], mlb[:, 0:8].unsqueeze(1).broadcast_to([128, NT, 8]), ALU.add),
         reads=[('ps', 0), 'pp'], writes=['G_ig'])
    S.op('dve', TT(G_lfn[:], pv[:, :, 8:16], mlb[:, 8:16].unsqueeze(1).broadcast_to([128, NT, 8]), ALU.add),
         reads=[('ps', 0), 'pp'], writes=['G_lfn'])
    S.op('act', ACT(G_lfn[:], G_lfn[:], AF.Exp, scale=-1.0), reads=[], writes=['G_lfn'])
    S.op('act', ACT(G_lfn[:], G_lfn[:], AF.Ln, bias=cmc(k, 'one')), reads=['cm'], writes=['G_lfn'])
    return G_ig, G_lfn


def scan_prep_common(k, d, lf, gatekey, HH, W, pfx, strict=False, pfxd=None):
    S = k.S
    N = HH * 128
    H2 = HH // 2
    pfxd = pfxd or pfx
    S.op('pool', TT(W['GxE'][:].rearrange("p (h t) -> p h t", h=HH),
                   cmc(k, 'U2n%d' % d).unsqueeze(1).broadcast_to([128, HH, 128]),
                   lf.unsqueeze(2).broadcast_to([128, HH, 128]), ALU.mult),
         reads=['cm', gatekey], writes=[pfx + 'GxE'])
    S.op('pe', MM(k.ps[0][:, 0:N], cmc(k, 'A2%d' % d), W['GxE'][:], True, False), reads=['cm', pfx + 'GxE'], writes=[('ps', 0)])
    S.op('pe', MM(k.ps[0][:, 0:N], k.ident[:], k.negfb[d][:, 0:N], False, True), reads=['ident', 'negfb'], writes=[('ps', 0)])
    S.op('act', ACT(W['decT'][:], k.ps[0][:, 0:N], AF.Exp), reads=[('ps', 0)], writes=[pfx + 'decT'])
    for h in range(HH):
        hp, h2 = h % 2, h // 2
        S.op('pe', MM(k.ps[1][64 * hp:64 * hp + 64, h2 * 128:(h2 + 1) * 128], cmc(k, 'ones'),
                      W['GxE'][:, h * 128:(h + 1) * 128], True, True),
             reads=['cm', pfx + 'GxE'], writes=[('ps', 1)])
    S.op('pe', MM(k.ps[1][:, 256:256 + HH], cmc(k, 'A2%d' % d), lf, True, True), reads=['cm', gatekey], writes=[('ps', 1)])
    S.op('act', ACT(W['Eexp'][:], k.ps[1][:, 0:H2 * 128], AF.Exp), reads=[('ps', 1)], writes=[pfxd + 'Eexp'])
    S.op('act', ACT(W['dend'][:], k.ps[1][:, 256:256 + HH], AF.Exp, scale=-1.0), reads=[('ps', 1)], writes=[pfxd + 'dend'])


def group_B(k, l):
    S, T, NT, nc = k.S, k.T, k.NT, k.nc
    dr = k.dr
    winv = dr['w_in'][l].rearrange("(kc p) n -> p kc n", p=128)
    woutv = dr['w_out'][l].rearrange("(kc p) n -> p kc n", p=128)
    TB = min(512, T)
    with ExitStack() as es0:
        def sb0(name, shape, dt):
            return es0.enter_context(nc.sbuf_tensor("sb_" + name + "_b%d" % l, shape, dt))
        G_ig, G_lfn = gate_tables(k, l, sb0)
        alloc = {}
        if DBG.get('stop') == 1:
            S.barrier()
            return
        for hpass in range(2):
            hb0 = 2 * hpass
            with ExitStack() as es:
                def sb(name, shape, dt):
                    if name not in alloc:
                        alloc[name] = es0.enter_context(nc.sbuf_tensor("sb_" + name + "_b%d" % l, shape, dt))
                    return alloc[name]
                wqk = sb("wqk", [128, 8, 256], BF16)
                wvo = sb("wvo", [128, 8, 512], BF16)
                woB = sb("woB", [128, 2, D], BF16)
                qT = sb("qT", [128, T], BF16)
                qTm = sb("qTm", [128, 2, T], BF16)
                kT = sb("kT", [128, T], BF16)
                ktok = sb("ktok", [128, NT, 128], BF16)
                vp = sb("vp", [128, NT, 2, 130], BF16)
                og = sb("og", [128, NT, 256], BF16)
                HB = sb("HB", [128, NT, 256], F32)
                Cst = sb("Cst", [128, 2, 130], F32)
                Cbf = sb("Cbf", [128, 2, 130], BF16)
                Wd = []
                for d in range(2):
                    W = dict(
                        GxE=sb("GxE%d" % d, [128, 256], F32),
                        decT=sb("decT%d" % d, [128, 256], F32), dend=sb("dend%d" % d, [128, 2], F32),
                        Eexp=sb("Eexp%d" % d, [128, 128], F32), eig=sb("eig%d" % d, [128, 2], F32),
                        qtm=sb("qtm%d" % d, [128, 2, 128], BF16), STm=sb("STm%d" % d, [128, 256], BF16),
                        vw=sb("vw%d" % d, [128, 2, 130], BF16), kdm=sb("kdm%d" % d, [128, 2, 128], BF16),
                        den=sb("den%d" % d, [128, 2], F32), rden=sb("rden%d" % d, [128, 2], F32),
                        tmp=sb("tmpo%d" % d, [128, 256], F32))
                    Wd.append(W)
                    S.op('pool', MSET(W['qtm'][:], 0.0), writes=['B%dqtm' % d])
                    S.op('pool', MSET(W['kdm'][:], 0.0), writes=['B%dkdm' % d])
                vt = sb("fvt", [128, 256], F32)
                ssq = sb("fssq", [128, 2], F32)
                rs2 = sb("frs2", [128, 2], F32)
                tm2 = sb("ftm2", [128, 2], F32)
                yb = sb("yb", [128, 256], BF16)
                ybT = sb("ybT", [128, 2, 128], BF16)

                load_w(k, wqk[:, :, 0:128], winv[:, :, 512 + hb0 * 64:512 + hb0 * 64 + 128], 'wqk', 'wqk')
                load_w(k, wqk[:, :, 128:256], winv[:, :, 768 + hb0 * 64:768 + hb0 * 64 + 128], 'wqk2', 'wqk2')
                load_w(k, wvo[:, :, 0:256], winv[:, :, 1024 + hb0 * 128:1024 + hb0 * 128 + 256], 'wvo', 'wvo')
                load_w(k, wvo[:, :, 256:512], winv[:, :, 1536 + hb0 * 128:1536 + hb0 * 128 + 256], 'wvo2', 'wvo2')
                load_w(k, woB[:], woutv[:, 2 + hb0:2 + hb0 + 2, :], 'woB', 'woB')
                S.op('pool', MSET(vp[:], 1.0), writes=['vp1'] + [('vp', i) for i in range(NT)])
                S.op('pool', MSET(Cst[:], 0.0), writes=['Cst0', 'Cst1'])
                S.op('pool', MSET(Cbf[:], 0.0), writes=['Cbf0', 'Cbf1'])
                S.op('pool', MSET(qTm[:], 0.0), writes=[('qTm', tb) for tb in range(T // TB)])
                if DBG.get('stop') == 21:
                    S.barrier()
                    return
                for tb in range(T // TB):
                    hreads = [('hT', tb * (TB // 128) + q) for q in range(TB // 128)]
                    sl = slice(tb * TB, (tb + 1) * TB)
                    for c in range(2):
                        bank = 2 + c
                        for kc in range(8):
                            S.op('pe', MM(k.ps[bank][:, 0:TB], wqk[:, kc, c * 128:(c + 1) * 128],
                                          k.hT[:, kc, 1 + tb * TB:1 + (tb + 1) * TB], kc == 0, kc == 7),
                                 reads=['wqk', 'wqk2'] + hreads, writes=[('ps', bank)])
                    S.op('act', ACT(qT[:, sl], k.ps[2][:, 0:TB], AF.Copy, scale=0.125), reads=[('ps', 2)], writes=[('qT', tb)])
                    S.op('act', ACT(qTm[0:64, 0, sl], k.ps[2][0:64, 0:TB], AF.Copy, scale=0.125), reads=[('ps', 2)], writes=[('qTm', tb)])
                    S.op('act', ACT(qTm[64:128, 1, sl], k.ps[2][64:128, 0:TB], AF.Copy, scale=0.125), reads=[('ps', 2)], writes=[('qTm', tb)])
                    S.op('act', ACT(kT[:, sl], k.ps[3][:, 0:TB], AF.Copy), reads=[('ps', 3)], writes=[('kT', tb)])
                if DBG.get('stop') == 22:
                    S.barrier()
                    return
                for i in range(NT):
                    b0, b1 = 4 + (i % 2) * 2, 5 + (i % 2) * 2
                    for kc in range(8):
                        S.op('pe', MM(k.ps[b0][:, 0:128], k.hT[:, kc, 1 + i * 128:1 + (i + 1) * 128], wqk[:, kc, 128:256], kc == 0, kc == 7),
                             reads=[('hT', i), 'wqk2'], writes=[('ps', b0)])
                    for kc in range(8):
                        S.op('pe', MM(k.ps[b1][:, :], k.hT[:, kc, 1 + i * 128:1 + (i + 1) * 128], wvo[:, kc, :], kc == 0, kc == 7),
                             reads=[('hT', i), 'wvo', 'wvo2'], writes=[('ps', b1)])
                    S.op('act', ACT(ktok[:, i, :], k.ps[b0][:, 0:128], AF.Copy), reads=[('ps', b0)], writes=[('ktok', i)])
                    S.op('act', ACT(vp[:, i, :, 0:128], k.ps[b1][:, 0:256].rearrange("p (h v) -> p h v", h=2), AF.Copy),
                         reads=[('ps', b1)], writes=[('vp', i)])
                    S.op('act', ACT(og[:, i, :], k.ps[b1][:, 256:512], AF.Sigmoid), reads=[('ps', b1)], writes=[('og', i)])
                    S.op('pool', TT(og[:, i, :], og[:, i, :], ppc(k, 'mln', hb0 * 128, hb0 * 128 + 256), ALU.mult), reads=['pp'], writes=[('og', i)])

                if DBG.get('stop') == 2:
                    S.barrier()
                    return

                def prep(d, ti):
                    W = Wd[d]
                    pfx = 'B%d' % d
                    tb = ti * 128 // TB
                    tsl = slice(ti * 128, (ti + 1) * 128)
                    lf2 = G_lfn[:, ti, d * 4 + hb0:d * 4 + hb0 + 2]
                    scan_prep_common(k, d, lf2, 'G_lfn', 2, W, pfx)
                    S.op('act', ACT(W['eig'][:], G_ig[:, ti, d * 4 + hb0:d * 4 + hb0 + 2], AF.Exp), reads=['G_ig'], writes=[pfx + 'eig'])
                    for hp in range(2):
                        HP = slice(64 * hp, 64 * hp + 64)
                        S.op('pool', TT(W['qtm'][HP, hp, :], qT[HP, tsl], W['Eexp'][HP, :], ALU.mult),
                             reads=[('qT', tb), pfx + 'Eexp'], writes=[pfx + 'qtm'])
                    for hp in range(2):
                        S.op('pe', MM(k.ps[2][:, hp * 128:(hp + 1) * 128], kT[:, tsl], qTm[:, hp, tsl], True, True),
                             reads=[('kT', tb), ('qTm', tb)], writes=[('ps', 2)])
                    S.op('dve', TT(W['STm'][:], k.ps[2][:, 0:256], W['decT'][:], ALU.mult), reads=[('ps', 2), pfx + 'decT'], writes=[pfx + 'STm'])
                    S.op('pool', TT(W['vw'][:], vp[:, ti, :, :], W['eig'][:].unsqueeze(2).broadcast_to([128, 2, 130]), ALU.mult),
                         reads=[('vp', ti), 'vp1', pfx + 'eig'], writes=[pfx + 'vw'])
                    for p in range(2):
                        P = slice(64 * p, 64 * p + 64)
                        S.op('pool', TT(W['kdm'][P, p, :].rearrange("p (h c) -> p h c", h=2), ktok[P, ti, :].rearrange("p (h c) -> p h c", h=2),
                                       W['dend'][P, :].unsqueeze(2).broadcast_to([64, 2, 64]), ALU.mult),
                             reads=[('ktok', ti), pfx + 'dend'], writes=[pfx + 'kdm'])

                def chain(d, ti, p, second):
                    W = Wd[d]
                    pfx = 'B%d' % d
                    P = slice(64 * p, 64 * p + 64)
                    for hp in range(2):
                        o_ = k.ps[3][P, hp * 130:(hp + 1) * 130]
                        S.op('pe', MM(o_, W['STm'][:, hp * 128 + 64 * p:hp * 128 + 64 * p + 64], W['vw'][:, hp, :], True, False),
                             reads=[pfx + 'STm', pfx + 'vw'], writes=[('ps', 3)])
                        S.op('pe', MM(o_, W['qtm'][:, hp, 64 * p:64 * p + 64], Cbf[:, d, :], False, True),
                             reads=[pfx + 'qtm', 'Cbf%d' % d], writes=[('ps', 3)])
                    for hp in range(2):
                        S.op('pe', MM(k.ps[4][64 * hp:64 * hp + 64, 0:130], W['kdm'][:, p, hp * 64:(hp + 1) * 64], W['vw'][:, hp, :], True, True),
                             reads=[pfx + 'kdm', pfx + 'vw'], writes=[('ps', 4)])
                    tlast = 64 * p + 63 if d == 0 else 64 * p
                    S.op('dve', STT(Cst[:, d, :], Cst[:, d, :], W['Eexp'][:, tlast:tlast + 1], k.ps[4][:, 0:130], ALU.mult, ALU.add),
                         reads=[pfx + 'Eexp', ('ps', 4)], writes=['Cst%d' % d])
                    S.op('act', ACT(Cbf[:, d, :], Cst[:, d, :], AF.Copy), reads=['Cst%d' % d], writes=['Cbf%d' % d])

                def norm_out(d, ti, second):
                    W = Wd[d]
                    pfx = 'B%d' % d
                    dcol = k.ps[3][:, 128:260:130]
                    S.op('dve', TS(W['den'][:], dcol, 1.0, None, ALU.max), reads=[('ps', 3)], writes=[pfx + 'den'])
                    S.op('dve', STT(W['den'][:], dcol, -1.0, W['den'][:], ALU.mult, ALU.max), reads=[('ps', 3)], writes=[pfx + 'den'])
                    den_ap, rden_ap = W['den'][:], W['rden'][:]
                    S.op('dve', (lambda a, b_: lambda e: e.reciprocal(out=a, in_=b_))(rden_ap, den_ap), reads=[pfx + 'den'], writes=[pfx + 'rden'])
                    src = k.ps[3][:, 0:260].rearrange("p (h c) -> p h c", h=2)[:, :, 0:128]
                    rb = W['rden'][:].unsqueeze(2).broadcast_to([128, 2, 128])
                    hbv = HB[:, ti, :].rearrange("p (h c) -> p h c", h=2)
                    hk = [('HB', ti, 0), ('HB', ti, 1)]
                    if not second:
                        S.op('dve', TT(hbv, src, rb, ALU.mult), reads=[('ps', 3), pfx + 'rden'], writes=hk)
                    else:
                        tv = W['tmp'][:].rearrange("p (h c) -> p h c", h=2)
                        S.op('dve', TT(tv, src, rb, ALU.mult), reads=[('ps', 3), pfx + 'rden'], writes=[pfx + 'tmp'])
                        S.op('dve', TT(HB[:, ti, :], HB[:, ti, :], W['tmp'][:], ALU.add), reads=[pfx + 'tmp'], writes=hk)

                def finalize(ti):
                    hb = HB[:, ti, :]
                    hbk = [('HB', ti, 0), ('HB', ti, 1)]
                    for h in range(2):
                        hs = slice(h * 128, (h + 1) * 128)
                        S.op('act', ACT(vt[:, hs], hb[:, hs], AF.Square, accum_out=ssq[:, h:h + 1]), reads=hbk, writes=['fvt', ('fssq', h)])
                    rsqrt_small(k, rs2[:], ssq[:], 2, 1.0 / 128, tm2[:], [('fssq', 0), ('fssq', 1)], ['frs2'], 'ftm2')
                    for h in range(2):
                        hs = slice(h * 128, (h + 1) * 128)
                        S.op('dve', STT(yb[:, hs], hb[:, hs], rs2[:, h:h + 1], og[:, ti, hs], ALU.mult, ALU.mult),
                             reads=['frs2', ('og', ti)] + hbk, writes=['yb'])
                    xT_and_wout(k, yb, 2, woB, 'woB', ti, 'yb', ybT, 'ybT', 5, (6, 7))

                touched = [False] * NT
                stages = []
                for step in range(NT):
                    for d in range(2):
                        ti = step if d == 0 else NT - 1 - step
                        second = touched[ti]

                        def body(d=d, ti=ti, second=second):
                            for p in ((0, 1) if d == 0 else (1, 0)):
                                chain(d, ti, p, second)
                            norm_out(d, ti, second)
                            if second:
                                finalize(ti)
                        stages.append(((lambda d=d, ti=ti: prep(d, ti)), body))
                        touched[ti] = True
                S.play(S.record(stages[0][0]))
                for i in range(len(stages)):
                    nxt = S.record(stages[i + 1][0]) if i + 1 < len(stages) else []
                    S.play(S.record(stages[i][1]), nxt)
        S.barrier()


def gdn_gate_tables(k, l, sb):
    S, NT, nc = k.S, k.NT, k.nc
    winv = k.dr['w_in'][l].rearrange("(kc p) n -> p kc n", p=128)
    wab = sb("wab", [128, 8, 16], BF16)
    gn = sb("gn", [128, NT, 8], F32)
    beta = sb("beta", [128, NT, 8], F32)
    nbeta = sb("nbeta", [128, NT, 8], F32)
    eA = sb("eA", [128, 8], F32)
    load_w(k, wab[:], winv[:, :, 3088:3104], 'wab', 'wab')
    for i in range(NT):
        for kc in range(8):
            S.op('pe', MM(k.ps[0][:, i * 16:(i + 1) * 16], k.hT[:, kc, 1 + i * 128:1 + (i + 1) * 128], wab[:, kc, :], kc == 0, kc == 7),
                 reads=[('hT', i), 'wab'], writes=[('ps', 0)])
    pv = k.ps[0][:, 0:NT * 16].rearrange("p (n c) -> p n c", c=16)
    S.op('dve', TT(gn[:], pv[:, :, 0:8], ppc(k, 'dtb').unsqueeze(1).broadcast_to([128, NT, 8]), ALU.add),
         reads=[('ps', 0), 'pp'], writes=['gn'])
    S.op('dve', CP(beta[:], pv[:, :, 8:16]), reads=[('ps', 0)], writes=['beta'])
    S.op('act', ACT(gn[:], gn[:], AF.Exp), reads=[], writes=['gn'])
    S.op('act', ACT(gn[:], gn[:], AF.Ln, bias=cmc(k, 'one')), reads=['cm'], writes=['gn'])
    S.op('act', ACT(eA[:], ppc(k, 'alog'), AF.Exp), reads=['pp'], writes=['eA'])
    S.op('dve', TT(gn[:], gn[:], eA[:].unsqueeze(1).broadcast_to([128, NT, 8]), ALU.mult), reads=['eA'], writes=['gn'])
    S.op('act', ACT(beta[:], beta[:], AF.Sigmoid), reads=[], writes=['beta'])
    S.op('dve', TS(nbeta[:], beta[:], -1.0, None, ALU.mult), reads=['beta'], writes=['nbeta'])
    return gn, beta, nbeta


def group_C(k, l):
    S, T, NT, nc = k.S, k.T, k.NT, k.nc
    dr = k.dr
    winv = dr['w_in'][l].rearrange("(kc p) n -> p kc n", p=128)
    woutv = dr['w_out'][l].rearrange("(kc p) n -> p kc n", p=128)
    TB = min(512, T)
    NTB = T // TB
    o_cg = PP_OFF['cw_gd'][0]
    identf = cmc(k, 'ident')
    with ExitStack() as es0:
        def sb0(name, shape, dt):
            return es0.enter_context(nc.sbuf_tensor("sb_" + name + "_c%d" % l, shape, dt))
        gn, beta, nbeta = gdn_gate_tables(k, l, sb0)
        for hpass in range(2):
            hb0 = 2 * hpass
            with ExitStack() as esp:
                def sbp(name, shape, dt):
                    return esp.enter_context(nc.sbuf_tensor("sb_" + name + "_c%d_%d" % (l, hpass), shape, dt))
                qT = sbp("qT", [128, T], BF16)
                kT = sbp("kT", [128, T], BF16)
                ktok = sbp("ktok", [128, NT, 128], BF16)
                vtok = sbp("vtok", [128, NT, 128], BF16)
                zg = sbp("zg", [128, NT, 128], BF16)
                with ExitStack() as es:
                    def sb(name, shape, dt):
                        return es.enter_context(nc.sbuf_tensor("sb_" + name + "_c1%d_%d" % (l, hpass), shape, dt))
                    convbufs = [sb("convbuf%d" % i, [128, T + 4], F32) for i in range(2)]
                    accs = [sb("cacc%d" % i, [128, T], F32) for i in range(2)]
                    xs = sb("cxs", [128, T], BF16)
                    sqf = [sb("sqf%d" % i, [128, TB], F32) for i in range(2)]
                    lnb = [sb("lnb%d" % i, [128, TB], F32) for i in range(2)]
                    wc = [sb("wc%d" % i, [128, 8, 128], BF16) for i in range(2)]
                    wz = sb("wz", [128, 8, 128], BF16)
                    for cb_ in convbufs:
                        S.op('pool', MSET(cb_[:, 0:2], 0.0), writes=['cbpad'])
                        S.op('pool', MSET(cb_[:, T + 2:T + 4], 0.0), writes=['cbpad'])
                    for ci, cc in enumerate((hpass, 2 + hpass, 4 + hpass)):
                        w_ = wc[ci % 2]
                        convbuf = convbufs[ci % 2]
                        acc = accs[ci % 2]
                        ck = 'cacc%d' % (ci % 2)
                        load_w(k, w_[:], winv[:, :, 2064 + cc * 128:2064 + (cc + 1) * 128], ('wc', ci % 2), 'wc%d' % (ci % 2))
                        for tb in range(NTB):
                            bank = tb % 2
                            hreads = [('hT', tb * (TB // 128) + q) for q in range(TB // 128)]
                            for kc in range(8):
                                S.op('pe', MM(k.ps[bank][:, 0:TB], w_[:, kc, :], k.hT[:, kc, 1 + tb * TB:1 + (tb + 1) * TB], kc == 0, kc == 7),
                                     reads=[('wc', ci % 2)] + hreads, writes=[('ps', bank)])
                            S.op('act', ACT(convbuf[:, 2 + tb * TB:2 + (tb + 1) * TB], k.ps[bank][:, 0:TB], AF.Copy),
                                 reads=[('ps', bank)], writes=[('cb', ci % 2, tb)])
                        cbk = [('cb', ci % 2, tb) for tb in range(NTB)] + ['cbpad']
                        for j in range(5):
                            wj = k.pp[:, o_cg + j * 6 + cc:o_cg + j * 6 + cc + 1]
                            if j == 0:
                                S.op('dve', TS(acc[:], convbuf[:, 0:T], wj, None, ALU.mult), reads=cbk + ['pp'], writes=[ck])
                            else:
                                S.op('dve', STT(acc[:], convbuf[:, j:j + T], wj, acc[:], ALU.mult, ALU.add), reads=cbk + ['pp'], writes=[ck])
                        if ci == 2:
                            S.op('act', ACT(xs[:], acc[:], AF.Silu), reads=[ck], writes=['cxs'])
                            for i in range(NT):
                                bank = 2 + i % 2
                                pst = k.ps[bank][:].bitcast(BF16)
                                S.op('pe', TR(pst[:, 0:128], xs[:, i * 128:(i + 1) * 128], k.ident[:]), reads=['cxs', 'ident'], writes=[('ps', bank)])
                                S.op('act', ACT(vtok[:, i, :], pst[:, 0:128], AF.Copy), reads=[('ps', bank)], writes=[('vtok', i)])
                        else:
                            S.op('act', ACT(acc[:], acc[:], AF.Silu), reads=[], writes=[ck])
                            dst = qT if ci == 0 else kT
                            dkey = 'cqT' if ci == 0 else 'ckT'
                            for tb in range(NTB):
                                b = tb % 2
                                bank = 2 + b
                                sl = slice(tb * TB, (tb + 1) * TB)
                                S.op('dve', TT(sqf[b][:], acc[:, sl], acc[:, sl], ALU.mult), reads=[ck], writes=[('sqf', b)])
                                S.op('pe', MM(k.ps[bank][:, 0:TB], cmc(k, 'BD'), sqf[b][:], True, True), reads=['cm', ('sqf', b)], writes=[('ps', bank)])
                                S.op('act', ACT(lnb[b][:], k.ps[bank][:, 0:TB], AF.Ln, bias=cmc(k, 'eps')), reads=[('ps', bank), 'cm'], writes=[('lnb', b)])
                                S.op('act', ACT(lnb[b][:], lnb[b][:], AF.Exp, scale=-0.5), reads=[], writes=[('lnb', b)])
                                if ci == 0:
                                    S.op('dve', STT(dst[:, sl], acc[:, sl], 0.125, lnb[b][:], ALU.mult, ALU.mult), reads=[ck, ('lnb', b)], writes=[(dkey, tb)])
                                else:
                                    S.op('dve', TT(dst[:, sl], acc[:, sl], lnb[b][:], ALU.mult), reads=[ck, ('lnb', b)], writes=[(dkey, tb)])
                            if ci == 1:
                                for i in range(NT):
                                    bank = 4 + i % 2
                                    pst = k.ps[bank][:].bitcast(BF16)
                                    S.op('pe', TR(pst[:, 0:128], kT[:, i * 128:(i + 1) * 128], k.ident[:]), reads=[('ckT', i * 128 // TB), 'ident'], writes=[('ps', bank)])
                                    S.op('act', ACT(ktok[:, i, :], pst[:, 0:128], AF.Copy), reads=[('ps', bank)], writes=[('cktok', i)])
                    load_w(k, wz[:], winv[:, :, 2832 + hpass * 128:2832 + (hpass + 1) * 128], 'wz', 'wz')
                    for i in range(NT):
                        bank = 6 + i % 2
                        for kc in range(8):
                            S.op('pe', MM(k.ps[bank][:, 0:128], k.hT[:, kc, 1 + i * 128:1 + (i + 1) * 128], wz[:, kc, :], kc == 0, kc == 7),
                                 reads=[('hT', i), 'wz'], writes=[('ps', bank)])
                        S.op('act', ACT(zg[:, i, :], k.ps[bank][:, 0:128], AF.Silu), reads=[('ps', bank)], writes=[('zg', i)])
                        S.op('pool', TT(zg[:, i, :], zg[:, i, :], ppc(k, 'gdn', hb0 * 64, hb0 * 64 + 128), ALU.mult), reads=['pp'], writes=[('zg', i)])
                    S.barrier()
                if DBG.get('stop') == 31:
                    S.barrier()
                    return
                with ExitStack() as es:
                    def sb(name, shape, dt):
                        return es.enter_context(nc.sbuf_tensor("sb_" + name + "_c2%d_%d" % (l, hpass), shape, dt))
                    OC = sb("OC", [128, NT, 128], F32)
                    woC = sb("woC", [128, 1, D], BF16)
                    Sst = sb("Sst", [128, 2, 64], F32)
                    Sbf = sb("Sbf", [128, 2, 64], BF16)
                    load_w(k, woC[:], woutv[:, 6 + hpass:7 + hpass, :], 'woC', 'woC')
                    S.op('pool', MSET(Sst[:], 0.0), writes=['Sst0', 'Sst1'])
                    S.op('pool', MSET(Sbf[:], 0.0), writes=['Sbf0', 'Sbf1'])
                    GxE = sb("GxE", [128, 256], F32)
                    decT = sb("decT", [128, 256], F32)
                    nbod = sb("nbod", [128, 256], F32)
                    offd = sb("offd", [128, 256], F32)
                    tmpf = sb("tmpf", [128, 256], F32)
                    qm = sb("qm", [128, 2, 128], BF16)
                    identf2 = sb("identf2", [128, 256], F32)
                    S.op('pool', MSET(qm[:], 0.0), writes=['Cqm'])
                    for hp in range(2):
                        S.op('dve', CP(identf2[:, hp * 128:(hp + 1) * 128], identf), reads=['cm'], writes=['identf2'])
                        S.op('dve', TS(offd[:, hp * 128:(hp + 1) * 128], identf, -1.0, 1.0, ALU.mult, ALU.add), reads=['cm'], writes=['offd'])
                    Dd = []
                    Wd = []
                    for d in range(2):
                        Dd.append(dict(
                            Qb=[sb("Qb%d_%d" % (d, i), [128, 256], BF16) for i in range(2)],
                            Pb=[sb("Pb%d_%d" % (d, i), [128, 256], BF16) for i in range(2)],
                            Wb=[sb("Wb%d_%d" % (d, i), [128, 256], BF16) for i in range(2)],
                            Wf=sb("Wf%d" % d, [128, 256], F32),
                            Wfin=sb("Wfin%d" % d, [128, 256], BF16), R0=sb("R0%d" % d, [128, 128], BF16),
                            vnew=sb("vnew%d" % d, [128, 128], BF16)))
                        for nm in ('R0', 'vnew'):
                            S.op('pool', MSET(Dd[d][nm][:], 0.0), writes=['C%d%s' % (d, nm)])
                        row = []
                        for par in range(2):
                            W = dict(GxE=GxE, decT=decT,
                                     Eexp=sb("Eexp%d%d" % (d, par), [128, 128], F32), dend=sb("dend%d%d" % (d, par), [128, 2], F32),
                                     qtm=sb("qtm%d%d" % (d, par), [128, 2, 128], BF16), ktm=sb("ktm%d%d" % (d, par), [128, 2, 128], BF16),
                                     kdm=sb("kdm%d%d" % (d, par), [128, 2, 128], BF16), attnT=sb("attnT%d%d" % (d, par), [128, 256], BF16))
                            for nm in ('qtm', 'ktm', 'kdm'):
                                S.op('pool', MSET(W[nm][:], 0.0), writes=['C%d%d%s' % (d, par, nm)])
                            row.append(W)
                        Wd.append(row)
                    vt = sb("fvt", [128, 128], F32)
                    ssq = sb("fssq", [128, 2], F32)
                    rs2 = sb("frs2", [128, 2], F32)
                    tm2 = sb("ftm2", [128, 2], F32)
                    yc = sb("yc", [128, 128], BF16)
                    ycT = sb("ycT", [128, 1, 128], BF16)

                    def prepA(d, ti, par):
                        W = Wd[d][par]
                        Dx = Dd[d]
                        Qb, Pb, Wb, Wf = Dx['Qb'], Dx['Pb'], Dx['Wb'], Dx['Wf']
                        pfx = 'C%d' % d
                        pf2 = 'C%d%d' % (d, par)
                        tb = ti * 128 // TB
                        tsl = slice(ti * 128, (ti + 1) * 128)
                        lf2 = gn[:, ti, d * 4 + hb0:d * 4 + hb0 + 2]
                        scan_prep_common(k, d, lf2, 'gn', 2, W, 'CS', pfxd=pf2)
                        for hp in range(2):
                            HP = slice(64 * hp, 64 * hp + 64)
                            S.op('pool', TT(W['qtm'][HP, hp, :], qT[HP, tsl], W['Eexp'][HP, :], ALU.mult),
                                 reads=[('cqT', tb), pf2 + 'Eexp'], writes=[pf2 + 'qtm'])
                            S.op('pool', TT(W['ktm'][HP, hp, :], kT[HP, tsl], W['Eexp'][HP, :], ALU.mult),
                                 reads=[('ckT', tb), pf2 + 'Eexp'], writes=[pf2 + 'ktm'])
                            S.op('act', ACT(qm[HP, hp, :], qT[HP, tsl], AF.Copy), reads=[('cqT', tb)], writes=['Cqm'])
                        for p in range(2):
                            P = slice(64 * p, 64 * p + 64)
                            S.op('pool', TT(W['kdm'][P, p, :].rearrange("p (h c) -> p h c", h=2), ktok[P, ti, :].rearrange("p (h c) -> p h c", h=2),
                                            W['dend'][P, :].unsqueeze(2).broadcast_to([64, 2, 64]), ALU.mult),
                                 reads=[('cktok', ti), pf2 + 'dend'], writes=[pf2 + 'kdm'])
                        for hp in range(2):
                            S.op('pe', MM(k.ps[3][:, hp * 128:(hp + 1) * 128], kT[:, tsl], qm[:, hp, :], True, True),
                                 reads=[('ckT', tb), 'Cqm'], writes=[('ps', 3)])
                        S.op('dve', TT(W['attnT'][:], k.ps[3][:, 0:256], decT[:], ALU.mult), reads=[('ps', 3), 'CSdecT'], writes=[pf2 + 'attnT'])
                        for hp in range(2):
                            HP = slice(64 * hp, 64 * hp + 64)
                            S.op('act', ACT(qm[HP, hp, :], kT[HP, tsl], AF.Copy), reads=[('ckT', tb)], writes=['Cqm'])
                        for hp in range(2):
                            S.op('pe', MM(k.ps[3][:, 256 + hp * 128:256 + (hp + 1) * 128], kT[:, tsl], qm[:, hp, :], True, True),
                                 reads=[('ckT', tb), 'Cqm'], writes=[('ps', 3)])
                        nb = nbeta[:, ti, d * 4 + hb0:d * 4 + hb0 + 2]
                        S.op('pool', TT(nbod[:].rearrange("p (h t) -> p h t", h=2), offd[:].rearrange("p (h t) -> p h t", h=2),
                                        nb.unsqueeze(2).broadcast_to([128, 2, 128]), ALU.mult), reads=['offd', 'nbeta'], writes=['Cnbod'])
                        S.op('dve', TT(tmpf[:], k.ps[3][:, 256:512], decT[:], ALU.mult), reads=[('ps', 3), 'CSdecT'], writes=['Ctmpf'])
                        S.op('dve', TT(Qb[0][:], tmpf[:], nbod[:], ALU.mult), reads=['Ctmpf', 'Cnbod'], writes=[(pfx + 'Q', 0)])
                        for hp in range(2):
                            S.op('pe', MM(k.ps[2][:, hp * 128:(hp + 1) * 128], Qb[0][:, hp * 128:(hp + 1) * 128], k.ident[:], True, True),
                                 reads=[(pfx + 'Q', 0), 'ident'], writes=[('ps', 2)])
                        S.op('dve', CP(Pb[0][:], k.ps[2][:, 0:256]), reads=[('ps', 2)], writes=[(pfx + 'P', 0)])
                        S.op('dve', TT(Wf[:], Qb[0][:], identf2[:], ALU.add), reads=[(pfx + 'Q', 0), 'identf2'], writes=[pfx + 'Wf'])
                        S.op('act', ACT(Wb[0][:], Wf[:], AF.Copy), reads=[pfx + 'Wf'], writes=[(pfx + 'Wb', 0)])

                    def prepB(d, ti, par):
                        Dx = Dd[d]
                        Qb, Pb, Wb, Wf = Dx['Qb'], Dx['Pb'], Dx['Wb'], Dx['Wf']
                        pfx = 'C%d' % d
                        NL = 5
                        for j in range(NL):
                            a, b_ = j % 2, (j + 1) % 2
                            last = (j == NL - 1)
                            if not last:
                                for hp in range(2):
                                    c_ = slice(hp * 128, (hp + 1) * 128)
                                    S.op('pe', MM(k.ps[4][:, c_], Pb[a][:, c_], Qb[a][:, c_], True, True),
                                         reads=[(pfx + 'P', a), (pfx + 'Q', a)], writes=[('ps', 4)])
                                S.op('act', ACT(Qb[b_][:], k.ps[4][:, 0:256], AF.Copy), reads=[('ps', 4)], writes=[(pfx + 'Q', b_)])
                            for hp in range(2):
                                c_ = slice(hp * 128, (hp + 1) * 128)
                                S.op('pe', MM(k.ps[5][:, c_], Qb[a][:, c_], Pb[a][:, c_], True, True),
                                     reads=[(pfx + 'P', a), (pfx + 'Q', a)], writes=[('ps', 5)])
                            S.op('dve', CP(Pb[b_][:], k.ps[5][:, 0:256]), reads=[('ps', 5)], writes=[(pfx + 'P', b_)])
                            for hp in range(2):
                                c_ = slice(hp * 128, (hp + 1) * 128)
                                S.op('pe', MM(k.ps[6][:, c_], Pb[b_][:, c_], Wb[a][:, c_], True, True),
                                     reads=[(pfx + 'P', b_), (pfx + 'Wb', a)], writes=[('ps', 6)])
                            if not last:
                                S.op('dve', TT(Wf[:], Wf[:], k.ps[6][:, 0:256], ALU.add), reads=[('ps', 6)], writes=[pfx + 'Wf'])
                                S.op('act', ACT(Wb[b_][:], Wf[:], AF.Copy), reads=[pfx + 'Wf'], writes=[(pfx + 'Wb', b_)])
                            else:
                                S.op('dve', TT(Dx['Wfin'][:], Wf[:], k.ps[6][:, 0:256], ALU.add), reads=[('ps', 6), pfx + 'Wf'], writes=[pfx + 'Wfin'])

                    def chain(d, ti, p, second, par):
                        W = dict(Wd[d][par])
                        W.update(Wfin=Dd[d]['Wfin'], R0=Dd[d]['R0'], vnew=Dd[d]['vnew'])
                        pfx = 'C%d' % d
                        pf2 = 'C%d%d' % (d, par)
                        P = slice(64 * p, 64 * p + 64)
                        cs = slice(64 * p, 64 * p + 64)
                        for hp in range(2):
                            S.op('pe', MM(k.ps[7][P, hp * 64:(hp + 1) * 64], W['ktm'][:, hp, cs], Sbf[:, d, :], True, True),
                                 reads=[pf2 + 'ktm', 'Sbf%d' % d], writes=[('ps', 7)])
                        S.op('dve', TT(W['R0'][P, :], vtok[P, ti, :], k.ps[7][P, 0:128], ALU.subtract), reads=[('vtok', ti), ('ps', 7)], writes=[pfx + 'R0'])
                        for hp in range(2):
                            S.op('pe', MM(k.ps[7][P, 128 + hp * 64:128 + (hp + 1) * 64], W['Wfin'][:, hp * 128 + 64 * p:hp * 128 + 64 * p + 64],
                                          W['R0'][:, hp * 64:(hp + 1) * 64], True, True),
                                 reads=[pfx + 'Wfin', pfx + 'R0'], writes=[('ps', 7)])
                        bt = beta[P, ti, d * 4 + hb0:d * 4 + hb0 + 2]
                        S.op('dve', TT(W['vnew'][P, :].rearrange("p (h c) -> p h c", h=2), k.ps[7][P, 128:256].rearrange("p (h c) -> p h c", h=2),
                                       bt.unsqueeze(2).broadcast_to([64, 2, 64]), ALU.mult), reads=[('ps', 7), 'beta'], writes=[pfx + 'vnew'])
                        for hp in range(2):
                            o_ = k.ps[7][P, 256 + hp * 64:256 + (hp + 1) * 64]
                            S.op('pe', MM(o_, W['qtm'][:, hp, cs], Sbf[:, d, :], True, False), reads=[pf2 + 'qtm', 'Sbf%d' % d], writes=[('ps', 7)])
                            S.op('pe', MM(o_, W['attnT'][:, hp * 128 + 64 * p:hp * 128 + 64 * p + 64], W['vnew'][:, hp * 64:(hp + 1) * 64], False, True),
                                 reads=[pf2 + 'attnT', pfx + 'vnew'], writes=[('ps', 7)])
                        if not second:
                            S.op('dve', CP(OC[P, ti, :], k.ps[7][P, 256:384]), reads=[('ps', 7)], writes=[('OC', ti, p)])
                        else:
                            S.op('dve', TT(OC[P, ti, :], OC[P, ti, :], k.ps[7][P, 256:384], ALU.add), reads=[('ps', 7)], writes=[('OC', ti, p)])
                        for hp in range(2):
                            S.op('pe', MM(k.ps[7][64 * hp:64 * hp + 64, 384:448], W['kdm'][:, p, hp * 64:(hp + 1) * 64], W['vnew'][:, hp * 64:(hp + 1) * 64], True, True),
                                 reads=[pf2 + 'kdm', pfx + 'vnew'], writes=[('ps', 7)])
                        tlast = 64 * p + 63 if d == 0 else 64 * p
                        S.op('dve', STT(Sst[:, d, :], Sst[:, d, :], W['Eexp'][:, tlast:tlast + 1], k.ps[7][:, 384:448], ALU.mult, ALU.add),
                             reads=[pf2 + 'Eexp', ('ps', 7)], writes=['Sst%d' % d])
                        S.op('act', ACT(Sbf[:, d, :], Sst[:, d, :], AF.Copy), reads=['Sst%d' % d], writes=['Sbf%d' % d])

                    def finalize(ti):
                        oc = OC[:, ti, :]
                        ock = [('OC', ti, 0), ('OC', ti, 1)]
                        for h in range(2):
                            hs = slice(h * 64, (h + 1) * 64)
                            S.op('act', ACT(vt[:, hs], oc[:, hs], AF.Square, accum_out=ssq[:, h:h + 1]), reads=ock, writes=['fvt', ('fssq', h)])
                        rsqrt_small(k, rs2[:], ssq[:], 2, 1.0 / 64, tm2[:], [('fssq', 0), ('fssq', 1)], ['frs2'], 'ftm2')
                        for h in range(2):
                            hs = slice(h * 64, (h + 1) * 64)
                            S.op('dve', STT(yc[:, hs], oc[:, hs], rs2[:, h:h + 1], zg[:, ti, hs], ALU.mult, ALU.mult),
                                 reads=['frs2', ('zg', ti)] + ock, writes=['yc'])
                        xT_and_wout(k, yc, 1, woC, 'woC', ti, 'yc', ycT, 'ycT', 7, (7, 7))

                    touched = [False] * NT
                    stg = []
                    for step in range(NT):
                        for d in range(2):
                            ti = step if d == 0 else NT - 1 - step
                            stg.append((d, ti, step % 2, touched[ti]))
                            touched[ti] = True

                    def recA(i):
                        d, ti, par, sec = stg[i]
                        return S.record(lambda: prepA(d, ti, par))

                    def recB(i):
                        d, ti, par, sec = stg[i]
                        return S.record(lambda: prepB(d, ti, par))

                    def recC(i):
                        d, ti, par, sec = stg[i]

                        def body():
                            for p in ((0, 1) if d == 0 else (1, 0)):
                                chain(d, ti, p, sec, par)
                            if sec:
                                finalize(ti)
                        return S.record(body)
                    n = len(stg)
                    S.play(recA(0))
                    S.play(recB(0), recA(1) if n > 1 else [])
                    for i in range(n):
                        S.play(recC(i), recB(i + 1) if i + 1 < n else [], recA(i + 2) if i + 2 < n else [])
                    S.barrier()
        S.barrier()


def make_in_maps(inputs, T=SEQ, depth=DEPTH, ncores=8):
    x = np.asarray(inputs['x'], np.float32)
    norms = np.zeros((2 * depth + 1, D), np.float32)
    for l in range(depth):
        norms[2 * l] = np.asarray(inputs['norm_mix'][l])
        norms[2 * l + 1] = np.asarray(inputs['norm_ffn'][l])
    norms[2 * depth] = np.asarray(inputs['norm_final'])
    pp = np.stack([pack_params(inputs, l) for l in range(depth)])
    ws = np.asarray(inputs['gm_ws'], np.float32)[:depth]
    wsT = np.ascontiguousarray(ws.transpose(0, 3, 1, 2)).reshape(depth, 128, 512)
    shared = dict(
        norms=norms, pp=pp, wsT=wsT, cm=const_masks(),
        w_in=np.ascontiguousarray(np.asarray(inputs['w_in'], np.float32)[:depth]),
        w_out=np.ascontiguousarray(np.asarray(inputs['w_out'], np.float32)[:depth]),
        w_up=np.ascontiguousarray(np.asarray(inputs['w_up'], np.float32)[:depth]),
        w_down=np.ascontiguousarray(np.asarray(inputs['w_down'], np.float32)[:depth]),
    )
    maps = []
    for c in range(ncores):
        m = dict(shared)
        m['x'] = np.ascontiguousarray(x[c, :T])
        maps.append(m)
    return maps


_NC_CACHE = {}
DBG = {}
_LAST = {}


def kernel(**inputs):
    if 'nc' not in _NC_CACHE:
        _NC_CACHE['nc'] = build()
    nc = _NC_CACHE['nc']
    maps = make_in_maps(inputs)
    res = run_bass_kernel_spmd(nc, maps, core_ids=list(range(8)))
    return np.stack([np.asarray(r['out'], np.float32) for r in res.results], axis=0)
```

```python
import numpy as np
from contextlib import ExitStack
import concourse.bass as bass
import concourse.mybir as mybir
from concourse.bass_utils import run_bass_kernel_spmd

F32 = mybir.dt.float32
BF16 = mybir.dt.bfloat16
AF = mybir.ActivationFunctionType
ALU = mybir.AluOpType
AX = mybir.AxisListType

D = 1024
DFF = 2816
NIN = 3104
EPS = 1e-6
NFC = DFF // 128
SEQ = 2048
DEPTH = 2


class Sched:
    ENG = ('pe', 'act', 'dve', 'pool', 'sp')

    def __init__(self, nc, es):
        self.nc = nc
        self.es = es
        self.sem = {e: es.enter_context(nc.semaphore('s_' + e)) for e in ('pe', 'act', 'dve', 'pool')}
        self.cnt = dict.fromkeys(('pe', 'act', 'dve', 'pool'), 0)
        self.q = {e: [] for e in self.ENG}
        self.lastw = {}
        self.readers = {}
        self.know = {}
        self._tm = dict(eng={}, w={}, r={})
        self.tokclock = {}
        self.tokseq = {}
        self.seq = 0
        self.chan = {}

    def _deps(self, reads, writes):
        toks = []
        for r in reads:
            t = self.lastw.get(r)
            if t:
                toks.append(t)
        for w in writes:
            t = self.lastw.get(w)
            if t:
                toks.append(t)
            toks.extend(self.readers.get(w, ()))
        return toks

    def _waits(self, eng, toks):
        K = self.know.setdefault(eng, {})
        need = {}
        for (k, v) in sorted(set(toks), key=lambda t: -self.tokseq.get(t, 0)):
            if eng == 'pe' and k == 'pe':
                continue
            if K.get(k, 0) >= v:
                continue
            if need.get(k, 0) < v:
                need[k] = v
            K[k] = v
            for kk, vv in self.tokclock.get((k, v), {}).items():
                if K.get(kk, 0) < vv:
                    K[kk] = vv
        return list(need.items())

    def _commit(self, tok, eng, reads, writes):
        self.seq += 1
        self.tokseq[tok] = self.seq
        clk = dict(self.know.get(eng, {}))
        self.tokclock[tok] = clk
        for r in reads:
            if r not in writes:
                self.readers.setdefault(r, []).append(tok)
        for w in writes:
            self.lastw[w] = tok
            self.readers[w] = []

    def op(self, eng, fn, reads=(), writes=()):
        psr = tuple(r for r in reads if isinstance(r, tuple) and r[0] == 'ps')
        reads = tuple(r for r in reads if not (isinstance(r, tuple) and r[0] == 'ps'))
        writes = tuple(writes) + tuple(r for r in psr if r not in writes)
        waits = self._waits(eng, self._deps(reads, writes))
        self.cnt[eng] += 1
        tok = (eng, self.cnt[eng])
        self.q[eng].append((waits, fn, ('inc', eng)))
        self._commit(tok, eng, reads, writes)

    def dma(self, queue, fn, reads, writes, chan):
        reads = tuple(reads)
        writes = tuple(writes)
        if chan not in self.chan:
            self.chan[chan] = [self.es.enter_context(self.nc.semaphore('c_' + chan)), 0]
        waits = self._waits(queue, self._deps(reads, writes))
        c = self.chan[chan]
        c[1] += 16
        tok = (('dma', chan), c[1])
        self.q[queue].append((waits, fn, ('dma', chan)))
        self._commit(tok, queue, reads, writes)

    def record(self, fn):
        lst = []
        orig = self.op
        self._cur = lst
        self.op = lambda *a, **kw: self._cur.append((a, kw))
        try:
            fn()
        finally:
            self.op = orig
            self._cur = None
        return lst

    def atomic(self):
        sched = self

        class _A:
            def __enter__(self_):
                self_.parent = getattr(sched, '_cur', None)
                if self_.parent is not None:
                    sched._cur = []
                return self_

            def __exit__(self_, *exc):
                if self_.parent is not None:
                    sub = sched._cur
                    sched._cur = self_.parent
                    sched._cur.append(('atomic', sub))
                return False
        return _A()

    COST = dict(pe=0.14, act=0.38, dve=0.42, pool=1.0, sp=2.0)

    def play(self, *lists):
        flat = []
        for l in lists:
            out = []

            def walk(items):
                for it in items:
                    if it[0] == 'atomic':
                        out.append(('grp', [x for x in self._flatten(it[1])]))
                    else:
                        out.append(('grp', [it]))
            walk(l)
            if out:
                flat.append(out)
        if not flat:
            return
        tm = self._tm
        idx = [0] * len(flat)
        remaining = sum(len(l) for l in flat)
        while remaining:
            best, best_t = None, None
            for li, l in enumerate(flat):
                if idx[li] >= len(l):
                    continue
                a, kw = l[idx[li]][1][0]
                t = self._est_start(a[0], kw.get('reads', ()), kw.get('writes', ()))
                key = (t, -(len(l) - idx[li]))
                if best is None or key < best_t:
                    best, best_t = li, key
            for a, kw in flat[best][idx[best]][1]:
                self._est_commit(a[0], kw.get('reads', ()), kw.get('writes', ()))
                self.op(*a, **kw)
            idx[best] += 1
            remaining -= 1

    def _flatten(self, items):
        for it in items:
            if it[0] == 'atomic':
                for x in self._flatten(it[1]):
                    yield x
            else:
                yield it

    def _est_start(self, eng, reads, writes):
        tm = self._tm
        t = tm['eng'].get(eng, 0.0)
        for r in reads:
            t = max(t, tm['w'].get(r, 0.0))
        for w in writes:
            t = max(t, tm['w'].get(w, 0.0), tm['r'].get(w, 0.0))
        return t

    def _est_commit(self, eng, reads, writes, cost=None):
        tm = self._tm
        if cost is None:
            cost = self.COST.get(eng, 0.4)
        t0 = self._est_start(eng, reads, writes)
        t1 = t0 + cost
        tm['eng'][eng] = t1
        lat = t1 + 0.15
        for r in reads:
            tm['r'][r] = max(tm['r'].get(r, 0.0), lat)
        for w in writes:
            tm['w'][w] = lat
            tm['r'][w] = 0.0

    def wait_all(self, eng):
        toks = [(e, self.cnt[e]) for e in self.cnt if self.cnt[e] > 0]
        toks += [(('dma', ch), c[1]) for ch, c in self.chan.items() if c[1] > 0]
        waits = self._waits(eng, toks)
        self.q[eng].append((waits, None, None))

    def barrier(self):
        for e in self.ENG:
            self.wait_all(e)

    def _semof(self, k):
        if isinstance(k, tuple):
            return self.chan[k[1]][0]
        return self.sem[k]

    def emit(self, block):
        handles = dict(pe=block.tensor, act=block.scalar, dve=block.vector, pool=block.gpsimd, sp=block.sync)

        def mk(eng):
            def body(e):
                for waits, fn, inc in self.q[eng]:
                    for k, v in waits:
                        e.wait_ge(self._semof(k), v)
                    if fn is None:
                        continue
                    ins = fn(e)
                    if inc[0] == 'inc':
                        ins.then_inc(self.sem[inc[1]], 1)
                    else:
                        ins.then_inc(self.chan[inc[1]][0], 16)
            return body
        for eng in self.ENG:
            handles[eng](mk(eng))


PP_FIELDS = [
    ('cw_ffn', 3 * 44), ('cb_ffn', 44),
    ('gmn', 256), ('bsf', 256), ('mlb', 16), ('mln', 512),
    ('alog', 8), ('dtb', 8), ('gdn', 256), ('cw_gd', 5 * 6),
]
PP_OFF = {}
_o = 0
for _n, _s in PP_FIELDS:
    PP_OFF[_n] = (_o, _s)
    _o += _s
NPP = _o


def pack_params(inp, l):
    pp = np.zeros((128, NPP), np.float32)

    def put(name, arr):
        o, s = PP_OFF[name]
        pp[:, o:o + s] = np.asarray(arr, np.float32).reshape(128, s)
    fc = np.asarray(inp['ffn_conv'][l])
    put('cw_ffn', fc.reshape(3, 44, 128).transpose(2, 0, 1))
    put('cb_ffn', np.asarray(inp['ffn_conv_b'][l]).reshape(44, 128).T)
    put('gmn', np.broadcast_to(np.asarray(inp['gm_norm'][l]).reshape(1, 256), (128, 256)))
    bs = np.asarray(inp['gm_bs'][l])
    put('bsf', np.broadcast_to(bs.T[:, :, None], (128, 4, 64)))
    put('mlb', np.broadcast_to(np.asarray(inp['ml_gate_bias'][l]).reshape(1, 16), (128, 16)))
    put('mln', np.broadcast_to(np.asarray(inp['ml_head_norm'][l]).reshape(1, 512), (128, 512)))
    put('alog', np.broadcast_to(np.asarray(inp['gd_A_log'][l]).reshape(1, 8), (128, 8)))
    put('dtb', np.broadcast_to(np.asarray(inp['gd_dt_bias'][l]).reshape(1, 8), (128, 8)))
    put('gdn', np.broadcast_to(np.asarray(inp['gd_head_norm'][l]).reshape(1, 256), (128, 256)))
    gc = np.asarray(inp['gd_conv'][l])
    put('cw_gd', gc.reshape(5, 6, 128).transpose(2, 0, 1))
    return pp


CM_FIELDS = [('ident', 128), ('eps', 1), ('one', 1), ('ones', 64), ('BD', 128),
             ('A20', 128), ('A21', 128), ('U2n0', 128), ('U2n1', 128),
             ('NEGF0', 128), ('NEGF1', 128), ('NEGSF0', 128), ('NEGSF1', 128)]
CM_OFF = {}
_o = 0
for _n, _s in CM_FIELDS:
    CM_OFF[_n] = (_o, _s)
    _o += _s
NCM = _o


def const_masks():
    cm = np.zeros((128, NCM), np.float32)

    def put(name, arr):
        o, n = CM_OFF[name]
        cm[:, o:o + n] = arr
    put('ident', np.eye(128, dtype=np.float32))
    put('eps', EPS)
    put('one', 1.0)
    put('ones', 1.0)
    r = np.arange(128)
    rl = r % 64
    same = (r[:, None] // 64) == (r[None, :] // 64)
    put('BD', same.astype(np.float32))
    for d in range(2):
        def peq(a, b):
            return (a <= b) if d == 0 else (a >= b)

        def prec(a, b):
            return (a < b) if d == 0 else (a > b)
        put('A2%d' % d, (same & prec(rl[None, :], rl[:, None])).astype(np.float32))
        put('U2n%d' % d, -(same & peq(rl[:, None], rl[None, :])).astype(np.float32))
        put('NEGF%d' % d, np.where(same & peq(rl[:, None], rl[None, :]), 0.0, -30000.0))
        put('NEGSF%d' % d, np.where(same & prec(rl[:, None], rl[None, :]), 0.0, -30000.0))
    return cm


def _nfree(ap):
    n = 1
    for d_ in ap.shape[1:]:
        n *= int(d_)
    return n


def _c(f, cost):
    f.cost = cost
    return f


def MM(out, lhsT, rhs, start=True, stop=True):
    n = _nfree(out)
    mult = 4.0 if lhsT.dtype == F32 else 1.0
    return _c(lambda e: e.matmul(out, lhsT=lhsT, rhs=rhs, start=start, stop=stop), 0.10 + mult * max(n, 64) / 1900.0)


def TR(out, in_, ident):
    return _c(lambda e: e.transpose(out=out, in_=in_, identity=ident), 0.16)


def ACT(out, in_, func, **kw):
    return _c(lambda e: e.activation(out=out, in_=in_, func=func, **kw), 0.22 + _nfree(out) / 1300.0)


def TT(out, in0, in1, op):
    return _c(lambda e: e.tensor_tensor(out=out, in0=in0, in1=in1, op=op), 0.16 + _nfree(out) / 960.0)


def STT(out, in0, scalar, in1, op0, op1):
    return _c(lambda e: e.scalar_tensor_tensor(out=out, in0=in0, scalar=scalar, in1=in1, op0=op0, op1=op1), 0.16 + _nfree(out) / 960.0)


def TS(out, in0, s1, s2=None, op0=ALU.mult, op1=None):
    c_ = 0.14 + _nfree(out) / 960.0
    if op1 is None:
        return _c(lambda e: e.tensor_scalar(out=out, in0=in0, scalar1=s1, scalar2=None, op0=op0), c_)
    return _c(lambda e: e.tensor_scalar(out=out, in0=in0, scalar1=s1, scalar2=s2, op0=op0, op1=op1), c_)


def RED(out, in_, op=ALU.add, axis=AX.X):
    return _c(lambda e: e.tensor_reduce(out=out, in_=in_, axis=axis, op=op), 0.16 + _nfree(in_) / 960.0)


def CP(out, in_):
    return _c(lambda e: e.tensor_copy(out=out, in_=in_), 0.16 + _nfree(out) / 960.0)


def MSET(ap, v):
    return _c(lambda e: e.memset(ap, v), 0.2 + _nfree(ap) / 960.0)


def DMA(out, in_):
    return _c(lambda e: e.dma_start(out=out, in_=in_), 2.0)


class K:
    pass


def build(T=SEQ, depth=DEPTH, do_mixer=True, do_ffn=True):
    NT = T // 128
    nc = bass.Bass("TRN2", target_bir_lowering=False)
    k = K()
    k.nc, k.T, k.NT, k.depth = nc, T, NT, depth
    dr = {}
    dr['x'] = nc.dram_tensor("x", [T, D], F32, kind="ExternalInput").ap()
    dr['out'] = nc.dram_tensor("out", [T, D], F32, kind="ExternalOutput").ap()
    dr['norms'] = nc.dram_tensor("norms", [2 * depth + 1, D], F32, kind="ExternalInput").ap()
    dr['w_in'] = nc.dram_tensor("w_in", [depth, D, NIN], F32, kind="ExternalInput").ap()
    dr['w_out'] = nc.dram_tensor("w_out", [depth, D, D], F32, kind="ExternalInput").ap()
    dr['w_up'] = nc.dram_tensor("w_up", [depth, D, 2 * DFF], F32, kind="ExternalInput").ap()
    dr['w_down'] = nc.dram_tensor("w_down", [depth, DFF, D], F32, kind="ExternalInput").ap()
    dr['pp'] = nc.dram_tensor("pp", [depth, 128, NPP], F32, kind="ExternalInput").ap()
    dr['wsT'] = nc.dram_tensor("wsT", [depth, 128, 512], F32, kind="ExternalInput").ap()
    dr['cm'] = nc.dram_tensor("cm", [128, NCM], F32, kind="ExternalInput").ap()
    k.dr = dr

    with ExitStack() as es:
        S = Sched(nc, es)
        k.S = S
        _LAST['S'] = S
        k.es = es

        def sb(name, shape, dt):
            return es.enter_context(nc.sbuf_tensor("sb_" + name, shape, dt))
        k.sb = sb
        k.X = sb("X", [128, NT, D], F32)
        k.hT = sb("hT", [128, 8, T + 2], BF16)
        k.gb = sb("gb", [128, D], F32)
        k.pp = sb("pp", [128, NPP], F32)
        k.cm = sb("cm", [128, NCM], F32)
        k.ident = sb("ident", [128, 128], BF16)
        k.ss = sb("ss", [128, NT], F32)
        k.rstd = sb("rstd", [128, NT], F32)
        k.xn = [sb("xn%d" % i, [128, D], BF16) for i in range(2)]
        k.ps = [es.enter_context(nc.psum_tensor("ps%d" % i, [128, 512], F32)) for i in range(8)]

        S.dma('sp', DMA(k.cm[:], dr['cm']), reads=[], writes=['cm'], chan='cm')
        S.op('dve', CP(k.ident[:], k.cm[:, CM_OFF['ident'][0]:CM_OFF['ident'][0] + 128]), reads=['cm'], writes=['ident'])
        k.negfb = [sb("negfb%d" % d_, [128, 256], BF16) for d_ in range(2)]
        for d_ in range(2):
            for h_ in range(2):
                S.op('dve', CP(k.negfb[d_][:, h_ * 128:(h_ + 1) * 128], cmc(k, 'NEGF%d' % d_)), reads=['cm'], writes=['negfb'])
        S.op('pool', MSET(k.hT[:, :, 0:1], 0.0), writes=['hTpad'])
        S.op('pool', MSET(k.hT[:, :, T + 1:T + 2], 0.0), writes=['hTpad'])

        xv = dr['x'].rearrange("(n p) d -> n p d", p=128)
        for i in range(NT):
            S.dma('sp' if i % 2 == 0 else 'act', DMA(k.X[:, i, :], xv[i]), reads=[], writes=[('X', i, 0), ('X', i, 1)], chan='x%d' % i)

        for l in range(depth):
            S.dma('sp', DMA(k.pp[:], dr['pp'][l]), reads=[], writes=['pp'], chan='pp')
            if do_mixer:
                phase_mixer(k, l, do_mixer if isinstance(do_mixer, str) else 'ABC')
            if do_ffn:
                phase_ffn(k, l)
        final_norm(k)
        S.wait_all('sp')
        with nc.Block() as block:
            S.emit(block)
    return nc


def rms_stats(k, norm_idx):
    S, NT = k.S, k.NT
    S.dma('sp', DMA(k.gb[:], k.dr['norms'][norm_idx:norm_idx + 1, :].partition_broadcast(128)),
          reads=[], writes=['gb'], chan='gb')
    for i in range(NT):
        if i % 2 == 0:
            S.op('act', ACT(k.xn[0][:], k.X[:, i, :], AF.Square, accum_out=k.ss[:, i:i + 1]),
                 reads=[('X', i, 0), ('X', i, 1)], writes=[('xn', 0), ('ss', i)])
        else:
            S.op('dve', (lambda o, a, acc: lambda e: e.scalar_tensor_tensor(out=o, in0=a, scalar=1.0, in1=a, op0=ALU.mult, op1=ALU.mult,
                                                                            accum_out=acc))(k.xn[1][:], k.X[:, i, :], k.ss[:, i:i + 1]),
                 reads=[('X', i, 0), ('X', i, 1)], writes=[('xn', 1), ('ss', i)])
    allss = [('ss', i) for i in range(NT)]
    S.op('act', ACT(k.rstd[:], k.ss[:], AF.Ln, scale=1.0 / D, bias=k.cm[:, CM_OFF['eps'][0]:CM_OFF['eps'][0] + 1]),
         reads=allss + ['cm'], writes=['rstd'])
    S.op('act', ACT(k.rstd[:], k.rstd[:], AF.Exp, scale=-0.5), reads=['rstd'], writes=['rstd'])


def norm_to_T(k, norm_idx, banks=(6, 7)):
    S, NT = k.S, k.NT
    rms_stats(k, norm_idx)
    for i in range(NT):
        xn = k.xn[i % 2]
        bank = banks[i % 2]
        pst = k.ps[bank][:].bitcast(BF16).rearrange("p (a b) -> p a b", a=8)
        S.op('dve', STT(xn[:], k.X[:, i, :], k.rstd[:, i:i + 1], k.gb[:], ALU.mult, ALU.mult),
             reads=[('X', i, 0), ('X', i, 1), 'rstd', 'gb'], writes=[('xn', i % 2)])
        for kc in range(8):
            S.op('pe', TR(pst[:, kc, :], xn[:, kc * 128:(kc + 1) * 128], k.ident[:]),
                 reads=[('xn', i % 2), 'ident'], writes=[('ps', bank)])
        S.op('act', ACT(k.hT[:, :, 1 + i * 128:1 + (i + 1) * 128], pst, AF.Copy),
             reads=[('ps', bank)], writes=[('hT', i)])


def final_norm(k):
    S, NT = k.S, k.NT
    rms_stats(k, 2 * k.depth)
    ov = k.dr['out'].rearrange("(n p) d -> n p d", p=128)
    for i in range(NT):
        S.op('dve', STT(k.X[:, i, :], k.X[:, i, :], k.rstd[:, i:i + 1], k.gb[:], ALU.mult, ALU.mult),
             reads=['rstd', 'gb'], writes=[('X', i, 0), ('X', i, 1)])
        S.dma('sp' if i % 2 == 0 else 'act', DMA(ov[i], k.X[:, i, :]), reads=[('X', i, 0), ('X', i, 1)], writes=[('out', i)], chan='o%d' % i)


def phase_ffn(k, l):
    S, T, NT, nc = k.S, k.T, k.NT, k.nc
    dr = k.dr
    HALF = min(1024, T)
    NBLK = T // HALF
    NB = max(1, HALF // 512)
    BW = min(512, HALF)
    TPB = HALF // 128
    GRP = 4
    o_cw = PP_OFF['cw_ffn'][0]
    o_cb = PP_OFF['cb_ffn'][0]
    wupv = dr['w_up'][l].rearrange("(kc p) n -> p kc n", p=128)
    wdnv = dr['w_down'][l].rearrange("(fc p) n -> p fc n", p=128)

    with ExitStack() as es:
        def sb(name, shape, dt):
            return es.enter_context(nc.sbuf_tensor("sb_" + name + "_f%d" % l, shape, dt))
        upbuf = [[sb("upbuf%d%d" % (a, b), [128, HALF + 2], F32) for b in range(2)] for a in range(2)]
        acc = [[sb("acc%d%d" % (a, b), [128, HALF], F32) for b in range(2)] for a in range(2)]
        gT = [sb("gT%d" % a, [128, GRP, HALF], BF16) for a in range(2)]
        wup = [[sb("wup%d_%d" % (p_, a), [128, 8, 256], BF16) for a in range(2)] for p_ in range(2)]
        wdn = [sb("wdn%d" % a, [128, GRP, D], BF16) for a in range(2)]

        norm_to_T(k, 2 * l + 1)
        wctr = gctr = uctr = dctr = pctr = 0
        for blk in range(NBLK):
            t0 = blk * HALF
            groups = [list(range(g, min(g + GRP, NFC))) for g in range(0, NFC, GRP)]
            for grp in groups:
                gi = gctr % 2
                gctr += 1
                f0 = grp[0]
                ng = len(grp)
                S.dma('pool', DMA(wdn[gi][:, 0:ng, :], wdnv[:, f0:f0 + ng, :]), reads=[], writes=[('wdn', gi)], chan='wdn%d' % gi)
                for jj, j in enumerate(grp):
                    pb = pctr % 2
                    pctr += 1
                    for part in range(2):
                        fc = j + part * NFC
                        wr = (wctr // 2) % 2
                        wsub = jj % 2
                        if wsub == 0:
                            npair = min(2, ng - jj)
                            S.dma('pool', DMA(wup[part][wr][:, :, 0:npair * 128], wupv[:, :, fc * 128:(fc + npair) * 128]),
                                  reads=[], writes=[('wup', part, wr)], chan='wup%d_%d' % (part, wr))
                        wt = wup[part][wr][:, :, wsub * 128:(wsub + 1) * 128]
                        wkey = ('wup', part, wr)
                        ui = uctr % 2
                        uctr += 1
                        for nb in range(NB):
                            bank = ui * 2 + nb
                            hreads = [('hT', (t0 + nb * BW) // 128 + q) for q in range(BW // 128)]
                            for kc in range(8):
                                S.op('pe', MM(k.ps[bank][:, 0:BW], wt[:, kc, :],
                                              k.hT[:, kc, 1 + t0 + nb * BW:1 + t0 + (nb + 1) * BW], kc == 0, kc == 7),
                                     reads=[wkey] + hreads, writes=[('ps', bank)])
                        hb = 4 + ui
                        for kc in range(8):
                            S.op('pe', MM(k.ps[hb][:, 0:2], wt[:, kc, :], k.hT[:, kc, t0:t0 + HALF + 2:HALF + 1], kc == 0, kc == 7),
                                 reads=[wkey, ('hT', max(0, t0 // 128 - 1)), ('hT', min(NT - 1, (t0 + HALF) // 128)), 'hTpad'],
                                 writes=[('ps', hb)])
                        ubuf = upbuf[part][pb]
                        ac = acc[part][pb]
                        for nb in range(NB):
                            bank = ui * 2 + nb
                            S.op('act', ACT(ubuf[:, 1 + nb * BW:1 + (nb + 1) * BW], k.ps[bank][:, 0:BW], AF.Copy),
                                 reads=[('ps', bank)], writes=[('upbuf', part, pb, nb)])
                            S.op('act', ACT(ac[:, nb * BW:(nb + 1) * BW], k.ps[bank][:, 0:BW], AF.Identity,
                                            scale=k.pp[:, o_cw + 44 + fc:o_cw + 44 + fc + 1],
                                            bias=k.pp[:, o_cb + fc:o_cb + fc + 1]),
                                 reads=[('ps', bank), 'pp'], writes=[('acc', part, pb, nb)])
                        S.op('act', ACT(ubuf[:, 0:HALF + 2:HALF + 1], k.ps[hb][:, 0:2], AF.Copy),
                             reads=[('ps', hb)], writes=[('upbuf', part, pb, 'h')])
                        allub = [('upbuf', part, pb, nb) for nb in range(NB)] + [('upbuf', part, pb, 'h')]
                        allac = [('acc', part, pb, nb) for nb in range(NB)]
                        for tap in (0, 2):
                            S.op('dve', STT(ac[:], ubuf[:, tap:tap + HALF],
                                            k.pp[:, o_cw + tap * 44 + fc:o_cw + tap * 44 + fc + 1], ac[:], ALU.mult, ALU.add),
                                 reads=allub + ['pp'], writes=allac)
                    wctr += 1
                    ag = acc[0][pb]
                    av = acc[1][pb]
                    S.op('act', ACT(ag[:], ag[:], AF.Silu), reads=[], writes=[('acc', 0, pb, nb) for nb in range(NB)])
                    S.op('dve', TT(gT[gi][:, jj, :], ag[:], av[:], ALU.mult),
                         reads=[('acc', 0, pb, nb) for nb in range(NB)] + [('acc', 1, pb, nb) for nb in range(NB)],
                         writes=[('gT', gi, jj)])
                for tl in range(TPB):
                    ti = t0 // 128 + tl
                    for half in range(2):
                        bank = 6 + dctr % 2
                        dctr += 1
                        for jj in range(ng):
                            S.op('pe', MM(k.ps[bank][:, :], gT[gi][:, jj, tl * 128:(tl + 1) * 128],
                                          wdn[gi][:, jj, half * 512:(half + 1) * 512], jj == 0, jj == ng - 1),
                                 reads=[('gT', gi, jj), ('wdn', gi)], writes=[('ps', bank)])
                        xs = k.X[:, ti, half * 512:(half + 1) * 512]
                        S.op('dve', TT(xs, xs, k.ps[bank][:, :], ALU.add), reads=[('ps', bank)], writes=[('X', ti, half)])
        S.barrier()


def cmc(k, name, a=0, b=None):
    o, n = CM_OFF[name]
    return k.cm[:, o + a:o + (n if b is None else b)]


def ppc(k, name, a=0, b=None):
    o, n = PP_OFF[name]
    return k.pp[:, o + a:o + (n if b is None else b)]


def rsqrt_small(k, out, in_, n, scale, tmp, reads, writes, tmpkey):
    S = k.S
    S.op('act', ACT(tmp, in_, AF.Ln, scale=scale, bias=cmc(k, 'eps')), reads=list(reads) + ['cm'], writes=[tmpkey])
    S.op('act', ACT(out, tmp, AF.Exp, scale=-0.5), reads=[tmpkey], writes=writes)


def phase_mixer(k, l, groups):
    S, T, NT, nc = k.S, k.T, k.NT, k.nc
    norm_to_T(k, 2 * l)
    if 'A' in groups:
        group_A(k, l)
    if 'B' in groups:
        group_B(k, l)
    if 'C' in groups:
        group_C(k, l)


def load_w(k, dst, src, key, chan, queue='pool'):
    k.S.dma(queue, DMA(dst, src), reads=[], writes=[key], chan=chan)


def xT_and_wout(k, y_bf, nkc, wo, wokey, ti, ykey, yT, yTkey, tbank, obanks):
    S = k.S
    pst = k.ps[tbank][:].bitcast(BF16).rearrange("p (a b) -> p a b", a=8)
    for kc in range(nkc):
        S.op('pe', TR(pst[:, kc, :], y_bf[:, kc * 128:(kc + 1) * 128], k.ident[:]), reads=[ykey, 'ident'], writes=[('ps', tbank)])
    S.op('act', ACT(yT[:, 0:nkc, :], pst[:, 0:nkc, :], AF.Copy), reads=[('ps', tbank)], writes=[yTkey])
    for half in range(2):
        bank = obanks[half]
        with S.atomic():
            for kc in range(nkc):
                S.op('pe', MM(k.ps[bank][:, :], yT[:, kc, :], wo[:, kc, half * 512:(half + 1) * 512], kc == 0, kc == nkc - 1),
                     reads=[yTkey, wokey], writes=[('ps', bank)])
        xs = k.X[:, ti, half * 512:(half + 1) * 512]
        S.op('dve', TT(xs, xs, k.ps[bank][:, :], ALU.add), reads=[('ps', bank)], writes=[('X', ti, half)])


def group_A(k, l):
    S, T, NT, nc = k.S, k.T, k.NT, k.nc
    dr = k.dr
    winv = dr['w_in'][l].rearrange("(kc p) n -> p kc n", p=128)
    woutv = dr['w_out'][l].rearrange("(kc p) n -> p kc n", p=128)
    with ExitStack() as es:
        def sb(name, shape, dt):
            return es.enter_context(nc.sbuf_tensor("sb_" + name + "_a%d" % l, shape, dt))
        wA = sb("wA", [128, 8, 512], BF16)
        woA = sb("woA", [128, 2, D], BF16)
        wsT = sb("wsT", [128, 512], BF16)
        B = []
        for b in range(2):
            B.append(dict(uv=sb("uv%d" % b, [128, 512], F32), vt=sb("vt%d" % b, [128, 256], F32), ssq=sb("ssq%d" % b, [128, 4], F32),
                          rs4=sb("rs4%d" % b, [128, 4], F32), tm4=sb("tm4%d" % b, [128, 4], F32), vnb=sb("vnb%d" % b, [128, 256], BF16),
                          ya=sb("ya%d" % b, [128, 256], BF16), yaT=sb("yaT%d" % b, [128, 2, 128], BF16)))
        load_w(k, wA[:], winv[:, :, 0:512], 'wA', 'wA')
        load_w(k, woA[:], woutv[:, 0:2, :], 'woA', 'woA')
        load_w(k, wsT[:], dr['wsT'][l], 'wsT', 'wsT')

        def tile_ops(i):
            b = i % 2
            W = B[b]
            u, vt, ssq, rs4, tm4, vnb, ya, yaT = W['uv'], W['vt'], W['ssq'], W['rs4'], W['tm4'], W['vnb'], W['ya'], W['yaT']
            kb = 'A%d' % b
            pa, pg, ptr, po = b, 2 + b, 4 + b, 6 + b
            for kc in range(8):
                S.op('pe', MM(k.ps[pa][:, :], k.hT[:, kc, 1 + i * 128:1 + (i + 1) * 128], wA[:, kc, :], kc == 0, kc == 7),
                     reads=[('hT', i), 'wA'], writes=[('ps', pa)])
            S.op('act', ACT(u[:], k.ps[pa][:, :], AF.Gelu_apprx_tanh), reads=[('ps', pa)], writes=[kb + 'uv'])
            S.op('dve', TT(vt[:], u[:, 256:512], u[:, 256:512], ALU.mult), reads=[kb + 'uv'], writes=[kb + 'vt'])
            S.op('dve', RED(ssq[:], vt[:].rearrange("p (g e) -> p g e", g=4)), reads=[kb + 'vt'], writes=[kb + 'ssq'])
            rsqrt_small(k, rs4[:], ssq[:], 4, 1.0 / 64, tm4[:], [kb + 'ssq'], [kb + 'rs4'], kb + 'tm4')
            S.op('dve', TT(vt[:].rearrange("p (g e) -> p g e", g=4), u[:, 256:512].rearrange("p (g e) -> p g e", g=4),
                           rs4[:].unsqueeze(2).broadcast_to([128, 4, 64]), ALU.mult), reads=[kb + 'uv', kb + 'rs4'], writes=[kb + 'vt'])
            S.op('dve', TT(vnb[:], vt[:], ppc(k, 'gmn'), ALU.mult), reads=[kb + 'vt', 'pp'], writes=[kb + 'vnb'])
            for g in range(4):
                S.op('pe', MM(k.ps[pg][:, g * 64:(g + 1) * 64], wsT[:, g * 128:(g + 1) * 128], vnb[:, g * 64:(g + 1) * 64], True, True),
                     reads=['wsT', kb + 'vnb'], writes=[('ps', pg)])
            S.op('dve', TT(vt[:], k.ps[pg][:, 0:256], ppc(k, 'bsf'), ALU.add), reads=[('ps', pg), 'pp'], writes=[kb + 'vt'])
            S.op('dve', TT(ya[:], vt[:], u[:, 0:256], ALU.mult), reads=[kb + 'vt', kb + 'uv'], writes=[kb + 'ya'])
            xT_and_wout(k, ya, 2, woA, 'woA', i, kb + 'ya', yaT, kb + 'yaT', ptr, (po, po))
        for i in range(0, NT, 2):
            S.play(S.record(lambda: tile_ops(i)), S.record(lambda: tile_ops(i + 1)) if i + 1 < NT else [])
        S.barrier()


def gate_tables(k, l, sb):
    S, NT, nc = k.S, k.NT, k.nc
    winv = k.dr['w_in'][l].rearrange("(kc p) n -> p kc n", p=128)
    wg = sb("wg", [128, 8, 16], BF16)
    G_ig = sb("G_ig", [128, NT, 8], F32)
    G_lfn = sb("G_lfn", [128, NT, 8], F32)
    load_w(k, wg[:], winv[:, :, 2048:2064], 'wg', 'wg')
    for i in range(NT):
        for kc in range(8):
            S.op('pe', MM(k.ps[0][:, i * 16:(i + 1) * 16], k.hT[:, kc, 1 + i * 128:1 + (i + 1) * 128], wg[:, kc, :], kc == 0, kc == 7),
                 reads=[('hT', i), 'wg'], writes=[('ps', 0)])
    pv = k.ps[0][:, 0:NT * 16].rearrange("p (n c) -> p n c", c=16)
    mlb = ppc(k, 'mlb')
    S.op('dve', TT(G_ig[:], pv[:, :, 0:8], mlb[:, 0:8].unsqueeze(1).broadcast_to([128, NT, 8]), ALU.add),
         reads=[('ps', 0), 'pp'], writes=['G_ig'])
    S.op('dve', TT(G_lfn[:], pv[:, :, 8:16], mlb[:, 8:16].unsqueeze(1).broadcast_to([128, NT, 8]), ALU.add),
         reads=[('ps', 0), 'pp'], writes=['G_lfn'])
    S.op('act', ACT(G_lfn[:], G_lfn[:], AF.Exp, scale=-1.0), reads=[], writes=['G_lfn'])
    S.op('act', ACT(G_lfn[:], G_lfn[:], AF.Ln, bias=cmc(k, 'one')), reads=['cm'], writes=['G_lfn'])
    return G_ig, G_lfn


def scan_prep_common(k, d, lf, gatekey, HH, W, pfx, strict=False, pfxd=None):
    S = k.S
    N = HH * 128
    H2 = HH // 2
    pfxd = pfxd or pfx
    S.op('pool', TT(W['GxE'][:].rearrange("p (h t) -> p h t", h=HH),
                   cmc(k, 'U2n%d' % d).unsqueeze(1).broadcast_to([128, HH, 128]),
                   lf.unsqueeze(2).broadcast_to([128, HH, 128]), ALU.mult),
         reads=['cm', gatekey], writes=[pfx + 'GxE'])
    S.op('pe', MM(k.ps[0][:, 0:N], cmc(k, 'A2%d' % d), W['GxE'][:], True, False), reads=['cm', pfx + 'GxE'], writes=[('ps', 0)])
    S.op('pe', MM(k.ps[0][:, 0:N], k.ident[:], k.negfb[d][:, 0:N], False, True), reads=['ident', 'negfb'], writes=[('ps', 0)])
    S.op('act', ACT(W['decT'][:], k.ps[0][:, 0:N], AF.Exp), reads=[('ps', 0)], writes=[pfx + 'decT'])
    for h in range(HH):
        hp, h2 = h % 2, h // 2
        S.op('pe', MM(k.ps[1][64 * hp:64 * hp + 64, h2 * 128:(h2 + 1) * 128], cmc(k, 'ones'),
                      W['GxE'][:, h * 128:(h + 1) * 128], True, True),
             reads=['cm', pfx + 'GxE'], writes=[('ps', 1)])
    S.op('pe', MM(k.ps[1][:, 256:256 + HH], cmc(k, 'A2%d' % d), lf, True, True), reads=['cm', gatekey], writes=[('ps', 1)])
    S.op('act', ACT(W['Eexp'][:], k.ps[1][:, 0:H2 * 128], AF.Exp), reads=[('ps', 1)], writes=[pfxd + 'Eexp'])
    S.op('act', ACT(W['dend'][:], k.ps[1][:, 256:256 + HH], AF.Exp, scale=-1.0), reads=[('ps', 1)], writes=[pfxd + 'dend'])


def group_B(k, l):
    S, T, NT, nc = k.S, k.T, k.NT, k.nc
    dr = k.dr
    winv = dr['w_in'][l].rearrange("(kc p) n -> p kc n", p=128)
    woutv = dr['w_out'][l].rearrange("(kc p) n -> p kc n", p=128)
    TB = min(512, T)
    with ExitStack() as es0:
        def sb0(name, shape, dt):
            return es0.enter_context(nc.sbuf_tensor("sb_" + name + "_b%d" % l, shape, dt))
        G_ig, G_lfn = gate_tables(k, l, sb0)
        alloc = {}
        if DBG.get('stop') == 1:
            S.barrier()
            return
        for hpass in range(2):
            hb0 = 2 * hpass
            with ExitStack() as es:
                def sb(name, shape, dt):
                    if name not in alloc:
                        alloc[name] = es0.enter_context(nc.sbuf_tensor("sb_" + name + "_b%d" % l, shape, dt))
                    return alloc[name]
                wqk = sb("wqk", [128, 8, 256], BF16)
                wvo = sb("wvo", [128, 8, 512], BF16)
                woB = sb("woB", [128, 2, D], BF16)
                qT = sb("qT", [128, T], BF16)
                qTm = sb("qTm", [128, 2, T], BF16)
                kT = sb("kT", [128, T], BF16)
                ktok = sb("ktok", [128, NT, 128], BF16)
                vp = sb("vp", [128, NT, 2, 130], BF16)
                og = sb("og", [128, NT, 256], BF16)
                HB = sb("HB", [128, NT, 256], F32)
                Cst = sb("Cst", [128, 2, 130], F32)
                Cbf = sb("Cbf", [128, 2, 130], BF16)
                Wd = []
                for d in range(2):
                    W = dict(
                        GxE=sb("GxE%d" % d, [128, 256], F32),
                        decT=sb("decT%d" % d, [128, 256], F32), dend=sb("dend%d" % d, [128, 2], F32),
                        Eexp=sb("Eexp%d" % d, [128, 128], F32), eig=sb("eig%d" % d, [128, 2], F32),
                        qtm=sb("qtm%d" % d, [128, 2, 128], BF16), STm=sb("STm%d" % d, [128, 256], BF16),
                        vw=sb("vw%d" % d, [128, 2, 130], BF16), kdm=sb("kdm%d" % d, [128, 2, 128], BF16),
                        den=sb("den%d" % d, [128, 2], F32), rden=sb("rden%d" % d, [128, 2], F32),
                        tmp=sb("tmpo%d" % d, [128, 256], F32))
                    Wd.append(W)
                    S.op('pool', MSET(W['qtm'][:], 0.0), writes=['B%dqtm' % d])
                    S.op('pool', MSET(W['kdm'][:], 0.0), writes=['B%dkdm' % d])
                vt = sb("fvt", [128, 256], F32)
                ssq = sb("fssq", [128, 2], F32)
                rs2 = sb("frs2", [128, 2], F32)
                tm2 = sb("ftm2", [128, 2], F32)
                yb = sb("yb", [128, 256], BF16)
                ybT = sb("ybT", [128, 2, 128], BF16)

                load_w(k, wqk[:, :, 0:128], winv[:, :, 512 + hb0 * 64:512 + hb0 * 64 + 128], 'wqk', 'wqk')
                load_w(k, wqk[:, :, 128:256], winv[:, :, 768 + hb0 * 64:768 + hb0 * 64 + 128], 'wqk2', 'wqk2')
                load_w(k, wvo[:, :, 0:256], winv[:, :, 1024 + hb0 * 128:1024 + hb0 * 128 + 256], 'wvo', 'wvo')
                load_w(k, wvo[:, :, 256:512], winv[:, :, 1536 + hb0 * 128:1536 + hb0 * 128 + 256], 'wvo2', 'wvo2')
                load_w(k, woB[:], woutv[:, 2 + hb0:2 + hb0 + 2, :], 'woB', 'woB')
                S.op('pool', MSET(vp[:], 1.0), writes=['vp1'] + [('vp', i) for i in range(NT)])
                S.op('pool', MSET(Cst[:], 0.0), writes=['Cst0', 'Cst1'])
                S.op('pool', MSET(Cbf[:], 0.0), writes=['Cbf0', 'Cbf1'])
                S.op('pool', MSET(qTm[:], 0.0), writes=[('qTm', tb) for tb in range(T // TB)])
                if DBG.get('stop') == 21:
                    S.barrier()
                    return
                for tb in range(T // TB):
                    hreads = [('hT', tb * (TB // 128) + q) for q in range(TB // 128)]
                    sl = slice(tb * TB, (tb + 1) * TB)
                    for c in range(2):
                        bank = 2 + c
                        for kc in range(8):
                            S.op('pe', MM(k.ps[bank][:, 0:TB], wqk[:, kc, c * 128:(c + 1) * 128],
                                          k.hT[:, kc, 1 + tb * TB:1 + (tb + 1) * TB], kc == 0, kc == 7),
                                 reads=['wqk', 'wqk2'] + hreads, writes=[('ps', bank)])
                    S.op('act', ACT(qT[:, sl], k.ps[2][:, 0:TB], AF.Copy, scale=0.125), reads=[('ps', 2)], writes=[('qT', tb)])
                    S.op('act', ACT(qTm[0:64, 0, sl], k.ps[2][0:64, 0:TB], AF.Copy, scale=0.125), reads=[('ps', 2)], writes=[('qTm', tb)])
                    S.op('act', ACT(qTm[64:128, 1, sl], k.ps[2][64:128, 0:TB], AF.Copy, scale=0.125), reads=[('ps', 2)], writes=[('qTm', tb)])
                    S.op('act', ACT(kT[:, sl], k.ps[3][:, 0:TB], AF.Copy), reads=[('ps', 3)], writes=[('kT', tb)])
                if DBG.get('stop') == 22:
                    S.barrier()
                    return
                for i in range(NT):
                    b0, b1 = 4 + (i % 2) * 2, 5 + (i % 2) * 2
                    for kc in range(8):
                        S.op('pe', MM(k.ps[b0][:, 0:128], k.hT[:, kc, 1 + i * 128:1 + (i + 1) * 128], wqk[:, kc, 128:256], kc == 0, kc == 7),
                             reads=[('hT', i), 'wqk2'], writes=[('ps', b0)])
                    for kc in range(8):
                        S.op('pe', MM(k.ps[b1][:, :], k.hT[:, kc, 1 + i * 128:1 + (i + 1) * 128], wvo[:, kc, :], kc == 0, kc == 7),
                             reads=[('hT', i), 'wvo', 'wvo2'], writes=[('ps', b1)])
                    S.op('act', ACT(ktok[:, i, :], k.ps[b0][:, 0:128], AF.Copy), reads=[('ps', b0)], writes=[('ktok', i)])
                    S.op('act', ACT(vp[:, i, :, 0:128], k.ps[b1][:, 0:256].rearrange("p (h v) -> p h v", h=2), AF.Copy),
                         reads=[('ps', b1)], writes=[('vp', i)])
                    S.op('act', ACT(og[:, i, :], k.ps[b1][:, 256:512], AF.Sigmoid), reads=[('ps', b1)], writes=[('og', i)])
                    S.op('pool', TT(og[:, i, :], og[:, i, :], ppc(k, 'mln', hb0 * 128, hb0 * 128 + 256), ALU.mult), reads=['pp'], writes=[('og', i)])

                if DBG.get('stop') == 2:
                    S.barrier()
                    return

                def prep(d, ti):
                    W = Wd[d]
                    pfx = 'B%d' % d
                    tb = ti * 128 // TB
                    tsl = slice(ti * 128, (ti + 1) * 128)
                    lf2 = G_lfn[:, ti, d * 4 + hb0:d * 4 + hb0 + 2]
                    scan_prep_common(k, d, lf2, 'G_lfn', 2, W, pfx)
                    S.op('act', ACT(W['eig'][:], G_ig[:, ti, d * 4 + hb0:d * 4 + hb0 + 2], AF.Exp), reads=['G_ig'], writes=[pfx + 'eig'])
                    for hp in range(2):
                        HP = slice(64 * hp, 64 * hp + 64)
                        S.op('pool', TT(W['qtm'][HP, hp, :], qT[HP, tsl], W['Eexp'][HP, :], ALU.mult),
                             reads=[('qT', tb), pfx + 'Eexp'], writes=[pfx + 'qtm'])
                    for hp in range(2):
                        S.op('pe', MM(k.ps[2][:, hp * 128:(hp + 1) * 128], kT[:, tsl], qTm[:, hp, tsl], True, True),
                             reads=[('kT', tb), ('qTm', tb)], writes=[('ps', 2)])
                    S.op('dve', TT(W['STm'][:], k.ps[2][:, 0:256], W['decT'][:], ALU.mult), reads=[('ps', 2), pfx + 'decT'], writes=[pfx + 'STm'])
                    S.op('pool', TT(W['vw'][:], vp[:, ti, :, :], W['eig'][:].unsqueeze(2).broadcast_to([128, 2, 130]), ALU.mult),
                         reads=[('vp', ti), 'vp1', pfx + 'eig'], writes=[pfx + 'vw'])
                    for p in range(2):
                        P = slice(64 * p, 64 * p + 64)
                        S.op('pool', TT(W['kdm'][P, p, :].rearrange("p (h c) -> p h c", h=2), ktok[P, ti, :].rearrange("p (h c) -> p h c", h=2),
                                       W['dend'][P, :].unsqueeze(2).broadcast_to([64, 2, 64]), ALU.mult),
                             reads=[('ktok', ti), pfx + 'dend'], writes=[pfx + 'kdm'])

                def chain(d, ti, p, second):
                    W = Wd[d]
                    pfx = 'B%d' % d
                    P = slice(64 * p, 64 * p + 64)
                    for hp in range(2):
                        o_ = k.ps[3][P, hp * 130:(hp + 1) * 130]
                        S.op('pe', MM(o_, W['STm'][:, hp * 128 + 64 * p:hp * 128 + 64 * p + 64], W['vw'][:, hp, :], True, False),
                             reads=[pfx + 'STm', pfx + 'vw'], writes=[('ps', 3)])
                        S.op('pe', MM(o_, W['qtm'][:, hp, 64 * p:64 * p + 64], Cbf[:, d, :], False, True),
                             reads=[pfx + 'qtm', 'Cbf%d' % d], writes=[('ps', 3)])
                    for hp in range(2):
                        S.op('pe', MM(k.ps[4][64 * hp:64 * hp + 64, 0:130], W['kdm'][:, p, hp * 64:(hp + 1) * 64], W['vw'][:, hp, :], True, True),
                             reads=[pfx + 'kdm', pfx + 'vw'], writes=[('ps', 4)])
                    tlast = 64 * p + 63 if d == 0 else 64 * p
                    S.op('dve', STT(Cst[:, d, :], Cst[:, d, :], W['Eexp'][:, tlast:tlast + 1], k.ps[4][:, 0:130], ALU.mult, ALU.add),
                         reads=[pfx + 'Eexp', ('ps', 4)], writes=['Cst%d' % d])
                    S.op('act', ACT(Cbf[:, d, :], Cst[:, d, :], AF.Copy), reads=['Cst%d' % d], writes=['Cbf%d' % d])

                def norm_out(d, ti, second):
                    W = Wd[d]
                    pfx = 'B%d' % d
                    dcol = k.ps[3][:, 128:260:130]
                    S.op('dve', TS(W['den'][:], dcol, 1.0, None, ALU.max), reads=[('ps', 3)], writes=[pfx + 'den'])
                    S.op('dve', STT(W['den'][:], dcol, -1.0, W['den'][:], ALU.mult, ALU.max), reads=[('ps', 3)], writes=[pfx + 'den'])
                    den_ap, rden_ap = W['den'][:], W['rden'][:]
                    S.op('dve', (lambda a, b_: lambda e: e.reciprocal(out=a, in_=b_))(rden_ap, den_ap), reads=[pfx + 'den'], writes=[pfx + 'rden'])
                    src = k.ps[3][:, 0:260].rearrange("p (h c) -> p h c", h=2)[:, :, 0:128]
                    rb = W['rden'][:].unsqueeze(2).broadcast_to([128, 2, 128])
                    hbv = HB[:, ti, :].rearrange("p (h c) -> p h c", h=2)
                    hk = [('HB', ti, 0), ('HB', ti, 1)]
                    if not second:
                        S.op('dve', TT(hbv, src, rb, ALU.mult), reads=[('ps', 3), pfx + 'rden'], writes=hk)
                    else:
                        tv = W['tmp'][:].rearrange("p (h c) -> p h c", h=2)
                        S.op('dve', TT(tv, src, rb, ALU.mult), reads=[('ps', 3), pfx + 'rden'], writes=[pfx + 'tmp'])
                        S.op('dve', TT(HB[:, ti, :], HB[:, ti, :], W['tmp'][:], ALU.add), reads=[pfx + 'tmp'], writes=hk)

                def finalize(ti):
                    hb = HB[:, ti, :]
                    hbk = [('HB', ti, 0), ('HB', ti, 1)]
                    for h in range(2):
                        hs = slice(h * 128, (h + 1) * 128)
                        S.op('act', ACT(vt[:, hs], hb[:, hs], AF.Square, accum_out=ssq[:, h:h + 1]), reads=hbk, writes=['fvt', ('fssq', h)])
                    rsqrt_small(k, rs2[:], ssq[:], 2, 1.0 / 128, tm2[:], [('fssq', 0), ('fssq', 1)], ['frs2'], 'ftm2')
                    for h in range(2):
                        hs = slice(h * 128, (h + 1) * 128)
                        S.op('dve', STT(yb[:, hs], hb[:, hs], rs2[:, h:h + 1], og[:, ti, hs], ALU.mult, ALU.mult),
                             reads=['frs2', ('og', ti)] + hbk, writes=['yb'])
                    xT_and_wout(k, yb, 2, woB, 'woB', ti, 'yb', ybT, 'ybT', 5, (6, 7))

                touched = [False] * NT
                stages = []
                for step in range(NT):
                    for d in range(2):
                        ti = step if d == 0 else NT - 1 - step
                        second = touched[ti]

                        def body(d=d, ti=ti, second=second):
                            for p in ((0, 1) if d == 0 else (1, 0)):
                                chain(d, ti, p, second)
                            norm_out(d, ti, second)
                            if second:
                                finalize(ti)
                        stages.append(((lambda d=d, ti=ti: prep(d, ti)), body))
                        touched[ti] = True
                S.play(S.record(stages[0][0]))
                for i in range(len(stages)):
                    nxt = S.record(stages[i + 1][0]) if i + 1 < len(stages) else []
                    S.play(S.record(stages[i][1]), nxt)
        S.barrier()


def gdn_gate_tables(k, l, sb):
    S, NT, nc = k.S, k.NT, k.nc
    winv = k.dr['w_in'][l].rearrange("(kc p) n -> p kc n", p=128)
    wab = sb("wab", [128, 8, 16], BF16)
    gn = sb("gn", [128, NT, 8], F32)
    beta = sb("beta", [128, NT, 8], F32)
    nbeta = sb("nbeta", [128, NT, 8], F32)
    eA = sb("eA", [128, 8], F32)
    load_w(k, wab[:], winv[:, :, 3088:3104], 'wab', 'wab')
    for i in range(NT):
        for kc in range(8):
            S.op('pe', MM(k.ps[0][:, i * 16:(i + 1) * 16], k.hT[:, kc, 1 + i * 128:1 + (i + 1) * 128], wab[:, kc, :], kc == 0, kc == 7),
                 reads=[('hT', i), 'wab'], writes=[('ps', 0)])
    pv = k.ps[0][:, 0:NT * 16].rearrange("p (n c) -> p n c", c=16)
    S.op('dve', TT(gn[:], pv[:, :, 0:8], ppc(k, 'dtb').unsqueeze(1).broadcast_to([128, NT, 8]), ALU.add),
         reads=[('ps', 0), 'pp'], writes=['gn'])
    S.op('dve', CP(beta[:], pv[:, :, 8:16]), reads=[('ps', 0)], writes=['beta'])
    S.op('act', ACT(gn[:], gn[:], AF.Exp), reads=[], writes=['gn'])
    S.op('act', ACT(gn[:], gn[:], AF.Ln, bias=cmc(k, 'one')), reads=['cm'], writes=['gn'])
    S.op('act', ACT(eA[:], ppc(k, 'alog'), AF.Exp), reads=['pp'], writes=['eA'])
    S.op('dve', TT(gn[:], gn[:], eA[:].unsqueeze(1).broadcast_to([128, NT, 8]), ALU.mult), reads=['eA'], writes=['gn'])
    S.op('act', ACT(beta[:], beta[:], AF.Sigmoid), reads=[], writes=['beta'])
    S.op('dve', TS(nbeta[:], beta[:], -1.0, None, ALU.mult), reads=['beta'], writes=['nbeta'])
    return gn, beta, nbeta


def group_C(k, l):
    S, T, NT, nc = k.S, k.T, k.NT, k.nc
    dr = k.dr
    winv = dr['w_in'][l].rearrange("(kc p) n -> p kc n", p=128)
    woutv = dr['w_out'][l].rearrange("(kc p) n -> p kc n", p=128)
    TB = min(512, T)
    NTB = T // TB
    o_cg = PP_OFF['cw_gd'][0]
    identf = cmc(k, 'ident')
    with ExitStack() as es0:
        def sb0(name, shape, dt):
            return es0.enter_context(nc.sbuf_tensor("sb_" + name + "_c%d" % l, shape, dt))
        gn, beta, nbeta = gdn_gate_tables(k, l, sb0)
        for hpass in range(2):
            hb0 = 2 * hpass
            with ExitStack() as esp:
                def sbp(name, shape, dt):
                    return esp.enter_context(nc.sbuf_tensor("sb_" + name + "_c%d_%d" % (l, hpass), shape, dt))
                qT = sbp("qT", [128, T], BF16)
                kT = sbp("kT", [128, T], BF16)
                ktok = sbp("ktok", [128, NT, 128], BF16)
                vtok = sbp("vtok", [128, NT, 128], BF16)
                zg = sbp("zg", [128, NT, 128], BF16)
                with ExitStack() as es:
                    def sb(name, shape, dt):
                        return es.enter_context(nc.sbuf_tensor("sb_" + name + "_c1%d_%d" % (l, hpass), shape, dt))
                    convbufs = [sb("convbuf%d" % i, [128, T + 4], F32) for i in range(2)]
                    accs = [sb("cacc%d" % i, [128, T], F32) for i in range(2)]
                    xs = sb("cxs", [128, T], BF16)
                    sqf = [sb("sqf%d" % i, [128, TB], F32) for i in range(2)]
                    lnb = [sb("lnb%d" % i, [128, TB], F32) for i in range(2)]
                    wc = [sb("wc%d" % i, [128, 8, 128], BF16) for i in range(2)]
                    wz = sb("wz", [128, 8, 128], BF16)
                    for cb_ in convbufs:
                        S.op('pool', MSET(cb_[:, 0:2], 0.0), writes=['cbpad'])
                        S.op('pool', MSET(cb_[:, T + 2:T + 4], 0.0), writes=['cbpad'])
                    for ci, cc in enumerate((hpass, 2 + hpass, 4 + hpass)):
                        w_ = wc[ci % 2]
                        convbuf = convbufs[ci % 2]
                        acc = accs[ci % 2]
                        ck = 'cacc%d' % (ci % 2)
                        load_w(k, w_[:], winv[:, :, 2064 + cc * 128:2064 + (cc + 1) * 128], ('wc', ci % 2), 'wc%d' % (ci % 2))
                        for tb in range(NTB):
                            bank = tb % 2
                            hreads = [('hT', tb * (TB // 128) + q) for q in range(TB // 128)]
                            for kc in range(8):
                                S.op('pe', MM(k.ps[bank][:, 0:TB], w_[:, kc, :], k.hT[:, kc, 1 + tb * TB:1 + (tb + 1) * TB], kc == 0, kc == 7),
                                     reads=[('wc', ci % 2)] + hreads, writes=[('ps', bank)])
                            S.op('act', ACT(convbuf[:, 2 + tb * TB:2 + (tb + 1) * TB], k.ps[bank][:, 0:TB], AF.Copy),
                                 reads=[('ps', bank)], writes=[('cb', ci % 2, tb)])
                        cbk = [('cb', ci % 2, tb) for tb in range(NTB)] + ['cbpad']
                        for j in range(5):
                            wj = k.pp[:, o_cg + j * 6 + cc:o_cg + j * 6 + cc + 1]
                            if j == 0:
                                S.op('dve', TS(acc[:], convbuf[:, 0:T], wj, None, ALU.mult), reads=cbk + ['pp'], writes=[ck])
                            else:
                                S.op('dve', STT(acc[:], convbuf[:, j:j + T], wj, acc[:], ALU.mult, ALU.add), reads=cbk + ['pp'], writes=[ck])
                        if ci == 2:
                            S.op('act', ACT(xs[:], acc[:], AF.Silu), reads=[ck], writes=['cxs'])
                            for i in range(NT):
                                bank = 2 + i % 2
                                pst = k.ps[bank][:].bitcast(BF16)
                                S.op('pe', TR(pst[:, 0:128], xs[:, i * 128:(i + 1) * 128], k.ident[:]), reads=['cxs', 'ident'], writes=[('ps', bank)])
                                S.op('act', ACT(vtok[:, i, :], pst[:, 0:128], AF.Copy), reads=[('ps', bank)], writes=[('vtok', i)])
                        else:
                            S.op('act', ACT(acc[:], acc[:], AF.Silu), reads=[], writes=[ck])
                            dst = qT if ci == 0 else kT
                            dkey = 'cqT' if ci == 0 else 'ckT'
                            for tb in range(NTB):
                                b = tb % 2
                                bank = 2 + b
                                sl = slice(tb * TB, (tb + 1) * TB)
                                S.op('dve', TT(sqf[b][:], acc[:, sl], acc[:, sl], ALU.mult), reads=[ck], writes=[('sqf', b)])
                                S.op('pe', MM(k.ps[bank][:, 0:TB], cmc(k, 'BD'), sqf[b][:], True, True), reads=['cm', ('sqf', b)], writes=[('ps', bank)])
                                S.op('act', ACT(lnb[b][:], k.ps[bank][:, 0:TB], AF.Ln, bias=cmc(k, 'eps')), reads=[('ps', bank), 'cm'], writes=[('lnb', b)])
                                S.op('act', ACT(lnb[b][:], lnb[b][:], AF.Exp, scale=-0.5), reads=[], writes=[('lnb', b)])
                                if ci == 0:
                                    S.op('dve', STT(dst[:, sl], acc[:, sl], 0.125, lnb[b][:], ALU.mult, ALU.mult), reads=[ck, ('lnb', b)], writes=[(dkey, tb)])
                                else:
                                    S.op('dve', TT(dst[:, sl], acc[:, sl], lnb[b][:], ALU.mult), reads=[ck, ('lnb', b)], writes=[(dkey, tb)])
                            if ci == 1:
                                for i in range(NT):
                                    bank = 4 + i % 2
                                    pst = k.ps[bank][:].bitcast(BF16)
                                    S.op('pe', TR(pst[:, 0:128], kT[:, i * 128:(i + 1) * 128], k.ident[:]), reads=[('ckT', i * 128 // TB), 'ident'], writes=[('ps', bank)])
                                    S.op('act', ACT(ktok[:, i, :], pst[:, 0:128], AF.Copy), reads=[('ps', bank)], writes=[('cktok', i)])
                    load_w(k, wz[:], winv[:, :, 2832 + hpass * 128:2832 + (hpass + 1) * 128], 'wz', 'wz')
                    for i in range(NT):
                        bank = 6 + i % 2
                        for kc in range(8):
                            S.op('pe', MM(k.ps[bank][:, 0:128], k.hT[:, kc, 1 + i * 128:1 + (i + 1) * 128], wz[:, kc, :], kc == 0, kc == 7),
                                 reads=[('hT', i), 'wz'], writes=[('ps', bank)])
                        S.op('act', ACT(zg[:, i, :], k.ps[bank][:, 0:128], AF.Silu), reads=[('ps', bank)], writes=[('zg', i)])
                        S.op('pool', TT(zg[:, i, :], zg[:, i, :], ppc(k, 'gdn', hb0 * 64, hb0 * 64 + 128), ALU.mult), reads=['pp'], writes=[('zg', i)])
                    S.barrier()
                if DBG.get('stop') == 31:
                    S.barrier()
                    return
                with ExitStack() as es:
                    def sb(name, shape, dt):
                        return es.enter_context(nc.sbuf_tensor("sb_" + name + "_c2%d_%d" % (l, hpass), shape, dt))
                    OC = sb("OC", [128, NT, 128], F32)
                    woC = sb("woC", [128, 1, D], BF16)
                    Sst = sb("Sst", [128, 2, 64], F32)
                    Sbf = sb("Sbf", [128, 2, 64], BF16)
                    load_w(k, woC[:], woutv[:, 6 + hpass:7 + hpass, :], 'woC', 'woC')
                    S.op('pool', MSET(Sst[:], 0.0), writes=['Sst0', 'Sst1'])
                    S.op('pool', MSET(Sbf[:], 0.0), writes=['Sbf0', 'Sbf1'])
                    GxE = sb("GxE", [128, 256], F32)
                    decT = sb("decT", [128, 256], F32)
                    nbod = sb("nbod", [128, 256], F32)
                    offd = sb("offd", [128, 256], F32)
                    tmpf = sb("tmpf", [128, 256], F32)
                    qm = sb("qm", [128, 2, 128], BF16)
                    identf2 = sb("identf2", [128, 256], F32)
                    S.op('pool', MSET(qm[:], 0.0), writes=['Cqm'])
                    for hp in range(2):
                        S.op('dve', CP(identf2[:, hp * 128:(hp + 1) * 128], identf), reads=['cm'], writes=['identf2'])
                        S.op('dve', TS(offd[:, hp * 128:(hp + 1) * 128], identf, -1.0, 1.0, ALU.mult, ALU.add), reads=['cm'], writes=['offd'])
                    Dd = []
                    Wd = []
                    for d in range(2):
                        Dd.append(dict(
                            Qb=[sb("Qb%d_%d" % (d, i), [128, 256], BF16) for i in range(2)],
                            Pb=[sb("Pb%d_%d" % (d, i), [128, 256], BF16) for i in range(2)],
                            Wb=[sb("Wb%d_%d" % (d, i), [128, 256], BF16) for i in range(2)],
                            Wf=sb("Wf%d" % d, [128, 256], F32),
                            Wfin=sb("Wfin%d" % d, [128, 256], BF16), R0=sb("R0%d" % d, [128, 128], BF16),
                            vnew=sb("vnew%d" % d, [128, 128], BF16)))
                        for nm in ('R0', 'vnew'):
                            S.op('pool', MSET(Dd[d][nm][:], 0.0), writes=['C%d%s' % (d, nm)])
                        row = []
                        for par in range(2):
                            W = dict(GxE=GxE, decT=decT,
                                     Eexp=sb("Eexp%d%d" % (d, par), [128, 128], F32), dend=sb("dend%d%d" % (d, par), [128, 2], F32),
                                     qtm=sb("qtm%d%d" % (d, par), [128, 2, 128], BF16), ktm=sb("ktm%d%d" % (d, par), [128, 2, 128], BF16),
                                     kdm=sb("kdm%d%d" % (d, par), [128, 2, 128], BF16), attnT=sb("attnT%d%d" % (d, par), [128, 256], BF16))
                            for nm in ('qtm', 'ktm', 'kdm'):
                                S.op('pool', MSET(W[nm][:], 0.0), writes=['C%d%d%s' % (d, par, nm)])
                            row.append(W)
                        Wd.append(row)
                    vt = sb("fvt", [128, 128], F32)
                    ssq = sb("fssq", [128, 2], F32)
                    rs2 = sb("frs2", [128, 2], F32)
                    tm2 = sb("ftm2", [128, 2], F32)
                    yc = sb("yc", [128, 128], BF16)
                    ycT = sb("ycT", [128, 1, 128], BF16)

                    def prepA(d, ti, par):
                        W = Wd[d][par]
                        Dx = Dd[d]
                        Qb, Pb, Wb, Wf = Dx['Qb'], Dx['Pb'], Dx['Wb'], Dx['Wf']
                        pfx = 'C%d' % d
                        pf2 = 'C%d%d' % (d, par)
                        tb = ti * 128 // TB
                        tsl = slice(ti * 128, (ti + 1) * 128)
                        lf2 = gn[:, ti, d * 4 + hb0:d * 4 + hb0 + 2]
                        scan_prep_common(k, d, lf2, 'gn', 2, W, 'CS', pfxd=pf2)
                        for hp in range(2):
                            HP = slice(64 * hp, 64 * hp + 64)
                            S.op('pool', TT(W['qtm'][HP, hp, :], qT[HP, tsl], W['Eexp'][HP, :], ALU.mult),
                                 reads=[('cqT', tb), pf2 + 'Eexp'], writes=[pf2 + 'qtm'])
                            S.op('pool', TT(W['ktm'][HP, hp, :], kT[HP, tsl], W['Eexp'][HP, :], ALU.mult),
                                 reads=[('ckT', tb), pf2 + 'Eexp'], writes=[pf2 + 'ktm'])
                            S.op('act', ACT(qm[HP, hp, :], qT[HP, tsl], AF.Copy), reads=[('cqT', tb)], writes=['Cqm'])
                        for p in range(2):
                            P = slice(64 * p, 64 * p + 64)
                            S.op('pool', TT(W['kdm'][P, p, :].rearrange("p (h c) -> p h c", h=2), ktok[P, ti, :].rearrange("p (h c) -> p h c", h=2),
                                            W['dend'][P, :].unsqueeze(2).broadcast_to([64, 2, 64]), ALU.mult),
                                 reads=[('cktok', ti), pf2 + 'dend'], writes=[pf2 + 'kdm'])
                        for hp in range(2):
                            S.op('pe', MM(k.ps[3][:, hp * 128:(hp + 1) * 128], kT[:, tsl], qm[:, hp, :], True, True),
                                 reads=[('ckT', tb), 'Cqm'], writes=[('ps', 3)])
                        S.op('dve', TT(W['attnT'][:], k.ps[3][:, 0:256], decT[:], ALU.mult), reads=[('ps', 3), 'CSdecT'], writes=[pf2 + 'attnT'])
                        for hp in range(2):
                            HP = slice(64 * hp, 64 * hp + 64)
                            S.op('act', ACT(qm[HP, hp, :], kT[HP, tsl], AF.Copy), reads=[('ckT', tb)], writes=['Cqm'])
                        for hp in range(2):
                            S.op('pe', MM(k.ps[3][:, 256 + hp * 128:256 + (hp + 1) * 128], kT[:, tsl], qm[:, hp, :], True, True),
                                 reads=[('ckT', tb), 'Cqm'], writes=[('ps', 3)])
                        nb = nbeta[:, ti, d * 4 + hb0:d * 4 + hb0 + 2]
                        S.op('pool', TT(nbod[:].rearrange("p (h t) -> p h t", h=2), offd[:].rearrange("p (h t) -> p h t", h=2),
                                        nb.unsqueeze(2).broadcast_to([128, 2, 128]), ALU.mult), reads=['offd', 'nbeta'], writes=['Cnbod'])
                        S.op('dve', TT(tmpf[:], k.ps[3][:, 256:512], decT[:], ALU.mult), reads=[('ps', 3), 'CSdecT'], writes=['Ctmpf'])
                        S.op('dve', TT(Qb[0][:], tmpf[:], nbod[:], ALU.mult), reads=['Ctmpf', 'Cnbod'], writes=[(pfx + 'Q', 0)])
                        for hp in range(2):
                            S.op('pe', MM(k.ps[2][:, hp * 128:(hp + 1) * 128], Qb[0][:, hp * 128:(hp + 1) * 128], k.ident[:], True, True),
                                 reads=[(pfx + 'Q', 0), 'ident'], writes=[('ps', 2)])
                        S.op('dve', CP(Pb[0][:], k.ps[2][:, 0:256]), reads=[('ps', 2)], writes=[(pfx + 'P', 0)])
                        S.op('dve', TT(Wf[:], Qb[0][:], identf2[:], ALU.add), reads=[(pfx + 'Q', 0), 'identf2'], writes=[pfx + 'Wf'])
                        S.op('act', ACT(Wb[0][:], Wf[:], AF.Copy), reads=[pfx + 'Wf'], writes=[(pfx + 'Wb', 0)])

                    def prepB(d, ti, par):
                        Dx = Dd[d]
                        Qb, Pb, Wb, Wf = Dx['Qb'], Dx['Pb'], Dx['Wb'], Dx['Wf']
                        pfx = 'C%d' % d
                        NL = 5
                        for j in range(NL):
                            a, b_ = j % 2, (j + 1) % 2
                            last = (j == NL - 1)
                            if not last:
                                for hp in range(2):
                                    c_ = slice(hp * 128, (hp + 1) * 128)
                                    S.op('pe', MM(k.ps[4][:, c_], Pb[a][:, c_], Qb[a][:, c_], True, True),
                                         reads=[(pfx + 'P', a), (pfx + 'Q', a)], writes=[('ps', 4)])
                                S.op('act', ACT(Qb[b_][:], k.ps[4][:, 0:256], AF.Copy), reads=[('ps', 4)], writes=[(pfx + 'Q', b_)])
                            for hp in range(2):
                                c_ = slice(hp * 128, (hp + 1) * 128)
                                S.op('pe', MM(k.ps[5][:, c_], Qb[a][:, c_], Pb[a][:, c_], True, True),
                                     reads=[(pfx + 'P', a), (pfx + 'Q', a)], writes=[('ps', 5)])
                            S.op('dve', CP(Pb[b_][:], k.ps[5][:, 0:256]), reads=[('ps', 5)], writes=[(pfx + 'P', b_)])
                            for hp in range(2):
                                c_ = slice(hp * 128, (hp + 1) * 128)
                                S.op('pe', MM(k.ps[6][:, c_], Pb[b_][:, c_], Wb[a][:, c_], True, True),
                                     reads=[(pfx + 'P', b_), (pfx + 'Wb', a)], writes=[('ps', 6)])
                            if not last:
                                S.op('dve', TT(Wf[:], Wf[:], k.ps[6][:, 0:256], ALU.add), reads=[('ps', 6)], writes=[pfx + 'Wf'])
                                S.op('act', ACT(Wb[b_][:], Wf[:], AF.Copy), reads=[pfx + 'Wf'], writes=[(pfx + 'Wb', b_)])
                            else:
                                S.op('dve', TT(Dx['Wfin'][:], Wf[:], k.ps[6][:, 0:256], ALU.add), reads=[('ps', 6), pfx + 'Wf'], writes=[pfx + 'Wfin'])

                    def chain(d, ti, p, second, par):
                        W = dict(Wd[d][par])
                        W.update(Wfin=Dd[d]['Wfin'], R0=Dd[d]['R0'], vnew=Dd[d]['vnew'])
                        pfx = 'C%d' % d
                        pf2 = 'C%d%d' % (d, par)
                        P = slice(64 * p, 64 * p + 64)
                        cs = slice(64 * p, 64 * p + 64)
                        for hp in range(2):
                            S.op('pe', MM(k.ps[7][P, hp * 64:(hp + 1) * 64], W['ktm'][:, hp, cs], Sbf[:, d, :], True, True),
                                 reads=[pf2 + 'ktm', 'Sbf%d' % d], writes=[('ps', 7)])
                        S.op('dve', TT(W['R0'][P, :], vtok[P, ti, :], k.ps[7][P, 0:128], ALU.subtract), reads=[('vtok', ti), ('ps', 7)], writes=[pfx + 'R0'])
                        for hp in range(2):
                            S.op('pe', MM(k.ps[7][P, 128 + hp * 64:128 + (hp + 1) * 64], W['Wfin'][:, hp * 128 + 64 * p:hp * 128 + 64 * p + 64],
                                          W['R0'][:, hp * 64:(hp + 1) * 64], True, True),
                                 reads=[pfx + 'Wfin', pfx + 'R0'], writes=[('ps', 7)])
                        bt = beta[P, ti, d * 4 + hb0:d * 4 + hb0 + 2]
                        S.op('dve', TT(W['vnew'][P, :].rearrange("p (h c) -> p h c", h=2), k.ps[7][P, 128:256].rearrange("p (h c) -> p h c", h=2),
                                       bt.unsqueeze(2).broadcast_to([64, 2, 64]), ALU.mult), reads=[('ps', 7), 'beta'], writes=[pfx + 'vnew'])
                        for hp in range(2):
                            o_ = k.ps[7][P, 256 + hp * 64:256 + (hp + 1) * 64]
                            S.op('pe', MM(o_, W['qtm'][:, hp, cs], Sbf[:, d, :], True, False), reads=[pf2 + 'qtm', 'Sbf%d' % d], writes=[('ps', 7)])
                            S.op('pe', MM(o_, W['attnT'][:, hp * 128 + 64 * p:hp * 128 + 64 * p + 64], W['vnew'][:, hp * 64:(hp + 1) * 64], False, True),
                                 reads=[pf2 + 'attnT', pfx + 'vnew'], writes=[('ps', 7)])
                        if not second:
                            S.op('dve', CP(OC[P, ti, :], k.ps[7][P, 256:384]), reads=[('ps', 7)], writes=[('OC', ti, p)])
                        else:
                            S.op('dve', TT(OC[P, ti, :], OC[P, ti, :], k.ps[7][P, 256:384], ALU.add), reads=[('ps', 7)], writes=[('OC', ti, p)])
                        for hp in range(2):
                            S.op('pe', MM(k.ps[7][64 * hp:64 * hp + 64, 384:448], W['kdm'][:, p, hp * 64:(hp + 1) * 64], W['vnew'][:, hp * 64:(hp + 1) * 64], True, True),
                                 reads=[pf2 + 'kdm', pfx + 'vnew'], writes=[('ps', 7)])
                        tlast = 64 * p + 63 if d == 0 else 64 * p
                        S.op('dve', STT(Sst[:, d, :], Sst[:, d, :], W['Eexp'][:, tlast:tlast + 1], k.ps[7][:, 384:448], ALU.mult, ALU.add),
                             reads=[pf2 + 'Eexp', ('ps', 7)], writes=['Sst%d' % d])
                        S.op('act', ACT(Sbf[:, d, :], Sst[:, d, :], AF.Copy), reads=['Sst%d' % d], writes=['Sbf%d' % d])

                    def finalize(ti):
                        oc = OC[:, ti, :]
                        ock = [('OC', ti, 0), ('OC', ti, 1)]
                        for h in range(2):
                            hs = slice(h * 64, (h + 1) * 64)
                            S.op('act', ACT(vt[:, hs], oc[:, hs], AF.Square, accum_out=ssq[:, h:h + 1]), reads=ock, writes=['fvt', ('fssq', h)])
                        rsqrt_small(k, rs2[:], ssq[:], 2, 1.0 / 64, tm2[:], [('fssq', 0), ('fssq', 1)], ['frs2'], 'ftm2')
                        for h in range(2):
                            hs = slice(h * 64, (h + 1) * 64)
                            S.op('dve', STT(yc[:, hs], oc[:, hs], rs2[:, h:h + 1], zg[:, ti, hs], ALU.mult, ALU.mult),
                                 reads=['frs2', ('zg', ti)] + ock, writes=['yc'])
                        xT_and_wout(k, yc, 1, woC, 'woC', ti, 'yc', ycT, 'ycT', 7, (7, 7))

                    touched = [False] * NT
                    stg = []
                    for step in range(NT):
                        for d in range(2):
                            ti = step if d == 0 else NT - 1 - step
                            stg.append((d, ti, step % 2, touched[ti]))
                            touched[ti] = True

                    def recA(i):
                        d, ti, par, sec = stg[i]
                        return S.record(lambda: prepA(d, ti, par))

                    def recB(i):
                        d, ti, par, sec = stg[i]
                        return S.record(lambda: prepB(d, ti, par))

                    def recC(i):
                        d, ti, par, sec = stg[i]

                        def body():
                            for p in ((0, 1) if d == 0 else (1, 0)):
                                chain(d, ti, p, sec, par)
                            if sec:
                                finalize(ti)
                        return S.record(body)
                    n = len(stg)
                    S.play(recA(0))
                    S.play(recB(0), recA(1) if n > 1 else [])
                    for i in range(n):
                        S.play(recC(i), recB(i + 1) if i + 1 < n else [], recA(i + 2) if i + 2 < n else [])
                    S.barrier()
        S.barrier()


def make_in_maps(inputs, T=SEQ, depth=DEPTH, ncores=8):
    x = np.asarray(inputs['x'], np.float32)
    norms = np.zeros((2 * depth + 1, D), np.float32)
    for l in range(depth):
        norms[2 * l] = np.asarray(inputs['norm_mix'][l])
        norms[2 * l + 1] = np.asarray(inputs['norm_ffn'][l])
    norms[2 * depth] = np.asarray(inputs['norm_final'])
    pp = np.stack([pack_params(inputs, l) for l in range(depth)])
    ws = np.asarray(inputs['gm_ws'], np.float32)[:depth]
    wsT = np.ascontiguousarray(ws.transpose(0, 3, 1, 2)).reshape(depth, 128, 512)
    shared = dict(
        norms=norms, pp=pp, wsT=wsT, cm=const_masks(),
        w_in=np.ascontiguousarray(np.asarray(inputs['w_in'], np.float32)[:depth]),
        w_out=np.ascontiguousarray(np.asarray(inputs['w_out'], np.float32)[:depth]),
        w_up=np.ascontiguousarray(np.asarray(inputs['w_up'], np.float32)[:depth]),
        w_down=np.ascontiguousarray(np.asarray(inputs['w_down'], np.float32)[:depth]),
    )
    maps = []
    for c in range(ncores):
        m = dict(shared)
        m['x'] = np.ascontiguousarray(x[c, :T])
        maps.append(m)
    return maps


_NC_CACHE = {}
DBG = {}
_LAST = {}


def kernel(**inputs):
    if 'nc' not in _NC_CACHE:
        _NC_CACHE['nc'] = build()
    nc = _NC_CACHE['nc']
    maps = make_in_maps(inputs)
    res = run_bass_kernel_spmd(nc, maps, core_ids=list(range(8)))
    return np.stack([np.asarray(r['out'], np.float32) for r in res.results], axis=0)
```
